# Optimizing a Trainium2 kernel written in Bass

```python
import math
import jax, jax.numpy as jnp
from jax import lax
import numpy as np

D_MODEL = 2048
BATCH = 2
SEQ = 16384
DEPTH = 4

N_MIXERS = 2
N_LAYERS_A = (DEPTH + 1) // 2
N_LAYERS_B = DEPTH // 2

DSA_PATTERNS = ((128, 1), (512, 4), (2048, 16))
DSA_GROUPS = len(DSA_PATTERNS)
DSA_HEADS = 16
DSA_HEAD_DIM = 128
DSA_WIDTH = DSA_HEADS * DSA_HEAD_DIM
DSA_BLOCK = 128
DSA_IN = 3 * DSA_GROUPS * DSA_WIDTH + DSA_WIDTH

MLA_HEADS = 16
MLA_Q_RANK = 512
MLA_KV_RANK = 512
MLA_NOPE = 128
MLA_ROPE = 64
MLA_V = 128
MLA_WIDTH = MLA_HEADS * MLA_V
MLA_IN = MLA_Q_RANK + MLA_KV_RANK + MLA_ROPE + MLA_WIDTH
MLA_QBLOCK = 128

ROPE_THETA = 10000.0
RMS_EPS = 1e-6
LN_EPS = 1e-5
DEEPNORM_ALPHA = (2 * DEPTH) ** 0.25
DEEPNORM_BETA = (8 * DEPTH) ** -0.25

kernel_name = "hybrid_dilated_mla_deepnorm"


def rope_tables(positions, dim):
    inv = 1.0 / (ROPE_THETA ** (jnp.arange(0, dim, 2, dtype=jnp.float32) / dim))
    ang = positions.astype(jnp.float32)[..., None] * inv
    return jnp.cos(ang), jnp.sin(ang)


def apply_rope(x, cos, sin):
    xf = x.astype(jnp.float32)
    x1, x2 = jnp.split(xf, 2, axis=-1)
    c = cos[:, :, None, :]
    s = sin[:, :, None, :]
    return jnp.concatenate([x1 * c - x2 * s, x2 * c + x1 * s], axis=-1).astype(x.dtype)


def rms_norm(x, g):
    xf = x.astype(jnp.float32)
    y = xf * lax.rsqrt(jnp.mean(xf * xf, axis=-1, keepdims=True) + RMS_EPS)
    return (y * g.astype(jnp.float32)).astype(x.dtype)


def layer_norm(x, g, b):
    xf = x.astype(jnp.float32)
    mu = jnp.mean(xf, axis=-1, keepdims=True)
    xc = xf - mu
    var = jnp.mean(xc * xc, axis=-1, keepdims=True)
    y = xc * lax.rsqrt(var + LN_EPS) * g.astype(jnp.float32) + b.astype(jnp.float32)
    return y.astype(x.dtype)


def dilated_window_attention(q, k, v, window, dilation):
    B, S, H, Dh = q.shape
    span = window // dilation
    assert span <= DSA_BLOCK
    L = S // dilation
    nb = -(-L // DSA_BLOCK)
    Lp = nb * DSA_BLOCK
    N = B * dilation

    def to_sub(t):
        t = t.reshape(B, L, dilation, H, Dh).transpose(0, 2, 1, 3, 4).reshape(N, L, H, Dh)
        t = jnp.pad(t, ((0, 0), (0, Lp - L), (0, 0), (0, 0)))
        return t.reshape(N, nb, DSA_BLOCK, H, Dh)

    def with_prev(t):
        prev = jnp.pad(t, ((0, 0), (1, 0), (0, 0), (0, 0), (0, 0)))[:, :-1]
        return jnp.concatenate([prev, t], axis=2)

    qb = to_sub(q)
    kk = with_prev(to_sub(k))
    vv = with_prev(to_sub(v))
    scores = jnp.einsum('nbqhd,nbkhd->nbhqk', qb, kk).astype(jnp.float32) * (Dh ** -0.5)
    qi = jnp.arange(nb)[:, None, None] * DSA_BLOCK + jnp.arange(DSA_BLOCK)[None, :, None]
    kj = (jnp.arange(nb)[:, None, None] - 1) * DSA_BLOCK + jnp.arange(2 * DSA_BLOCK)[None, None, :]
    dist = qi - kj
    mask = (dist >= 0) & (dist <= span) & (kj >= 0)
    scores = jnp.where(mask[None, :, None], scores, -jnp.inf)
    lse = jax.nn.logsumexp(scores, axis=-1)
    p = jnp.exp(scores - lse[..., None])
    out = jnp.einsum('nbhqk,nbkhd->nbqhd', p.astype(v.dtype), vv)
    out = out.reshape(B, dilation, Lp, H, Dh)[:, :, :L].transpose(0, 2, 1, 3, 4).reshape(B, S, H, Dh)
    lse = lse.transpose(0, 1, 3, 2).reshape(B, dilation, Lp, H)[:, :, :L]
    lse = lse.transpose(0, 2, 1, 3).reshape(B, S, H)
    return out, lse


def dilated_mixer(x, w_in, w_out, cos, sin):
    B, S, _ = x.shape
    h = x @ w_in
    qkv = h[..., :3 * DSA_GROUPS * DSA_WIDTH].reshape(B, S, DSA_GROUPS, 3, DSA_HEADS, DSA_HEAD_DIM)
    z = h[..., 3 * DSA_GROUPS * DSA_WIDTH:]
    outs = []
    lses = []
    for g, (window, dilation) in enumerate(DSA_PATTERNS):
        q = apply_rope(qkv[:, :, g, 0], cos, sin)
        k = apply_rope(qkv[:, :, g, 1], cos, sin)
        o, l = dilated_window_attention(q, k, qkv[:, :, g, 2], window, dilation)
        outs.append(o)
        lses.append(l)
    wts = jax.nn.softmax(jnp.stack(lses, axis=0), axis=0)
    o = jnp.einsum('gbsh,gbshd->bshd', wts, jnp.stack(outs, axis=0).astype(jnp.float32))
    y = o.reshape(B, S, DSA_WIDTH).astype(x.dtype) * jax.nn.silu(z)
    return y @ w_out


def causal_mla_attention(q_nope, q_pe, k_nope, k_pe, v):
    B, S, H, _ = q_nope.shape
    nq = S // MLA_QBLOCK
    scale = (MLA_NOPE + MLA_ROPE) ** -0.5
    qn = q_nope.reshape(B, nq, MLA_QBLOCK, H, MLA_NOPE).transpose(1, 0, 2, 3, 4)
    qp = q_pe.reshape(B, nq, MLA_QBLOCK, H, MLA_ROPE).transpose(1, 0, 2, 3, 4)
    kpos = jnp.arange(S)

    def one_block(args):
        i, qn_i, qp_i = args
        s = (jnp.einsum('bqhd,bkhd->bhqk', qn_i, k_nope).astype(jnp.float32)
             + jnp.einsum('bqhr,bkr->bhqk', qp_i, k_pe).astype(jnp.float32)) * scale
        qpos = i * MLA_QBLOCK + jnp.arange(MLA_QBLOCK)
        s = jnp.where(kpos[None, :] <= qpos[:, None], s, -jnp.inf)
        p = jax.nn.softmax(s, axis=-1)
        return jnp.einsum('bhqk,bkhd->bqhd', p.astype(v.dtype), v)

    o = lax.map(one_block, (jnp.arange(nq), qn, qp))
    return o.transpose(1, 0, 2, 3, 4).reshape(B, S, H, MLA_V)


def mla_mixer(x, w_in, q_norm, w_uq, kv_norm, w_ukv, w_out, cos, sin):
    B, S, _ = x.shape
    h = x @ w_in
    o1 = MLA_Q_RANK
    o2 = o1 + MLA_KV_RANK
    o3 = o2 + MLA_ROPE
    c_q = h[..., :o1]
    c_kv = h[..., o1:o2]
    k_pe = h[..., o2:o3]
    z = h[..., o3:]
    q = (rms_norm(c_q, q_norm) @ w_uq).reshape(B, S, MLA_HEADS, MLA_NOPE + MLA_ROPE)
    q_nope = q[..., :MLA_NOPE]
    q_pe = apply_rope(q[..., MLA_NOPE:], cos, sin)
    k_pe = apply_rope(k_pe[:, :, None, :], cos, sin)[:, :, 0]
    kv = (rms_norm(c_kv, kv_norm) @ w_ukv).reshape(B, S, MLA_HEADS, MLA_NOPE + MLA_V)
    k_nope = kv[..., :MLA_NOPE]
    v = kv[..., MLA_NOPE:]
    o = causal_mla_attention(q_nope, q_pe, k_nope, k_pe, v)
    y = o.reshape(B, S, MLA_WIDTH) * jax.nn.silu(z)
    return y @ w_out


def setup_inputs(seed: int = 0) -> dict:
    key = jax.random.key(seed)
    ks = jax.random.split(key, 12)
    f32 = jnp.float32
    x = jax.random.normal(ks[0], (BATCH, SEQ, D_MODEL), f32)
    start = jax.random.randint(ks[1], (BATCH, 1), 0, 4096, dtype=jnp.int32)
    positions = (start + jnp.arange(SEQ, dtype=jnp.int32)[None, :]).astype(jnp.int32)
    dsa_w_in = jax.random.normal(ks[2], (N_LAYERS_A, D_MODEL, DSA_IN), f32) * D_MODEL ** -0.5
    dsa_w_out = jax.random.normal(ks[3], (N_LAYERS_A, DSA_WIDTH, D_MODEL), f32) * (DSA_WIDTH ** -0.5 * DEEPNORM_BETA)
    mla_w_in = jax.random.normal(ks[4], (N_LAYERS_B, D_MODEL, MLA_IN), f32) * D_MODEL ** -0.5
    mla_q_norm = 1.0 + 0.01 * jax.random.normal(ks[5], (N_LAYERS_B, MLA_Q_RANK), f32)
    mla_w_uq = jax.random.normal(ks[6], (N_LAYERS_B, MLA_Q_RANK, MLA_HEADS * (MLA_NOPE + MLA_ROPE)), f32) * MLA_Q_RANK ** -0.5
    mla_kv_norm = 1.0 + 0.01 * jax.random.normal(ks[7], (N_LAYERS_B, MLA_KV_RANK), f32)
    mla_w_ukv = jax.random.normal(ks[8], (N_LAYERS_B, MLA_KV_RANK, MLA_HEADS * (MLA_NOPE + MLA_V)), f32) * MLA_KV_RANK ** -0.5
    mla_w_out = jax.random.normal(ks[9], (N_LAYERS_B, MLA_WIDTH, D_MODEL), f32) * (MLA_WIDTH ** -0.5 * DEEPNORM_BETA)
    ln_g = 1.0 + 0.01 * jax.random.normal(ks[10], (DEPTH, D_MODEL), f32)
    ln_b = 0.01 * jax.random.normal(ks[11], (DEPTH, D_MODEL), f32)
    return {"x": x, "positions": positions, "dsa_w_in": dsa_w_in, "dsa_w_out": dsa_w_out,
            "mla_w_in": mla_w_in, "mla_q_norm": mla_q_norm, "mla_w_uq": mla_w_uq,
            "mla_kv_norm": mla_kv_norm, "mla_w_ukv": mla_w_ukv, "mla_w_out": mla_w_out,
            "ln_g": ln_g, "ln_b": ln_b}


def reference(x, positions, dsa_w_in, dsa_w_out, mla_w_in, mla_q_norm, mla_w_uq,
              mla_kv_norm, mla_w_ukv, mla_w_out, ln_g, ln_b):
    cos_a, sin_a = rope_tables(positions, DSA_HEAD_DIM)
    cos_b, sin_b = rope_tables(positions, MLA_ROPE)
    for layer in range(DEPTH):
        j = layer // N_MIXERS
        if layer % N_MIXERS == 0:
            y = dilated_mixer(x, dsa_w_in[j], dsa_w_out[j], cos_a, sin_a)
        else:
            y = mla_mixer(x, mla_w_in[j], mla_q_norm[j], mla_w_uq[j], mla_kv_norm[j],
                          mla_w_ukv[j], mla_w_out[j], cos_b, sin_b)
        x = layer_norm(DEEPNORM_ALPHA * x + y, ln_g[layer], ln_b[layer])
    return x
```

```python
import math
from contextlib import ExitStack

import numpy as np
import ml_dtypes
import jax.numpy as jnp

import concourse.bass as bass
import concourse.mybir as mybir
from concourse.bass_utils import run_bass_kernel_spmd

F32 = mybir.dt.float32
BF16 = mybir.dt.bfloat16
I32 = mybir.dt.int32
AF = mybir.ActivationFunctionType
ALU = mybir.AluOpType

D = 2048
SEQ = 16384
NTOK = 4096
DEPTH = 4
ALPHA = (2 * DEPTH) ** 0.25
LN_EPS = 1e-5
RMS_EPS = 1e-6
DIL = (1, 4, 16)
NCORES = 8
TWO_PI = 2.0 * math.pi
CW1 = 6.28125
CW2 = TWO_PI - CW1
MAGIC = 12582912.0


class Buf:
    __slots__ = ("t", "w", "r")

    def __init__(self, t):
        self.t = t
        self.w = {}
        self.r = {}

    def __getitem__(self, k):
        return self.t[k]


class Sched:
    def __init__(self, nc, es):
        self.nc = nc
        self.es = es
        self.es_sem = es
        self.eng = {"pe": nc.tensor, "act": nc.scalar, "dve": nc.vector, "pool": nc.gpsimd, "sp": nc.sync}
        self.esem = {}
        self.ecnt = {}
        self.known = {e: {} for e in self.eng}
        self.nsem = 0
        for e in ("pe", "act", "dve", "pool"):
            self._roll(e)
        self.dpool = {}
        self.dpos = {}
        for q, n in (("sp", 20), ("act", 6), ("pool", 8)):
            self.dpool[q] = [[self._newsem(), 0] for _ in range(n)]
            self.dpos[q] = 0

    def _newsem(self):
        self.nsem += 1
        return self.es_sem.enter_context(self.nc.semaphore("s%d" % self.nsem))

    def _roll(self, e):
        self.esem[e] = self._newsem()
        self.ecnt[e] = 0

    def sb(self, name, shape, dt=F32):
        self.nsem += 1
        return Buf(self.es.enter_context(self.nc.sbuf_tensor("sb%d_%s" % (self.nsem, name), shape, dt)))

    def ps(self, name, shape, dt=F32):
        self.nsem += 1
        return Buf(self.es.enter_context(self.nc.psum_tensor("ps%d_%s" % (self.nsem, name), shape, dt)))

    def dram(self, t):
        return Buf(t)

    def _waits(self, e, reads, writes):
        deps = {}

        def add(tok):
            if tok is None:
                return
            s, v = tok
            if deps.get(id(s), (None, 0))[1] < v:
                deps[id(s)] = (s, v)

        for b in reads:
            for s, v in b.w.values():
                add((s, v))
        for b in writes:
            for s, v in b.w.values():
                add((s, v))
            for s, v in b.r.values():
                add((s, v))
        kn = self.known[e]
        for sid, (s, v) in deps.items():
            if e == "pe" and s is self.esem["pe"]:
                continue
            if kn.get(sid, 0) >= v:
                continue
            self.eng[e].wait_ge(s, v)
            kn[sid] = v

    def _commit(self, tok, reads, writes):
        s, v = tok
        for b in reads:
            b.r[id(s)] = (s, v)
        for b in writes:
            b.w[id(s)] = tok
            b.r = {}

    def op(self, e, fn, reads=(), writes=(), signal=True):
        self._waits(e, reads, writes)
        inst = fn(self.eng[e])
        if signal:
            if self.ecnt[e] >= 30000:
                self._roll(e)
            self.ecnt[e] += 1
            inst.then_inc(self.esem[e], 1)
            self._commit((self.esem[e], self.ecnt[e]), reads, writes)
        return inst

    def dma(self, q, out, in_, reads=(), writes=(), **kw):
        pool = self.dpool[q]
        i = self.dpos[q]
        self.dpos[q] = (i + 1) % len(pool)
        slot = pool[i]
        if slot[1] >= 1800:
            slot[0] = self._newsem()
            slot[1] = 0
        s = slot[0]
        kn = self.known[q]
        if slot[1] > 0 and kn.get(id(s), 0) < slot[1] * 16:
            self.eng[q].wait_ge(s, slot[1] * 16)
            kn[id(s)] = slot[1] * 16
        self._waits(q, reads, writes)
        slot[1] += 1
        self.eng[q].dma_start(out=out, in_=in_, **kw).then_inc(s, 16)
        self._commit((s, slot[1] * 16), reads, writes)

    def collective(self, send_ap, recv_ap, send_d, recv_d):
        if not hasattr(self, "csem"):
            self.csem = self._newsem()
            self.ccnt = 0
        self._waits("pool", [send_d], [recv_d])
        self.ccnt += 1
        self.nc.gpsimd.collective_compute(
            "AllGather", ALU.bypass, replica_groups=[[0, 1, 2, 3], [4, 5, 6, 7]],
            ins=[send_ap], outs=[recv_ap]).then_inc(self.csem)
        self._commit((self.csem, self.ccnt), [send_d], [recv_d])

    def allgather16(self, send_t, recv_t, send_d, recv_d):
        sa = send_t.ap()
        ra = recv_t.ap()
        for k in range(16):
            self.collective(sa[k * 128:(k + 1) * 128, :], ra[k * 512:(k + 1) * 512, :], send_d, recv_d)

    def barrier(self):
        toks = []
        for e in ("pe", "act", "dve", "pool"):
            if self.ecnt[e] > 0:
                toks.append((self.esem[e], self.ecnt[e]))
        for q in self.dpool:
            for s_, n in self.dpool[q]:
                if n > 0:
                    toks.append((s_, n * 16))
        if hasattr(self, "csem") and self.ccnt > 0:
            toks.append((self.csem, self.ccnt))
        for e in self.eng:
            kn = self.known[e]
            for s_, v in toks:
                if e == "pe" and s_ is self.esem["pe"]:
                    continue
                if kn.get(id(s_), 0) >= v:
                    continue
                self.eng[e].wait_ge(s_, v)
                kn[id(s_)] = v

    def drain(self, bufs):
        self._waits("sp", bufs, ())


def ap3(t, off, dims):
    return bass.AP(t.tensor if hasattr(t, "tensor") else t, off, dims)


def sb_ap(buf, off, dims):
    base = buf.t[:]
    return bass.AP(base.tensor, base.offset + off, dims)


def make_consts(S):
    nc = S.nc
    ident = S.sb("ident", [128, 128], BF16)
    ones = S.sb("ones", [128, 128], BF16)
    S.op("pool", lambda g: g.memset(ident[:], 1.0), writes=[ident])
    S.op("pool", lambda g: g.affine_select(out=ident[:], in_=ident[:], pattern=[[-1, 128]],
                                           compare_op=ALU.is_equal, fill=0.0, base=0, channel_multiplier=1),
         reads=[ident], writes=[ident])
    S.op("pool", lambda g: g.memset(ones[:], 1.0), writes=[ones])
    return ident, ones


def emit_to_xt(S, src_bf, ident, ptr, xts, XTd, XT_ap_fn, t, q="sp"):
    for c in range(16):
        S.op("pe", lambda e, c=c: e.transpose(ptr[:, c * 128:(c + 1) * 128], src_bf[:, c * 128:(c + 1) * 128], ident[:]),
             reads=[src_bf, ident], writes=[ptr], signal=(c == 15))
    S.op("dve", lambda e: e.tensor_copy(out=xts[:], in_=ptr[:]), reads=[ptr], writes=[xts])
    S.dma(q, XT_ap_fn(t), xts[:].rearrange("p (c t) -> p c t", c=16), reads=[xts], writes=[XTd])


def rope_tables(S, pos_ap, ntok, inv, sgn, Cd, Sd, nparts, chunk=2048):
    P = nparts
    pi_ = S.sb("rt_pi", [P, chunk], I32)
    pf = S.sb("rt_pf", [P, chunk])
    ang = S.sb("rt_ang", [P, chunk])
    k = S.sb("rt_k", [P, chunk])
    r = S.sb("rt_r", [P, chunk])
    o = S.sb("rt_o", [P, chunk])
    for c0 in range(0, ntok, chunk):
        src = bass.AP(pos_ap.tensor, pos_ap.offset + c0, [[0, P], [1, chunk]])
        S.dma("sp", pi_[:], src, writes=[pi_])
        S.op("dve", lambda e: e.tensor_copy(out=pf[:], in_=pi_[:]), reads=[pi_], writes=[pf])
        S.op("dve", lambda e: e.tensor_scalar(out=ang[:], in0=pf[:], scalar1=inv[:, 0:1], scalar2=None, op0=ALU.mult),
             reads=[pf, inv], writes=[ang])
        for which, dst in ((0, Sd), (1, Cd)):
            S.op("dve", lambda e: e.tensor_scalar(out=k[:], in0=ang[:], scalar1=1.0 / TWO_PI, scalar2=MAGIC,
                                                  op0=ALU.mult, op1=ALU.add), reads=[ang], writes=[k])
            S.op("dve", lambda e: e.tensor_scalar(out=k[:], in0=k[:], scalar1=-MAGIC, scalar2=None, op0=ALU.add),
                 reads=[k], writes=[k])
            S.op("dve", lambda e: e.scalar_tensor_tensor(out=r[:], in0=k[:], scalar=-CW1, in1=ang[:], op0=ALU.mult, op1=ALU.add),
                 reads=[k, ang], writes=[r])
            S.op("dve", lambda e: e.scalar_tensor_tensor(out=r[:], in0=k[:], scalar=-CW2, in1=r[:], op0=ALU.mult, op1=ALU.add),
                 reads=[k, r], writes=[r])
            S.op("dve", lambda e: e.tensor_scalar(out=r[:], in0=r[:], scalar1=3.1415925, scalar2=-3.1415925, op0=ALU.min, op1=ALU.max),
                 reads=[r], writes=[r])
            if which == 1:
                S.op("dve", lambda e: e.scalar_tensor_tensor(out=r[:], in0=r[:], scalar=-1.0, in1=r[:], op0=ALU.mult, op1=ALU.max),
                     reads=[r], writes=[r])
                S.op("dve", lambda e: e.tensor_scalar(out=r[:], in0=r[:], scalar1=-1.0, scalar2=math.pi / 2, op0=ALU.mult, op1=ALU.add),
                     reads=[r], writes=[r])
            S.op("act", lambda e: e.activation(out=o[:], in_=r[:], func=AF.Sin), reads=[r], writes=[o])
            if which == 0:
                S.op("dve", lambda e: e.tensor_scalar(out=o[:], in0=o[:], scalar1=sgn[:, 0:1], scalar2=None, op0=ALU.mult),
                     reads=[o, sgn], writes=[o])
            S.dma("sp", dst.t[:, c0:c0 + chunk], o[:], reads=[o], writes=[dst])


def dil_dims(g, m):
    if g == 0:
        return 512 * m, [[1, 512]]
    if g == 1:
        return 512 * m, [[1, 4], [4, 128]]
    return 4 * m, [[1, 4], [16, 128]]


def dil_tile_dims(g, tt):
    if g == 0:
        return 128 * tt, [[1, 128]]
    if g == 1:
        return 512 * (tt // 4) + (tt % 4), [[4, 128]]
    return tt, [[16, 128]]


def emit_p0(S, ident, x, ntok, XT, XTd):
    with ExitStack() as es:
        S.es = es
        xf = [S.sb("xf%d" % i, [128, D]) for i in range(2)]
        xb = [S.sb("xb%d" % i, [128, D], BF16) for i in range(2)]
        ptr = [S.ps("ptr%d" % i, [128, D], BF16) for i in range(2)]
        xts = [S.sb("xts%d" % i, [128, D], BF16) for i in range(2)]
        XTv = XT.rearrange("(c p) t -> p c t", p=128)
        for t in range(ntok // 128):
            i = t % 2
            S.dma("sp", xf[i][:], x[t * 128:(t + 1) * 128, :], writes=[xf[i]])
            S.op("pool", lambda e, i=i: e.tensor_copy(out=xb[i][:], in_=xf[i][:]), reads=[xf[i]], writes=[xb[i]])
            emit_to_xt(S, xb[i], ident, ptr[i], xts[i], XTd, lambda t: XTv[:, :, t * 128:(t + 1) * 128], t)
    S.barrier()


MLA_PERM = [(c % 4) * 4 + c // 4 for c in range(16)]


def emit_epilogue(S, ident, gt_load, resid, resid_d, wout, lng, lnb, resid_out, resid_out_d, XT_out, XT_out_d, perm=None):
    perm = perm or list(range(16))
    nc = S.nc
    wo = S.sb("wo", [128, 16, D], BF16)
    wst = [S.sb("wst%d" % i, [128, 4, 512]) for i in range(2)]
    woutv = wout.rearrange("(c p) n -> p c n", p=128)
    k = 0
    for c4 in range(4):
        for n in range(4):
            i = k % 2
            k += 1
            S.dma("sp", wst[i][:], woutv[:, c4 * 4:(c4 + 1) * 4, n * 512:(n + 1) * 512], writes=[wst[i]])
            S.op("pool", lambda e, i=i, c4=c4, n=n: e.tensor_copy(out=wo[:, c4 * 4:(c4 + 1) * 4, n * 512:(n + 1) * 512], in_=wst[i][:]),
                 reads=[wst[i]], writes=[wo])
    g_b = S.sb("lng_b", [128, D])
    b_b = S.sb("lnb_b", [128, D])
    S.dma("sp", g_b[:], bass.AP(lng.tensor, lng.offset, [[0, 128], [1, D]]), writes=[g_b])
    S.dma("sp", b_b[:], bass.AP(lnb.tensor, lnb.offset, [[0, 128], [1, D]]), writes=[b_b])
    gt = [S.sb("gt%d" % i, [128, 16, 128], BF16) for i in range(2)]
    rs = [S.sb("rs%d" % i, [128, D]) for i in range(2)]
    v = [S.sb("v%d" % i, [128, D]) for i in range(2)]
    xo = [S.sb("xo%d" % i, [128, D]) for i in range(2)]
    xb = [S.sb("exb%d" % i, [128, D], BF16) for i in range(2)]
    st6_ = [S.sb("st6_%d" % i, [128, 4, 6]) for i in range(2)]
    mv_ = [S.sb("mv%d" % i, [128, 2]) for i in range(2)]
    rstd_ = [S.sb("rstd%d" % i, [128, 1]) for i in range(2)]
    py = [S.ps("py%d" % n, [128, 512]) for n in range(4)]
    ptr = S.ps("eptr", [128, D], BF16)
    xts = [S.sb("exts%d" % i, [128, D], BF16) for i in range(2)]
    XTv = XT_out.rearrange("(c p) t -> p c t", p=128) if XT_out is not None else None
    def _loads(t):
        gt_load(t, gt[t % 2])
        S.dma("sp", rs[t % 2][:], resid[t * 128:(t + 1) * 128, :], reads=[resid_d], writes=[rs[t % 2]])

    _loads(0)
    for t in range(NTOK // 128):
        i = t % 2
        if t + 1 < NTOK // 128:
            _loads(t + 1)
        for n in range(4):
            for c in range(16):
                S.op("pe", lambda e, c=c, n=n, i=i: e.matmul(py[n][:], gt[i][:, c, :], wo[:, perm[c], n * 512:(n + 1) * 512],
                                                             start=(c == 0), stop=(c == 15)),
                     reads=[gt[i], wo], writes=[py[n]], signal=(c == 15))
            S.op("dve", lambda e, n=n, i=i: e.scalar_tensor_tensor(out=v[i][:, n * 512:(n + 1) * 512], in0=rs[i][:, n * 512:(n + 1) * 512],
                                                                    scalar=ALPHA, in1=py[n][:], op0=ALU.mult, op1=ALU.add),
                 reads=[rs[i], py[n]], writes=[v[i]])
        st6, mv, rstd = st6_[i], mv_[i], rstd_[i]
        for n in range(4):
            S.op("dve", lambda e, n=n, i=i: e.bn_stats(out=st6[:, n, :], in_=v[i][:, n * 512:(n + 1) * 512]),
                 reads=[v[i]], writes=[st6])
        S.op("dve", lambda e: e.bn_aggr(out=mv[:], in_=st6[:]), reads=[st6], writes=[mv])
        S.op("dve", lambda e: e.tensor_scalar(out=rstd[:], in0=mv[:, 1:2], scalar1=LN_EPS, scalar2=None, op0=ALU.add),
             reads=[mv], writes=[rstd])
        S.op("act", lambda e: e.activation(out=rstd[:], in_=rstd[:], func=AF.Sqrt), reads=[rstd], writes=[rstd])
        S.op("dve", lambda e: e.reciprocal(out=rstd[:], in_=rstd[:]), reads=[rstd], writes=[rstd])
        S.op("dve", lambda e, i=i: e.scalar_tensor_tensor(out=v[i][:], in0=v[i][:], scalar=mv[:, 0:1], in1=g_b[:],
                                                          op0=ALU.subtract, op1=ALU.mult), reads=[v[i], mv, g_b], writes=[v[i]])
        S.op("dve", lambda e, i=i: e.scalar_tensor_tensor(out=xo[i][:], in0=v[i][:], scalar=rstd[:, 0:1], in1=b_b[:],
                                                          op0=ALU.mult, op1=ALU.add), reads=[v[i], rstd, b_b], writes=[xo[i]])
        S.dma("act", resid_out[t * 128:(t + 1) * 128, :], xo[i][:], reads=[xo[i]], writes=[resid_out_d])
        if XT_out is not None:
            S.op("pool", lambda e, i=i: e.tensor_copy(out=xb[i][:], in_=xo[i][:]), reads=[xo[i]], writes=[xb[i]])
            emit_to_xt(S, xb[i], ident, ptr, xts[i], XT_out_d, lambda t: XTv[:, :, t * 128:(t + 1) * 128], t, q="act")


def emit_dil(S, ident, ones, io):
    XTo_in, XTo_in_d = io["XT_own"], io["XT_own_d"]
    resid, resid_d = io["resid"], io["resid_d"]
    w_in, wout, lng, lnb, masks_in = io["w_in"], io["wout"], io["lng"], io["lnb"], io["masks"]
    ro, rod, XTo, xtd = io["ro"], io["rod"], io["XTo"], io["xtd"]
    QT, KT, VV, ZT, GT, Ctab, Stab = (io[k] for k in ("QT", "KT", "VV", "ZT", "GT", "Ctab", "Stab"))
    QTd, KTd, VVd, ZTd, GTd, Cd, Sd = (io[k + "_d"] for k in ("QT", "KT", "VV", "ZT", "GT", "Ctab", "Stab"))
    if True:
        with ExitStack() as es:
            S.es = es
            xT = S.sb("xT", [128, 16, 2048], BF16)
            Cb = S.sb("Cb", [128, 2048])
            Sb = S.sb("Sb", [128, 2048])
            wf = [S.sb("wf%d" % i, [128, 16, 256]) for i in range(3)]
            wb = [S.sb("wb%d" % i, [128, 16, 256], BF16) for i in range(2)]
            qst = [S.sb("qst%d" % i, [128, 2048], BF16) for i in range(2)]
            vst = [S.sb("vst%d" % i, [128, 256], BF16) for i in range(2)]
            t1 = [S.sb("t1_%d" % i, [128, 512]) for i in range(2)]
            t2 = [S.sb("t2_%d" % i, [128, 512]) for i in range(2)]
            pp = [S.ps("pp%d" % i, [128, 512]) for i in range(4)]
            pv = [S.ps("pv%d" % i, [128, 256]) for i in range(2)]
            w_inv = w_in.rearrange("(c p) n -> p c n", p=128)
            ppk = 0
            qk = 0
            vk = 0
            alltiles = []
            for blk in range(3):
                for g in range(3):
                    for kind in range(3):
                        if blk == 0 and kind == 0:
                            continue
                        for h0 in range(0, 16, 2):
                            alltiles.append((blk, kind, g, h0))
                if blk > 0:
                    for h0 in range(0, 16, 2):
                        alltiles.append((blk, 3, 0, h0))

            def w_load(n):
                (_, kind, g, h0) = alltiles[n]
                col0 = 18432 + h0 * 128 if kind == 3 else ((g * 3 + kind) * 16 + h0) * 128
                S.dma("sp", wf[n % 3][:], w_inv[:, :, col0:col0 + 256], writes=[wf[n % 3]])

            def w_cast(n):
                S.op("act", lambda e: e.activation(out=wb[n % 2][:], in_=wf[n % 3][:], func=AF.Copy),
                     reads=[wf[n % 3]], writes=[wb[n % 2]])

            w_load(0)
            w_load(1)
            w_cast(0)
            cur_blk = -1
            for n, (blk, kind, g, h0) in enumerate(alltiles):
                if blk != cur_blk:
                    cur_blk = blk
                    if blk == 0:
                        io["halo_load"](xT)
                    else:
                        S.dma("sp", xT[:], XTo_in[:, (blk - 1) * 2048:blk * 2048].rearrange("(c p) t -> p c t", p=128),
                              reads=[XTo_in_d], writes=[xT])
                    S.dma("sp", Cb[:], Ctab[:, blk * 2048:(blk + 1) * 2048], reads=[Cd], writes=[Cb])
                    S.dma("sp", Sb[:], Stab[:, blk * 2048:(blk + 1) * 2048], reads=[Sd], writes=[Sb])
                if n + 2 < len(alltiles):
                    w_load(n + 2)
                if n + 1 < len(alltiles):
                    w_cast(n + 1)
                wi = n % 2
                if True:
                    if blk == 0:
                        ms = [3] if g < 2 else [0, 1, 2, 3]
                        tts = [15] if g == 0 else ([12, 13, 14, 15] if g == 1 else list(range(16)))
                    else:
                        ms = [0, 1, 2, 3]
                        tts = list(range(16))
                    if kind in (0, 1, 3):
                        for hh in range(2):
                            h = h0 + hh
                            qi = qk % 2
                            qk += 1
                            for m in ms:
                                p = pp[ppk % 4]
                                ppk += 1
                                for c in range(16):
                                    rhs = xT[:, c, m * 512:(m + 1) * 512]
                                    S.op("pe", lambda e, p=p, wi=wi, c=c, hh=hh, rhs=rhs: e.matmul(
                                        p[:], wb[wi][:, c, hh * 128:(hh + 1) * 128], rhs, start=(c == 0), stop=(c == 15)),
                                        reads=[wb[wi], xT], writes=[p], signal=(c == 15))
                                if kind == 3:
                                    dst = qst[qi][:, m * 512:(m + 1) * 512]
                                    S.op("act", lambda e, p=p, dst=dst: e.activation(out=dst, in_=p[:], func=AF.Copy),
                                         reads=[p], writes=[qst[qi]])
                                else:
                                    ti = ppk % 2
                                    ms_ = slice(m * 512, (m + 1) * 512)
                                    S.op("dve", lambda e, p=p, ti=ti, ms_=ms_: e.tensor_tensor(out=t1[ti][:], in0=p[:], in1=Cb[:, ms_], op=ALU.mult),
                                         reads=[p, Cb], writes=[t1[ti]])
                                    S.op("dve", lambda e, p=p, ti=ti, ms_=ms_: e.tensor_tensor(out=t2[ti][0:64, :], in0=p[64:128, :], in1=Sb[0:64, ms_], op=ALU.mult),
                                         reads=[p, Sb], writes=[t2[ti]])
                                    S.op("dve", lambda e, p=p, ti=ti, ms_=ms_: e.tensor_tensor(out=t2[ti][64:128, :], in0=p[0:64, :], in1=Sb[64:128, ms_], op=ALU.mult),
                                         reads=[p, Sb], writes=[t2[ti]])
                                    if g == 0:
                                        dst = qst[qi][:, ms_]
                                        a0, a1 = t1[ti][:], t2[ti][:]
                                    elif g == 1:
                                        dst = sb_ap(qst[qi], 512 * m, [[2048, 128], [1, 128], [128, 4]])
                                        a0 = sb_ap(t1[ti], 0, [[512, 128], [4, 128], [1, 4]])
                                        a1 = sb_ap(t2[ti], 0, [[512, 128], [4, 128], [1, 4]])
                                    else:
                                        dst = sb_ap(qst[qi], 32 * m, [[2048, 128], [1, 32], [128, 16]])
                                        a0 = sb_ap(t1[ti], 0, [[512, 128], [16, 32], [1, 16]])
                                        a1 = sb_ap(t2[ti], 0, [[512, 128], [16, 32], [1, 16]])
                                    S.op("pool", lambda e, dst=dst, a0=a0, a1=a1: e.tensor_tensor(out=dst, in0=a0, in1=a1, op=ALU.add),
                                         reads=[t1[ti], t2[ti]], writes=[qst[qi]])
                            c_lo, c_hi = ms[0] * 512, (ms[-1] + 1) * 512
                            if kind == 0:
                                S.dma("pool", QT[g * 16 + h, :, (blk - 1) * 2048 + c_lo:(blk - 1) * 2048 + c_hi], qst[qi][:, c_lo:c_hi],
                                      reads=[qst[qi]], writes=[QTd])
                            elif kind == 1:
                                S.dma("pool", KT[g * 16 + h, :, blk * 2048 + c_lo:blk * 2048 + c_hi], qst[qi][:, c_lo:c_hi],
                                      reads=[qst[qi]], writes=[KTd])
                            else:
                                S.dma("pool", ZT[h * 128:(h + 1) * 128, (blk - 1) * 2048:blk * 2048], qst[qi][:],
                                      reads=[qst[qi]], writes=[ZTd])
                    else:
                        for tt in tts:
                            off, dims = dil_tile_dims(g, tt)
                            p = pv[vk % 2]
                            vi = vk % 2
                            vk += 1
                            for c in range(16):
                                lhsT = sb_ap(xT, c * 2048 + off, [[16 * 2048, 128]] + dims)
                                S.op("pe", lambda e, p=p, wi=wi, c=c, lhsT=lhsT: e.matmul(
                                    p[:], lhsT, wb[wi][:, c, :], start=(c == 0), stop=(c == 15)),
                                    reads=[wb[wi], xT], writes=[p], signal=(c == 15))
                            S.op("act", lambda e, p=p, vi=vi: e.activation(out=vst[vi][:], in_=p[:], func=AF.Copy),
                                 reads=[p], writes=[vst[vi]])
                            S.dma("act", VV[g, blk * 2048 + tt * 128:blk * 2048 + (tt + 1) * 128, h0 * 128:h0 * 128 + 256], vst[vi][:],
                                  reads=[vst[vi]], writes=[VVd])
        S.barrier()
        with ExitStack() as es:
            S.es = es
            msk = S.sb("msk", [128, 512], BF16)
            S.dma("sp", msk[:], masks_in, writes=[msk])
            accO = S.sb("accO", [128, NTOK])
            accL = S.sb("accL", [128, NTOK])
            qt = [S.sb("qt%d" % i, [128, 4096], BF16) for i in range(2)]
            kt = [S.sb("kt%d" % i, [128, 6144], BF16) for i in range(2)]
            vt = [S.sb("vt%d" % i, [128, 48, 128], BF16) for i in range(2)]
            zt = S.sb("zt", [128, NTOK], BF16)
            gst = S.sb("gst", [128, NTOK], BF16)
            pt = [S.sb("pt%d" % i, [128, 256], BF16) for i in range(4)]
            rl = [S.sb("rl%d" % i, [128, 512]) for i in range(2)]
            ot = [S.sb("ot%d" % i, [128, 512]) for i in range(2)]
            sz = [S.sb("sz%d" % i, [128, 512]) for i in range(2)]
            ps_s = [S.ps("ps_s%d" % i, [128, 256]) for i in range(4)]
            po = [S.ps("po%d" % i, [128, 512]) for i in range(2)]
            pl = [S.ps("pl%d" % i, [128, 512]) for i in range(2)]
            scale = 128.0 ** -0.5
            hg = [(h, g) for h in range(16) for g in range(3)]

            def load_hg(idx):
                h, g = hg[idx]
                bi = idx % 2
                S.dma("sp", qt[bi][:], QT[g * 16 + h], reads=[QTd], writes=[qt[bi]])
                S.dma("sp", kt[bi][:], KT[g * 16 + h], reads=[KTd], writes=[kt[bi]])
                S.dma("sp", vt[bi][:], VV[g, :, h * 128:(h + 1) * 128].rearrange("(t p) d -> p t d", p=128),
                      reads=[VVd], writes=[vt[bi]])

            load_hg(0)
            sk = 0
            ok = 0
            for h in range(16):
                S.dma("sp", zt[:], ZT[h * 128:(h + 1) * 128, :], reads=[ZTd], writes=[zt])
                for g in range(3):
                    idx = h * 3 + g
                    bi = idx % 2
                    if idx + 1 < len(hg):
                        load_hg(idx + 1)
                    Pg = 128 * DIL[g]
                    qbs = []
                    for sbk in range(2):
                        for m in range(4):
                            for qb in range(4):
                                col0 = sbk * 2048 + m * 512 + qb * 128
                                qbs.append((sbk, m, qb, col0, 2048 + col0, 2048 + col0 - Pg))

                    def emitS(i):
                        (sbk, m, qb, col0, kc, prev) = qbs[i]
                        si = (sk + i) % 4
                        S.op("pe", lambda e: e.matmul(ps_s[si][:, 0:128], kt[bi][:, prev:prev + 128], qt[bi][:, col0:col0 + 128],
                                                      start=True, stop=True),
                             reads=[kt[bi], qt[bi]], writes=[ps_s[si]], signal=False)
                        S.op("pe", lambda e: e.matmul(ps_s[si][:, 128:256], kt[bi][:, kc:kc + 128], qt[bi][:, col0:col0 + 128],
                                                      start=True, stop=True),
                             reads=[kt[bi], qt[bi]], writes=[ps_s[si]])

                    emitS(0)
                    emitS(1)
                    for i, (sbk, m, qb, col0, kc, prev) in enumerate(qbs):
                        if i + 2 < len(qbs):
                            emitS(i + 2)
                        si = (sk + i) % 4
                        if qb == 0:
                            oi = ok % 2
                            ok += 1
                        S.op("act", lambda e, si=si: e.activation(out=pt[si][:], in_=ps_s[si][:], func=AF.Exp, scale=scale),
                             reads=[ps_s[si]], writes=[pt[si]])
                        moff = 256 if prev < 2048 else 0
                        S.op("dve", lambda e, si=si, moff=moff: e.tensor_tensor(out=pt[si][:], in0=pt[si][:], in1=msk[:, moff:moff + 256], op=ALU.mult),
                             reads=[pt[si], msk], writes=[pt[si]])
                        oc = slice(qb * 128, (qb + 1) * 128)
                        S.op("pe", lambda e, oi=oi, si=si, prev=prev, oc=oc: e.matmul(
                            po[oi][:, oc], vt[bi][:, prev // 128, :], pt[si][:, 0:128], start=True, stop=False),
                            reads=[vt[bi], pt[si]], writes=[po[oi]], signal=False)
                        S.op("pe", lambda e, oi=oi, si=si, kc=kc, oc=oc: e.matmul(
                            po[oi][:, oc], vt[bi][:, kc // 128, :], pt[si][:, 128:256], start=False, stop=True),
                            reads=[vt[bi], pt[si]], writes=[po[oi]], signal=False)
                        S.op("pe", lambda e, oi=oi, si=si, oc=oc: e.matmul(
                            pl[oi][:, oc], ones[:], pt[si][:, 0:128], start=True, stop=False),
                            reads=[ones, pt[si]], writes=[pl[oi]], signal=False)
                        S.op("pe", lambda e, oi=oi, si=si, oc=oc: e.matmul(
                            pl[oi][:, oc], ones[:], pt[si][:, 128:256], start=False, stop=True),
                            reads=[ones, pt[si], vt[bi]], writes=[pl[oi], po[oi]])
                        if qb == 3:
                            off, dims = dil_dims(g, m)
                            dO = sb_ap(accO, sbk * 2048 + off, [[NTOK, 128]] + dims)
                            dL = sb_ap(accL, sbk * 2048 + off, [[NTOK, 128]] + dims)
                            if g == 0:
                                S.op("dve", lambda e, oi=oi, dO=dO: e.tensor_copy(out=dO, in_=po[oi][:]), reads=[po[oi]], writes=[accO])
                                S.op("act", lambda e, oi=oi, dL=dL: e.activation(out=dL, in_=pl[oi][:], func=AF.Copy), reads=[pl[oi]], writes=[accL])
                            else:
                                S.op("dve", lambda e, oi=oi, dO=dO: e.tensor_tensor(out=dO, in0=po[oi][:], in1=dO, op=ALU.add),
                                     reads=[po[oi], accO], writes=[accO])
                                S.op("dve", lambda e, oi=oi, dL=dL: e.tensor_tensor(out=dL, in0=pl[oi][:], in1=dL, op=ALU.add),
                                     reads=[pl[oi], accL], writes=[accL])
                    sk += len(qbs)
                for c8 in range(8):
                    i = c8 % 2
                    cs = slice(c8 * 512, (c8 + 1) * 512)
                    S.op("dve", lambda e, i=i, cs=cs: e.reciprocal(out=rl[i][:], in_=accL[:, cs]), reads=[accL], writes=[rl[i]])
                    S.op("pool", lambda e, i=i, cs=cs: e.tensor_tensor(out=ot[i][:], in0=accO[:, cs], in1=rl[i][:], op=ALU.mult),
                         reads=[accO, rl[i]], writes=[ot[i]])
                    S.op("act", lambda e, i=i, cs=cs: e.activation(out=sz[i][:], in_=zt[:, cs], func=AF.Silu), reads=[zt], writes=[sz[i]])
                    S.op("pool", lambda e, i=i, cs=cs: e.tensor_tensor(out=gst[:, cs], in0=ot[i][:], in1=sz[i][:], op=ALU.mult),
                         reads=[ot[i], sz[i]], writes=[gst])
                S.dma("act", GT[h * 128:(h + 1) * 128, :], gst[:], reads=[gst], writes=[GTd])
        S.barrier()
        with ExitStack() as es:
            S.es = es
            GTv = GT.rearrange("(c p) t -> p c t", p=128)
            emit_epilogue(S, ident, lambda t, dst: S.dma("sp", dst[:], GTv[:, :, t * 128:(t + 1) * 128], reads=[GTd], writes=[dst]),
                          resid, resid_d, wout, lng, lnb, ro, rod, XTo, xtd)
        S.barrier()


def emit_mla_a(S, ident, ones, io):
    w_in_c, w_uq_c, w_kk_c, w_kv_c, qg_in, kg_in, cmask_in = (io[k] for k in ("w_in_c", "w_uq_c", "w_kk_c", "w_kv_c", "qg", "kg", "cmask"))
    GTP, GTPd = io["GTP"], io["GTP_d"]
    QN, QP, KN, KP, VV, SZT, Ctab, Stab = (io[k] for k in ("QN", "QP", "KN", "KP", "VVm", "SZT", "Ctab64", "Stab64"))
    QNd, QPd, KNd, KPd, VVd, SZd, Cd, Sd = (io[k + "_d"] for k in ("QN", "QP", "KN", "KP", "VVm", "SZT", "Ctab64", "Stab64"))
    if True:
        with ExitStack() as es:
            S.es = es
            wi = S.sb("wi", [128, 16, 1600], BF16)
            wuq = S.sb("wuq", [128, 4, 768], BF16)
            wkk = S.sb("wkk", [128, 4, 512], BF16)
            wkv = S.sb("wkv", [128, 4, 512], BF16)
            qg = S.sb("qg", [128, 4])
            kg = S.sb("kg", [128, 4])
            wst = [S.sb("wst%d" % i, [128, 4, 800]) for i in range(2)]
            S.dma("sp", qg[:], qg_in, writes=[qg])
            S.dma("sp", kg[:], kg_in, writes=[kg])
            w_inv = w_in_c.rearrange("(c p) n -> p c n", p=128)
            k = 0
            for c4 in range(4):
                for half in range(2):
                    i = k % 2
                    k += 1
                    S.dma("sp", wst[i][:], w_inv[:, c4 * 4:(c4 + 1) * 4, half * 800:(half + 1) * 800], writes=[wst[i]])
                    S.op("pool", lambda e, i=i, c4=c4, half=half: e.tensor_copy(
                        out=wi[:, c4 * 4:(c4 + 1) * 4, half * 800:(half + 1) * 800], in_=wst[i][:]), reads=[wst[i]], writes=[wi])
            for (src, dstw, gain, ncol) in ((w_uq_c, wuq, qg, 768), (w_kk_c, wkk, kg, 512), (w_kv_c, wkv, kg, 512)):
                i = k % 2
                k += 1
                S.dma("sp", wst[i][:, :, 0:ncol], src.rearrange("(c p) n -> p c n", p=128), writes=[wst[i]])
                for c in range(4):
                    S.op("dve", lambda e, i=i, c=c, dstw=dstw, gain=gain, ncol=ncol: e.tensor_scalar(
                        out=dstw[:, c, :], in0=wst[i][:, c, 0:ncol], scalar1=gain[:, c:c + 1], scalar2=None, op0=ALU.mult),
                        reads=[wst[i], gain], writes=[dstw])
            xT = [S.sb("xT%d" % i, [128, 16, 512], BF16) for i in range(2)]
            Cb = [S.sb("Cb%d" % i, [64, 512]) for i in range(2)]
            Sb = [S.sb("Sb%d" % i, [64, 512]) for i in range(2)]
            cqb = S.sb("cqb", [128, 4, 512], BF16)
            sq = S.sb("sq", [128, 4, 512], BF16)
            ckvb = S.sb("ckvb", [128, 4, 512], BF16)
            sq2 = S.sb("sq2", [128, 4, 512], BF16)
            rq = S.sb("rq", [128, 512])
            rk = S.sb("rk", [128, 512])
            rtok = S.sb("rtok", [128, 4])
            st = [S.sb("st%d" % i, [128, 512], BF16) for i in range(3)]
            t1 = S.sb("t1", [64, 512])
            ta = S.sb("ta", [64, 512])
            tb = S.sb("tb", [64, 512])
            pa = [S.ps("pa%d" % i, [128, 512]) for i in range(3)]
            pb = S.ps("pb", [128, 512])
            pc = [S.ps("pc%d" % i, [64, 512]) for i in range(2)]
            pd = S.ps("pd", [128, 4])
            cnt = {"pa": 0, "st": 0, "pc": 0}

            def nxt(key, n):
                cnt[key] += 1
                return (cnt[key] - 1) % n

            def big_mm(p, lhs_fn, xi):
                for c in range(16):
                    S.op("pe", lambda e, c=c: e.matmul(p[:], lhs_fn(c), xT[xi][:, c, :], start=(c == 0), stop=(c == 15)),
                         reads=[wi, xT[xi]], writes=[p], signal=(c == 15))

            def rstd_from(ps_buf, dst, width):
                S.op("dve", lambda e: e.tensor_scalar(out=dst[:, 0:width], in0=ps_buf[:, 0:width], scalar1=1.0 / 512, scalar2=RMS_EPS,
                                                      op0=ALU.mult, op1=ALU.add), reads=[ps_buf], writes=[dst])
                S.op("act", lambda e: e.activation(out=dst[:, 0:width], in_=dst[:, 0:width], func=AF.Sqrt), reads=[dst], writes=[dst])
                S.op("dve", lambda e: e.reciprocal(out=dst[:, 0:width], in_=dst[:, 0:width]), reads=[dst], writes=[dst])

            def rope64(srcbuf, xi, dst_ap, dstd, post=None):
                si = nxt("st", 3)
                S.op("dve", lambda e: e.tensor_tensor(out=ta[:], in0=srcbuf[0:64, :], in1=Cb[xi][:], op=ALU.mult),
                     reads=[srcbuf, Cb[xi]], writes=[ta])
                S.op("dve", lambda e: e.tensor_tensor(out=tb[0:32, :], in0=srcbuf[32:64, :], in1=Sb[xi][0:32, :], op=ALU.mult),
                     reads=[srcbuf, Sb[xi]], writes=[tb])
                S.op("dve", lambda e: e.tensor_tensor(out=tb[32:64, :], in0=srcbuf[0:32, :], in1=Sb[xi][32:64, :], op=ALU.mult),
                     reads=[srcbuf, Sb[xi]], writes=[tb])
                if post is None:
                    S.op("pool", lambda e: e.tensor_tensor(out=st[si][0:64, :], in0=ta[:], in1=tb[:], op=ALU.add),
                         reads=[ta, tb], writes=[st[si]])
                else:
                    S.op("pool", lambda e: e.tensor_tensor(out=t1[:], in0=ta[:], in1=tb[:], op=ALU.add),
                         reads=[ta, tb], writes=[t1])
                    S.op("dve", lambda e: e.tensor_tensor(out=st[si][0:64, :], in0=t1[:], in1=post[0:64, :], op=ALU.mult),
                         reads=[t1, post], writes=[st[si]])
                S.dma("act", dst_ap, st[si][0:64, :], reads=[st[si]], writes=[dstd])

            def load_blk(b):
                xi = b % 2
                io["xt_load"](b, xT[xi])
                S.dma("sp", Cb[xi][:], Ctab[:, b * 512:(b + 1) * 512], reads=[Cd], writes=[Cb[xi]])
                S.dma("sp", Sb[xi][:], Stab[:, b * 512:(b + 1) * 512], reads=[Sd], writes=[Sb[xi]])

            load_blk(0)
            for b in range(SEQ // 512):
                xi = b % 2
                if b + 1 < SEQ // 512:
                    load_blk(b + 1)
                bs = slice(b * 512, (b + 1) * 512)
                for (coff, cb_, sq_, rr) in ((0, cqb, sq, rq), (512, ckvb, sq2, rk)):
                    for f in range(4):
                        p = pa[nxt("pa", 3)]
                        big_mm(p, lambda c, f=f, coff=coff: wi[:, c, coff + f * 128:coff + (f + 1) * 128], xi)
                        S.op("act", lambda e, p=p, f=f, cb_=cb_: e.activation(out=cb_[:, f, :], in_=p[:], func=AF.Copy),
                             reads=[p], writes=[cb_])
                        S.op("act", lambda e, p=p, f=f, sq_=sq_: e.activation(out=sq_[:, f, :], in_=p[:], func=AF.Square),
                             reads=[p], writes=[sq_])
                    for f in range(4):
                        S.op("pe", lambda e, f=f, sq_=sq_: e.matmul(pb[:], ones[:], sq_[:, f, :], start=(f == 0), stop=(f == 3)),
                             reads=[ones, sq_], writes=[pb], signal=(f == 3))
                    rstd_from(pb, rr, 512)
                for tt in range(4):
                    for f in range(4):
                        S.op("pe", lambda e, f=f, tt=tt: e.matmul(pd[:, tt:tt + 1], sq2[:, f, tt * 128:(tt + 1) * 128], ones[:, 0:1],
                                                                  start=(f == 0), stop=(f == 3)),
                             reads=[ones, sq2], writes=[pd], signal=(f == 3 and tt == 3))
                rstd_from(pd, rtok, 4)
                for h in range(4):
                    p = pa[nxt("pa", 3)]
                    for f in range(4):
                        S.op("pe", lambda e, p=p, f=f, h=h: e.matmul(p[:], wuq[:, f, h * 192:h * 192 + 128], cqb[:, f, :],
                                                                     start=(f == 0), stop=(f == 3)),
                             reads=[wuq, cqb], writes=[p], signal=(f == 3))
                    si = nxt("st", 3)
                    S.op("dve", lambda e, p=p, si=si: e.tensor_tensor(out=st[si][:], in0=p[:], in1=rq[:], op=ALU.mult),
                         reads=[p, rq], writes=[st[si]])
                    S.dma("act", QN[h, :, bs], st[si][:], reads=[st[si]], writes=[QNd])
                    p2 = pc[nxt("pc", 2)]
                    for f in range(4):
                        S.op("pe", lambda e, p2=p2, f=f, h=h: e.matmul(p2[:], wuq[:, f, h * 192 + 128:h * 192 + 192], cqb[:, f, :],
                                                                       start=(f == 0), stop=(f == 3)),
                             reads=[wuq, cqb], writes=[p2], signal=(f == 3))
                    rope64(p2, xi, QP[h, :, bs], QPd, post=rq)
                p2 = pc[nxt("pc", 2)]
                for c in range(16):
                    S.op("pe", lambda e, p2=p2, c=c: e.matmul(p2[:], wi[:, c, 1024:1088], xT[xi][:, c, :], start=(c == 0), stop=(c == 15)),
                         reads=[wi, xT[xi]], writes=[p2], signal=(c == 15))
                rope64(p2, xi, KP[:, bs], KPd)
                for h in range(4):
                    p = pa[nxt("pa", 3)]
                    for f in range(4):
                        S.op("pe", lambda e, p=p, f=f, h=h: e.matmul(p[:], wkk[:, f, h * 128:(h + 1) * 128], ckvb[:, f, :],
                                                                     start=(f == 0), stop=(f == 3)),
                             reads=[wkk, ckvb], writes=[p], signal=(f == 3))
                    si = nxt("st", 3)
                    S.op("dve", lambda e, p=p, si=si: e.tensor_tensor(out=st[si][:], in0=p[:], in1=rk[:], op=ALU.mult),
                         reads=[p, rk], writes=[st[si]])
                    S.dma("act", KN[h, :, bs], st[si][:], reads=[st[si]], writes=[KNd])
                for tt in range(4):
                    p = pa[nxt("pa", 3)]
                    for f in range(4):
                        S.op("pe", lambda e, p=p, f=f, tt=tt: e.matmul(p[:], ckvb[:, f, tt * 128:(tt + 1) * 128], wkv[:, f, :],
                                                                       start=(f == 0), stop=(f == 3)),
                             reads=[wkv, ckvb], writes=[p], signal=(f == 3))
                    si = nxt("st", 3)
                    S.op("dve", lambda e, p=p, si=si, tt=tt: e.tensor_scalar(out=st[si][:], in0=p[:], scalar1=rtok[:, tt:tt + 1], scalar2=None,
                                                                             op0=ALU.mult), reads=[p, rtok], writes=[st[si]])
                    S.dma("act", VV[b * 512 + tt * 128:b * 512 + (tt + 1) * 128, :], st[si][:], reads=[st[si]], writes=[VVd])
                for f in range(4):
                    p = pa[nxt("pa", 3)]
                    big_mm(p, lambda c, f=f: wi[:, c, 1088 + f * 128:1088 + (f + 1) * 128], xi)
                    si = nxt("st", 3)
                    S.op("act", lambda e, p=p, si=si: e.activation(out=st[si][:], in_=p[:], func=AF.Silu), reads=[p], writes=[st[si]])
                    S.dma("act", SZT[f * 128:(f + 1) * 128, bs], st[si][:], reads=[st[si]], writes=[SZd])
        S.barrier()
        with ExitStack() as es:
            S.es = es
            cm = S.sb("cm", [128, 2048], BF16)
            S.dma("sp", cm[:], cmask_in, writes=[cm])
            kn = S.sb("kn", [128, SEQ], BF16)
            kp = S.sb("kp", [64, SEQ], BF16)
            vt = S.sb("vt", [128, 128, 128], BF16)
            S.dma("sp", kp[:], KP, reads=[KPd], writes=[kp])
            qn = [S.sb("qn%d" % i, [128, 512], BF16) for i in range(2)]
            qp = [S.sb("qp%d" % i, [64, 512], BF16) for i in range(2)]
            szb = [S.sb("szb%d" % i, [128, 512], BF16) for i in range(2)]
            pt = [S.sb("pt%d" % i, [128, 512], BF16) for i in range(3)]
            rl = [S.sb("rl%d" % i, [128, 512]) for i in range(2)]
            ot = [S.sb("ot%d" % i, [128, 512]) for i in range(2)]
            gst = [S.sb("gst%d" % i, [128, 512], BF16) for i in range(2)]
            lacc = [S.sb("lacc%d" % i, [128, 512]) for i in range(2)]
            ones_f = S.sb("ones_f", [128, 128])
            S.op("pool", lambda e: e.memset(ones_f[:], 1.0), writes=[ones_f])
            ps_s = [S.ps("ps_s%d" % i, [128, 512]) for i in range(3)]
            po = [S.ps("po%d" % i, [128, 512]) for i in range(2)]
            pl = [S.ps("pl%d" % i, [128, 512]) for i in range(2)]
            scale = 192.0 ** -0.5
            sk = [0]
            work = [(h, qc) for h in range(4) for qc in range(SEQ // 512)]

            def load_q(idx):
                h, qc = work[idx]
                i = idx % 2
                cs = slice(qc * 512, (qc + 1) * 512)
                S.dma("sp", qn[i][:], QN[h, :, cs], reads=[QNd], writes=[qn[i]])
                S.dma("sp", qp[i][:], QP[h, :, cs], reads=[QPd], writes=[qp[i]])
                S.dma("sp", szb[i][:], SZT[h * 128:(h + 1) * 128, cs], reads=[SZd], writes=[szb[i]])

            for idx, (h, qc) in enumerate(work):
                i = idx % 2
                if qc == 0:
                    S.dma("sp", kn[:], KN[h], reads=[KNd], writes=[kn])
                    for v4 in range(4):
                        S.dma("sp", vt[:, v4 * 32:(v4 + 1) * 32, :],
                              VV[v4 * 4096:(v4 + 1) * 4096, h * 128:(h + 1) * 128].rearrange("(t p) d -> p t d", p=128),
                              reads=[VVd], writes=[vt])
                    load_q(idx)
                if idx + 1 < len(work) and work[idx + 1][1] != 0:
                    load_q(idx + 1)
                nk = 4 * (qc + 1)

                def Sm(kt):
                    si = (sk[0] + kt) % 3
                    S.op("pe", lambda e: e.matmul(ps_s[si][:], kn[:, kt * 128:(kt + 1) * 128], qn[i][:], start=True, stop=False),
                         reads=[kn, qn[i]], writes=[ps_s[si]], signal=False)
                    S.op("pe", lambda e: e.matmul(ps_s[si][:], kp[:, kt * 128:(kt + 1) * 128], qp[i][:], start=False, stop=True),
                         reads=[kp, qp[i], kn, qn[i]], writes=[ps_s[si]])

                Sm(0)
                for kt in range(nk):
                    if kt + 1 < nk:
                        Sm(kt + 1)
                    si = (sk[0] + kt) % 3
                    S.op("act", lambda e, si=si: e.activation(out=pt[si][:], in_=ps_s[si][:], func=AF.Exp, scale=scale),
                         reads=[ps_s[si]], writes=[pt[si]])
                    d = kt - 4 * qc
                    if d >= 0:
                        S.op("dve", lambda e, si=si, d=d: e.tensor_tensor(out=pt[si][:], in0=pt[si][:], in1=cm[:, d * 512:(d + 1) * 512], op=ALU.mult),
                             reads=[pt[si], cm], writes=[pt[si]])
                    S.op("pe", lambda e, si=si, kt=kt: e.matmul(po[i][:], vt[:, kt, :], pt[si][:], start=(kt == 0), stop=(kt == nk - 1)),
                         reads=[vt, pt[si]], writes=[po[i]])
                    if kt == 0:
                        S.op("dve", lambda e, si=si: e.tensor_copy(out=lacc[i][:], in_=pt[si][:]), reads=[pt[si]], writes=[lacc[i]])
                    else:
                        S.op("dve", lambda e, si=si: e.tensor_tensor(out=lacc[i][:], in0=lacc[i][:], in1=pt[si][:], op=ALU.add),
                             reads=[lacc[i], pt[si]], writes=[lacc[i]])
                sk[0] += nk
                S.op("pe", lambda e: e.matmul(pl[i][:], ones_f[:], lacc[i][:], start=True, stop=True),
                     reads=[ones_f, lacc[i]], writes=[pl[i]])
                S.op("dve", lambda e: e.reciprocal(out=rl[i][:], in_=pl[i][:]), reads=[pl[i]], writes=[rl[i]])
                S.op("dve", lambda e: e.tensor_tensor(out=ot[i][:], in0=po[i][:], in1=rl[i][:], op=ALU.mult),
                     reads=[po[i], rl[i]], writes=[ot[i]])
                S.op("pool", lambda e: e.tensor_tensor(out=gst[i][:], in0=ot[i][:], in1=szb[i][:], op=ALU.mult),
                     reads=[ot[i], szb[i]], writes=[gst[i]])
                S.dma("act", GTP[(qc // 8) * 512 + h * 128:(qc // 8) * 512 + (h + 1) * 128, (qc % 8) * 512:(qc % 8 + 1) * 512], gst[i][:],
                      reads=[gst[i]], writes=[GTPd])
                if qc == SEQ // 512 - 1:
                    io["gt_head_done"](h)
        S.barrier()


def _inv_freq(dim):
    return np.asarray(1.0 / (10000.0 ** (jnp.arange(0, dim, 2, dtype=jnp.float32) / dim)), dtype=np.float32)


def _dil_consts():
    inv = _inv_freq(128)
    inv128 = np.concatenate([inv, inv]).reshape(128, 1).astype(np.float32)
    sgn = np.concatenate([-np.ones(64), np.ones(64)]).reshape(128, 1).astype(np.float32)
    k = np.arange(128)[:, None]
    q = np.arange(128)[None, :]
    m_prev = (k >= q).astype(np.float32)
    m_cur = (k <= q).astype(np.float32)
    return inv128, sgn, m_prev, m_cur


_PROGS = {}


def build_fused():
    nc = bass.Bass("TRN2", target_bir_lowering=False)

    def I(name, shape, dt=F32):
        return nc.dram_tensor(name, shape, dt, kind="ExternalInput").ap()

    def T(name, shape, dt):
        return nc.dram_tensor(name, shape, dt, kind="Internal").ap()

    x_own = I("x_own", [NTOK, D])
    x_halo = I("x_halo", [2048, D])
    pos_d = I("pos_d", [1, 6144], I32)
    pos_a = I("pos_a", [1, SEQ], I32)
    inv128, sgn128 = I("inv128", [128, 1]), I("sgn128", [128, 1])
    inv64, sgn64 = I("inv64", [64, 1]), I("sgn64", [64, 1])
    masks = I("masks", [128, 512], BF16)
    cmask = I("cmask", [128, 2048], BF16)
    dsa_w_in = [I("dsa_w_in%d" % j, [D, 20480]) for j in range(2)]
    dsa_w_out = [I("dsa_w_out%d" % j, [D, D]) for j in range(2)]
    mla_w_in_c = [I("mla_w_in_c%d" % j, [D, 1600]) for j in range(2)]
    mla_w_uq_c = [I("mla_w_uq_c%d" % j, [512, 768]) for j in range(2)]
    mla_w_kk_c = [I("mla_w_kk_c%d" % j, [512, 512]) for j in range(2)]
    mla_w_kv_c = [I("mla_w_kv_c%d" % j, [512, 512]) for j in range(2)]
    mla_qg = [I("mla_qg%d" % j, [128, 4]) for j in range(2)]
    mla_kg = [I("mla_kg%d" % j, [128, 4]) for j in range(2)]
    mla_w_out = [I("mla_w_out%d" % j, [D, D]) for j in range(2)]
    lng = [I("lng%d" % l, [1, D]) for l in range(DEPTH)]
    lnb = [I("lnb%d" % l, [1, D]) for l in range(DEPTH)]
    out = nc.dram_tensor("out", [NTOK, D], F32, kind="ExternalOutput").ap()
    XTS_t = nc.dram_tensor("XT_send", [D, NTOK], BF16)
    XTALL_t = nc.dram_tensor("XT_allr", [4 * D, NTOK], BF16)
    GTS_t = nc.dram_tensor("GT_send", [2048, NTOK], BF16)
    GTALL_t = nc.dram_tensor("GT_allr", [4 * 2048, NTOK], BF16)
    XTS, XTALL, GTS, GTALL = XTS_t.ap(), XTALL_t.ap(), GTS_t.ap(), GTALL_t.ap()
    XTH0 = T("XT_halo0", [D, 2048], BF16)
    R = [T("R%d" % i, [NTOK, D], F32) for i in range(3)]
    scr = {
        "QT": T("QT", [48, 128, 4096], BF16), "KT": T("KT", [48, 128, 6144], BF16), "VV": T("VV", [3, 6144, D], BF16),
        "ZT": T("ZT", [D, NTOK], BF16), "GT": T("GT", [D, NTOK], BF16),
        "Ctab": T("Ctab", [128, 6144], F32), "Stab": T("Stab", [128, 6144], F32),
        "QN": T("QN", [4, 128, SEQ], BF16), "QP": T("QP", [4, 64, SEQ], BF16), "KN": T("KN", [4, 128, SEQ], BF16),
        "KP": T("KP", [64, SEQ], BF16), "VVm": T("VVm", [SEQ, 512], BF16), "SZT": T("SZT", [512, SEQ], BF16),
        "Ctab64": T("Ctab64", [64, SEQ], F32), "Stab64": T("Stab64", [64, SEQ], F32),
    }
    with ExitStack() as es0:
        S = Sched(nc, es0)
        ident, ones = make_consts(S)
        io0 = dict(scr)
        for k in list(scr):
            io0[k + "_d"] = S.dram(scr[k])
        XTS_d, XTALL_d, GTS_d, GTALL_d, XTH0_d = S.dram(XTS), S.dram(XTALL), S.dram(GTS), S.dram(GTALL), S.dram(XTH0)
        R_d = [S.dram(r_) for r_ in R]
        out_d = S.dram(out)
        ext = Buf(None)
        pid = nc.gpsimd.partition_id()
        rown = (pid % 4) * 2048
        rprev = (pid + 3) % 4
        XTALL4 = XTALL.rearrange("(c s p) t -> c s p t", c=16, s=4)
        for (pp_, n_, iv, sg, ck, sk_, P) in ((pos_d, 6144, inv128, sgn128, "Ctab", "Stab", 128),
                                             (pos_a, SEQ, inv64, sgn64, "Ctab64", "Stab64", 64)):
            with ExitStack() as es:
                S.es = es
                inv = S.sb("inv", [P, 1])
                sgn = S.sb("sgn", [P, 1])
                S.dma("sp", inv[:], iv, writes=[inv])
                S.dma("sp", sgn[:], sg, writes=[sgn])
                rope_tables(S, pp_, n_, inv, sgn, io0[ck + "_d"], io0[sk_ + "_d"], P)
            S.barrier()
        emit_p0(S, ident, x_own, NTOK, XTS, XTS_d)
        emit_p0(S, ident, x_halo, 2048, XTH0, XTH0_d)
        XTH0v = XTH0.rearrange("(c p) t -> p c t", p=128)

        def xt_load(b, dst):
            rr, cb = b // 8, b % 8
            S.dma("sp", dst[:], XTALL4[:, rr, :, cb * 512:(cb + 1) * 512].rearrange("c p t -> p c t"),
                  reads=[XTALL_d], writes=[dst])

        GTloc, GTloc_d = scr["GT"], io0["GT_d"]
        GTlv = GTloc.rearrange("(c p) t -> p c t", p=128)

        def gt_gather():
            S.dma("pool", GTloc, GTALL[bass.ds(rown, 2048), :], reads=[GTALL_d], writes=[GTloc_d])

        def gt_head_done(h):
            for tr in range(4):
                k = tr * 4 + h
                S.collective(GTS[k * 128:(k + 1) * 128, :], GTALL[k * 512:(k + 1) * 512, :], GTS_d, GTALL_d)

        def gt_load_loc(t, dst):
            S.dma("sp", dst[:], GTlv[:, :, t * 128:(t + 1) * 128], reads=[GTloc_d], writes=[dst])

        def halo_static(xT):
            S.dma("sp", xT[:], XTH0v, reads=[XTH0_d], writes=[xT])

        def halo_dyn(xT):
            src = XTALL4[:, bass.ds(rprev, 1), :, 2048:4096].rearrange("c o p t -> p (c o) t")
            S.dma("pool", xT[:], src, reads=[XTALL_d], writes=[xT])

        resid_in, resid_in_d = x_own, ext
        for layer in range(DEPTH):
            j = layer // 2
            last = layer == DEPTH - 1
            ro, rod = (out, out_d) if last else (R[layer], R_d[layer])
            if layer % 2 == 0:
                io = dict(io0)
                io.update({"XT_own": XTS, "XT_own_d": XTS_d, "resid": resid_in, "resid_d": resid_in_d,
                           "w_in": dsa_w_in[j], "wout": dsa_w_out[j], "lng": lng[layer], "lnb": lnb[layer], "masks": masks,
                           "ro": ro, "rod": rod, "XTo": XTS, "xtd": XTS_d,
                           "halo_load": halo_static if layer == 0 else halo_dyn})
                emit_dil(S, ident, ones, io)
            else:
                io = dict(io0)
                io.update({"w_in_c": mla_w_in_c[j], "w_uq_c": mla_w_uq_c[j], "w_kk_c": mla_w_kk_c[j], "w_kv_c": mla_w_kv_c[j],
                           "qg": mla_qg[j], "kg": mla_kg[j], "cmask": cmask, "GTP": GTS, "GTP_d": GTS_d, "xt_load": xt_load,
                           "gt_head_done": gt_head_done})
                emit_mla_a(S, ident, ones, io)
                gt_gather()
                with ExitStack() as es:
                    S.es = es
                    emit_epilogue(S, ident, gt_load_loc, resid_in, resid_in_d, mla_w_out[j], lng[layer], lnb[layer],
                                  ro, rod, None if last else XTS, None if last else XTS_d, perm=MLA_PERM)
                S.barrier()
            if not last:
                S.allgather16(XTS_t, XTALL_t, XTS_d, XTALL_d)
            resid_in, resid_in_d = ro, rod
        S.drain([out_d])
    return nc


def kernel(x, positions, dsa_w_in, dsa_w_out, mla_w_in, mla_q_norm, mla_w_uq, mla_kv_norm, mla_w_ukv,
           mla_w_out, ln_g, ln_b):
    x = np.asarray(x)
    positions = np.asarray(positions)
    args = [np.asarray(a) for a in (dsa_w_in, dsa_w_out, mla_w_in, mla_q_norm, mla_w_uq, mla_kv_norm, mla_w_ukv, mla_w_out, ln_g, ln_b)]
    dsa_w_in, dsa_w_out, mla_w_in, mla_q_norm, mla_w_uq, mla_kv_norm, mla_w_ukv, mla_w_out, ln_g, ln_b = args
    if "fused" not in _PROGS:
        _PROGS["fused"] = build_fused()
    nc = _PROGS["fused"]
    bf = ml_dtypes.bfloat16
    inv128, sgn128, m_prev, m_cur = _dil_consts()
    inv = _inv_freq(64)
    inv64 = np.concatenate([inv, inv]).reshape(64, 1).astype(np.float32)
    sgn64 = np.concatenate([-np.ones(32), np.ones(32)]).reshape(64, 1).astype(np.float32)
    kk = np.arange(128)[:, None]
    qq = np.arange(512)[None, :]
    cmask = np.concatenate([((d * 128 + kk) <= qq).astype(np.float32) for d in range(4)], axis=1).astype(bf)
    in_maps = []
    for c in range(NCORES):
        bb, r = c // 4, c % 4
        m = {"x_own": np.ascontiguousarray(x[bb, r * NTOK:(r + 1) * NTOK])}
        if r == 0:
            m["x_halo"] = np.zeros((2048, D), np.float32)
            hpos = np.zeros((2048,), np.int32)
            m_halo = np.zeros_like(m_prev)
        else:
            m["x_halo"] = np.ascontiguousarray(x[bb, r * NTOK - 2048:r * NTOK])
            hpos = positions[bb, r * NTOK - 2048:r * NTOK]
            m_halo = m_prev
        m["masks"] = np.concatenate([m_prev, m_cur, m_halo, m_cur], axis=1).astype(bf)
        m["pos_d"] = np.concatenate([hpos, positions[bb, r * NTOK:(r + 1) * NTOK]]).reshape(1, 6144).astype(np.int32)
        m["pos_a"] = np.ascontiguousarray(positions[bb].reshape(1, SEQ)).astype(np.int32)
        m.update({"inv128": inv128, "sgn128": sgn128, "inv64": inv64, "sgn64": sgn64, "cmask": cmask})
        for j in range(2):
            m["dsa_w_in%d" % j] = dsa_w_in[j]
            m["dsa_w_out%d" % j] = dsa_w_out[j]
            w_in = mla_w_in[j]
            m["mla_w_in_c%d" % j] = np.ascontiguousarray(np.concatenate([w_in[:, 0:1088], w_in[:, 1088 + r * 512:1088 + (r + 1) * 512]], axis=1))
            m["mla_w_uq_c%d" % j] = np.ascontiguousarray(mla_w_uq[j][:, r * 768:(r + 1) * 768])
            wk4 = mla_w_ukv[j].reshape(512, 16, 256)[:, 4 * r:4 * r + 4]
            m["mla_w_kk_c%d" % j] = np.ascontiguousarray(wk4[:, :, 0:128].reshape(512, 512))
            m["mla_w_kv_c%d" % j] = np.ascontiguousarray(wk4[:, :, 128:256].reshape(512, 512))
            m["mla_qg%d" % j] = np.ascontiguousarray(mla_q_norm[j].reshape(4, 128).T).astype(np.float32)
            m["mla_kg%d" % j] = np.ascontiguousarray(mla_kv_norm[j].reshape(4, 128).T).astype(np.float32)
            m["mla_w_out%d" % j] = mla_w_out[j]
        for l in range(DEPTH):
            m["lng%d" % l] = np.ascontiguousarray(ln_g[l].reshape(1, D))
            m["lnb%d" % l] = np.ascontiguousarray(ln_b[l].reshape(1, D))
        in_maps.append(m)
    res = run_bass_kernel_spmd(nc, in_maps, core_ids=list(range(NCORES)))
    out = np.empty((2, SEQ, D), np.float32)
    for c in range(NCORES):
        out[c // 4, (c % 4) * NTOK:(c % 4 + 1) * NTOK] = res.results[c]["out"]
    return out
```

```python
import math
from contextlib import ExitStack

import numpy as np
import ml_dtypes
import jax.numpy as jnp

import concourse.bass as bass
import concourse.mybir as mybir
from concourse.bass_utils import run_bass_kernel_spmd

F32 = mybir.dt.float32
BF16 = mybir.dt.bfloat16
I32 = mybir.dt.int32
AF = mybir.ActivationFunctionType
ALU = mybir.AluOpType

D = 2048
SEQ = 16384
NTOK = 4096
DEPTH = 4
ALPHA = (2 * DEPTH) ** 0.25
LN_EPS = 1e-5
RMS_EPS = 1e-6
DIL = (1, 4, 16)
NCORES = 8
TWO_PI = 2.0 * math.pi
CW1 = 6.28125
CW2 = TWO_PI - CW1
MAGIC = 12582912.0


class Buf:
    __slots__ = ("t", "w", "r")

    def __init__(self, t):
        self.t = t
        self.w = {}
        self.r = {}

    def __getitem__(self, k):
        return self.t[k]


class Sched:
    def __init__(self, nc, es):
        self.nc = nc
        self.es = es
        self.es_sem = es
        self.eng = {"pe": nc.tensor, "act": nc.scalar, "dve": nc.vector, "pool": nc.gpsimd, "sp": nc.sync}
        self.esem = {}
        self.ecnt = {}
        self.known = {e: {} for e in self.eng}
        self.nsem = 0
        for e in ("pe", "act", "dve", "pool"):
            self._roll(e)
        self.dpool = {}
        self.dpos = {}
        for q, n in (("sp", 20), ("act", 6), ("pool", 8)):
            self.dpool[q] = [[self._newsem(), 0] for _ in range(n)]
            self.dpos[q] = 0

    def _newsem(self):
        self.nsem += 1
        return self.es_sem.enter_context(self.nc.semaphore("s%d" % self.nsem))

    def _roll(self, e):
        self.esem[e] = self._newsem()
        self.ecnt[e] = 0

    def sb(self, name, shape, dt=F32):
        self.nsem += 1
        return Buf(self.es.enter_context(self.nc.sbuf_tensor("sb%d_%s" % (self.nsem, name), shape, dt)))

    def ps(self, name, shape, dt=F32):
        self.nsem += 1
        return Buf(self.es.enter_context(self.nc.psum_tensor("ps%d_%s" % (self.nsem, name), shape, dt)))

    def dram(self, t):
        return Buf(t)

    def _waits(self, e, reads, writes):
        deps = {}

        def add(tok):
            if tok is None:
                return
            s, v = tok
            if deps.get(id(s), (None, 0))[1] < v:
                deps[id(s)] = (s, v)

        for b in reads:
            for s, v in b.w.values():
                add((s, v))
        for b in writes:
            for s, v in b.w.values():
                add((s, v))
            for s, v in b.r.values():
                add((s, v))
        kn = self.known[e]
        for sid, (s, v) in deps.items():
            if e == "pe" and s is self.esem["pe"]:
                continue
            if kn.get(sid, 0) >= v:
                continue
            self.eng[e].wait_ge(s, v)
            kn[sid] = v

    def _commit(self, tok, reads, writes):
        s, v = tok
        for b in reads:
            b.r[id(s)] = (s, v)
        for b in writes:
            b.w[id(s)] = tok
            b.r = {}

    def op(self, e, fn, reads=(), writes=(), signal=True):
        self._waits(e, reads, writes)
        inst = fn(self.eng[e])
        if signal:
            if self.ecnt[e] >= 30000:
                self._roll(e)
            self.ecnt[e] += 1
            inst.then_inc(self.esem[e], 1)
            self._commit((self.esem[e], self.ecnt[e]), reads, writes)
        return inst

    def dma(self, q, out, in_, reads=(), writes=(), **kw):
        pool = self.dpool[q]
        i = self.dpos[q]
        self.dpos[q] = (i + 1) % len(pool)
        slot = pool[i]
        if slot[1] >= 1800:
            slot[0] = self._newsem()
            slot[1] = 0
        s = slot[0]
        kn = self.known[q]
        if slot[1] > 0 and kn.get(id(s), 0) < slot[1] * 16:
            self.eng[q].wait_ge(s, slot[1] * 16)
            kn[id(s)] = slot[1] * 16
        self._waits(q, reads, writes)
        slot[1] += 1
        self.eng[q].dma_start(out=out, in_=in_, **kw).then_inc(s, 16)
        self._commit((s, slot[1] * 16), reads, writes)

    def collective(self, send_ap, recv_ap, send_d, recv_d):
        if not hasattr(self, "csem"):
            self.csem = self._newsem()
            self.ccnt = 0
        self._waits("pool", [send_d], [recv_d])
        self.ccnt += 1
        self.nc.gpsimd.collective_compute(
            "AllGather", ALU.bypass, replica_groups=[[0, 1, 2, 3], [4, 5, 6, 7]],
            ins=[send_ap], outs=[recv_ap]).then_inc(self.csem)
        self._commit((self.csem, self.ccnt), [send_d], [recv_d])

    def allgather16(self, send_t, recv_t, send_d, recv_d):
        sa = send_t.ap()
        ra = recv_t.ap()
        for k in range(16):
            self.collective(sa[k * 128:(k + 1) * 128, :], ra[k * 512:(k + 1) * 512, :], send_d, recv_d)

    def barrier(self):
        toks = []
        for e in ("pe", "act", "dve", "pool"):
            if self.ecnt[e] > 0:
                toks.append((self.esem[e], self.ecnt[e]))
        for q in self.dpool:
            for s_, n in self.dpool[q]:
                if n > 0:
                    toks.append((s_, n * 16))
        if hasattr(self, "csem") and self.ccnt > 0:
            toks.append((self.csem, self.ccnt))
        for e in self.eng:
            kn = self.known[e]
            for s_, v in toks:
                if e == "pe" and s_ is self.esem["pe"]:
                    continue
                if kn.get(id(s_), 0) >= v:
                    continue
                self.eng[e].wait_ge(s_, v)
                kn[id(s_)] = v

    def drain(self, bufs):
        self._waits("sp", bufs, ())


def ap3(t, off, dims):
    return bass.AP(t.tensor if hasattr(t, "tensor") else t, off, dims)


def sb_ap(buf, off, dims):
    base = buf.t[:]
    return bass.AP(base.tensor, base.offset + off, dims)


def make_consts(S):
    nc = S.nc
    ident = S.sb("ident", [128, 128], BF16)
    ones = S.sb("ones", [128, 128], BF16)
    S.op("pool", lambda g: g.memset(ident[:], 1.0), writes=[ident])
    S.op("pool", lambda g: g.affine_select(out=ident[:], in_=ident[:], pattern=[[-1, 128]],
                                           compare_op=ALU.is_equal, fill=0.0, base=0, channel_multiplier=1),
         reads=[ident], writes=[ident])
    S.op("pool", lambda g: g.memset(ones[:], 1.0), writes=[ones])
    return ident, ones


def emit_to_xt(S, src_bf, ident, ptr, xts, XTd, XT_ap_fn, t, q="sp"):
    for c in range(16):
        S.op("pe", lambda e, c=c: e.transpose(ptr[:, c * 128:(c + 1) * 128], src_bf[:, c * 128:(c + 1) * 128], ident[:]),
             reads=[src_bf, ident], writes=[ptr], signal=(c == 15))
    S.op("dve", lambda e: e.tensor_copy(out=xts[:], in_=ptr[:]), reads=[ptr], writes=[xts])
    S.dma(q, XT_ap_fn(t), xts[:].rearrange("p (c t) -> p c t", c=16), reads=[xts], writes=[XTd])


def rope_tables(S, pos_ap, ntok, inv, sgn, Cd, Sd, nparts, chunk=2048):
    P = nparts
    pi_ = S.sb("rt_pi", [P, chunk], I32)
    pf = S.sb("rt_pf", [P, chunk])
    ang = S.sb("rt_ang", [P, chunk])
    k = S.sb("rt_k", [P, chunk])
    r = S.sb("rt_r", [P, chunk])
    o = S.sb("rt_o", [P, chunk])
    for c0 in range(0, ntok, chunk):
        src = bass.AP(pos_ap.tensor, pos_ap.offset + c0, [[0, P], [1, chunk]])
        S.dma("sp", pi_[:], src, writes=[pi_])
        S.op("dve", lambda e: e.tensor_copy(out=pf[:], in_=pi_[:]), reads=[pi_], writes=[pf])
        S.op("dve", lambda e: e.tensor_scalar(out=ang[:], in0=pf[:], scalar1=inv[:, 0:1], scalar2=None, op0=ALU.mult),
             reads=[pf, inv], writes=[ang])
        for which, dst in ((0, Sd), (1, Cd)):
            S.op("dve", lambda e: e.tensor_scalar(out=k[:], in0=ang[:], scalar1=1.0 / TWO_PI, scalar2=MAGIC,
                                                  op0=ALU.mult, op1=ALU.add), reads=[ang], writes=[k])
            S.op("dve", lambda e: e.tensor_scalar(out=k[:], in0=k[:], scalar1=-MAGIC, scalar2=None, op0=ALU.add),
                 reads=[k], writes=[k])
            S.op("dve", lambda e: e.scalar_tensor_tensor(out=r[:], in0=k[:], scalar=-CW1, in1=ang[:], op0=ALU.mult, op1=ALU.add),
                 reads=[k, ang], writes=[r])
            S.op("dve", lambda e: e.scalar_tensor_tensor(out=r[:], in0=k[:], scalar=-CW2, in1=r[:], op0=ALU.mult, op1=ALU.add),
                 reads=[k, r], writes=[r])
            S.op("dve", lambda e: e.tensor_scalar(out=r[:], in0=r[:], scalar1=3.1415925, scalar2=-3.1415925, op0=ALU.min, op1=ALU.max),
                 reads=[r], writes=[r])
            if which == 1:
                S.op("dve", lambda e: e.scalar_tensor_tensor(out=r[:], in0=r[:], scalar=-1.0, in1=r[:], op0=ALU.mult, op1=ALU.max),
                     reads=[r], writes=[r])
                S.op("dve", lambda e: e.tensor_scalar(out=r[:], in0=r[:], scalar1=-1.0, scalar2=math.pi / 2, op0=ALU.mult, op1=ALU.add),
                     reads=[r], writes=[r])
            S.op("act", lambda e: e.activation(out=o[:], in_=r[:], func=AF.Sin), reads=[r], writes=[o])
            if which == 0:
                S.op("dve", lambda e: e.tensor_scalar(out=o[:], in0=o[:], scalar1=sgn[:, 0:1], scalar2=None, op0=ALU.mult),
                     reads=[o, sgn], writes=[o])
            S.dma("sp", dst.t[:, c0:c0 + chunk], o[:], reads=[o], writes=[dst])


def dil_dims(g, m):
    if g == 0:
        return 512 * m, [[1, 512]]
    if g == 1:
        return 512 * m, [[1, 4], [4, 128]]
    return 4 * m, [[1, 4], [16, 128]]


def dil_tile_dims(g, tt):
    if g == 0:
        return 128 * tt, [[1, 128]]
    if g == 1:
        return 512 * (tt // 4) + (tt % 4), [[4, 128]]
    return tt, [[16, 128]]


def emit_p0(S, ident, x, ntok, XT, XTd):
    with ExitStack() as es:
        S.es = es
        xf = [S.sb("xf%d" % i, [128, D]) for i in range(2)]
        xb = [S.sb("xb%d" % i, [128, D], BF16) for i in range(2)]
        ptr = [S.ps("ptr%d" % i, [128, D], BF16) for i in range(2)]
        xts = [S.sb("xts%d" % i, [128, D], BF16) for i in range(2)]
        XTv = XT.rearrange("(c p) t -> p c t", p=128)
        for t in range(ntok // 128):
            i = t % 2
            S.dma("sp", xf[i][:], x[t * 128:(t + 1) * 128, :], writes=[xf[i]])
            S.op("pool", lambda e, i=i: e.tensor_copy(out=xb[i][:], in_=xf[i][:]), reads=[xf[i]], writes=[xb[i]])
            emit_to_xt(S, xb[i], ident, ptr[i], xts[i], XTd, lambda t: XTv[:, :, t * 128:(t + 1) * 128], t)
    S.barrier()


MLA_PERM = [(c % 4) * 4 + c // 4 for c in range(16)]


def emit_epilogue(S, ident, gt_load, resid, resid_d, wout, lng, lnb, resid_out, resid_out_d, XT_out, XT_out_d, perm=None):
    perm = perm or list(range(16))
    nc = S.nc
    wo = S.sb("wo", [128, 16, D], BF16)
    wst = [S.sb("wst%d" % i, [128, 4, 512]) for i in range(2)]
    woutv = wout.rearrange("(c p) n -> p c n", p=128)
    k = 0
    for c4 in range(4):
        for n in range(4):
            i = k % 2
            k += 1
            S.dma("sp", wst[i][:], woutv[:, c4 * 4:(c4 + 1) * 4, n * 512:(n + 1) * 512], writes=[wst[i]])
            S.op("pool", lambda e, i=i, c4=c4, n=n: e.tensor_copy(out=wo[:, c4 * 4:(c4 + 1) * 4, n * 512:(n + 1) * 512], in_=wst[i][:]),
                 reads=[wst[i]], writes=[wo])
    g_b = S.sb("lng_b", [128, D])
    b_b = S.sb("lnb_b", [128, D])
    S.dma("sp", g_b[:], bass.AP(lng.tensor, lng.offset, [[0, 128], [1, D]]), writes=[g_b])
    S.dma("sp", b_b[:], bass.AP(lnb.tensor, lnb.offset, [[0, 128], [1, D]]), writes=[b_b])
    gt = [S.sb("gt%d" % i, [128, 16, 128], BF16) for i in range(2)]
    rs = [S.sb("rs%d" % i, [128, D]) for i in range(2)]
    v = [S.sb("v%d" % i, [128, D]) for i in range(2)]
    xo = [S.sb("xo%d" % i, [128, D]) for i in range(2)]
    xb = [S.sb("exb%d" % i, [128, D], BF16) for i in range(2)]
    st6_ = [S.sb("st6_%d" % i, [128, 4, 6]) for i in range(2)]
    mv_ = [S.sb("mv%d" % i, [128, 2]) for i in range(2)]
    rstd_ = [S.sb("rstd%d" % i, [128, 1]) for i in range(2)]
    py = [S.ps("py%d" % n, [128, 512]) for n in range(4)]
    ptr = S.ps("eptr", [128, D], BF16)
    xts = [S.sb("exts%d" % i, [128, D], BF16) for i in range(2)]
    XTv = XT_out.rearrange("(c p) t -> p c t", p=128) if XT_out is not None else None
    def _loads(t):
        gt_load(t, gt[t % 2])
        S.dma("sp", rs[t % 2][:], resid[t * 128:(t + 1) * 128, :], reads=[resid_d], writes=[rs[t % 2]])

    _loads(0)
    for t in range(NTOK // 128):
        i = t % 2
        if t + 1 < NTOK // 128:
            _loads(t + 1)
        for n in range(4):
            for c in range(16):
                S.op("pe", lambda e, c=c, n=n, i=i: e.matmul(py[n][:], gt[i][:, c, :], wo[:, perm[c], n * 512:(n + 1) * 512],
                                                             start=(c == 0), stop=(c == 15)),
                     reads=[gt[i], wo], writes=[py[n]], signal=(c == 15))
            S.op("dve", lambda e, n=n, i=i: e.scalar_tensor_tensor(out=v[i][:, n * 512:(n + 1) * 512], in0=rs[i][:, n * 512:(n + 1) * 512],
                                                                    scalar=ALPHA, in1=py[n][:], op0=ALU.mult, op1=ALU.add),
                 reads=[rs[i], py[n]], writes=[v[i]])
        st6, mv, rstd = st6_[i], mv_[i], rstd_[i]
        for n in range(4):
            S.op("dve", lambda e, n=n, i=i: e.bn_stats(out=st6[:, n, :], in_=v[i][:, n * 512:(n + 1) * 512]),
                 reads=[v[i]], writes=[st6])
        S.op("dve", lambda e: e.bn_aggr(out=mv[:], in_=st6[:]), reads=[st6], writes=[mv])
        S.op("dve", lambda e: e.tensor_scalar(out=rstd[:], in0=mv[:, 1:2], scalar1=LN_EPS, scalar2=None, op0=ALU.add),
             reads=[mv], writes=[rstd])
        S.op("act", lambda e: e.activation(out=rstd[:], in_=rstd[:], func=AF.Sqrt), reads=[rstd], writes=[rstd])
        S.op("dve", lambda e: e.reciprocal(out=rstd[:], in_=rstd[:]), reads=[rstd], writes=[rstd])
        S.op("dve", lambda e, i=i: e.scalar_tensor_tensor(out=v[i][:], in0=v[i][:], scalar=mv[:, 0:1], in1=g_b[:],
                                                          op0=ALU.subtract, op1=ALU.mult), reads=[v[i], mv, g_b], writes=[v[i]])
        S.op("dve", lambda e, i=i: e.scalar_tensor_tensor(out=xo[i][:], in0=v[i][:], scalar=rstd[:, 0:1], in1=b_b[:],
                                                          op0=ALU.mult, op1=ALU.add), reads=[v[i], rstd, b_b], writes=[xo[i]])
        S.dma("act", resid_out[t * 128:(t + 1) * 128, :], xo[i][:], reads=[xo[i]], writes=[resid_out_d])
        if XT_out is not None:
            S.op("pool", lambda e, i=i: e.tensor_copy(out=xb[i][:], in_=xo[i][:]), reads=[xo[i]], writes=[xb[i]])
            emit_to_xt(S, xb[i], ident, ptr, xts[i], XT_out_d, lambda t: XTv[:, :, t * 128:(t + 1) * 128], t, q="act")


def emit_dil(S, ident, ones, io):
    XTo_in, XTo_in_d = io["XT_own"], io["XT_own_d"]
    resid, resid_d = io["resid"], io["resid_d"]
    w_in, wout, lng, lnb, masks_in = io["w_in"], io["wout"], io["lng"], io["lnb"], io["masks"]
    ro, rod, XTo, xtd = io["ro"], io["rod"], io["XTo"], io["xtd"]
    QT, KT, VV, ZT, GT, Ctab, Stab = (io[k] for k in ("QT", "KT", "VV", "ZT", "GT", "Ctab", "Stab"))
    QTd, KTd, VVd, ZTd, GTd, Cd, Sd = (io[k + "_d"] for k in ("QT", "KT", "VV", "ZT", "GT", "Ctab", "Stab"))
    if True:
        with ExitStack() as es:
            S.es = es
            xT = S.sb("xT", [128, 16, 2048], BF16)
            Cb = S.sb("Cb", [128, 2048])
            Sb = S.sb("Sb", [128, 2048])
            wf = [S.sb("wf%d" % i, [128, 16, 256]) for i in range(3)]
            wb = [S.sb("wb%d" % i, [128, 16, 256], BF16) for i in range(2)]
            qst = [S.sb("qst%d" % i, [128, 2048], BF16) for i in range(2)]
            vst = [S.sb("vst%d" % i, [128, 256], BF16) for i in range(2)]
            t1 = [S.sb("t1_%d" % i, [128, 512]) for i in range(2)]
            t2 = [S.sb("t2_%d" % i, [128, 512]) for i in range(2)]
            pp = [S.ps("pp%d" % i, [128, 512]) for i in range(4)]
            pv = [S.ps("pv%d" % i, [128, 256]) for i in range(2)]
            w_inv = w_in.rearrange("(c p) n -> p c n", p=128)
            ppk = 0
            qk = 0
            vk = 0
            alltiles = []
            for blk in range(3):
                for g in range(3):
                    for kind in range(3):
                        if blk == 0 and kind == 0:
                            continue
                        for h0 in range(0, 16, 2):
                            alltiles.append((blk, kind, g, h0))
                if blk > 0:
                    for h0 in range(0, 16, 2):
                        alltiles.append((blk, 3, 0, h0))

            def w_load(n):
                (_, kind, g, h0) = alltiles[n]
                col0 = 18432 + h0 * 128 if kind == 3 else ((g * 3 + kind) * 16 + h0) * 128
                S.dma("sp", wf[n % 3][:], w_inv[:, :, col0:col0 + 256], writes=[wf[n % 3]])

            def w_cast(n):
                S.op("act", lambda e: e.activation(out=wb[n % 2][:], in_=wf[n % 3][:], func=AF.Copy),
                     reads=[wf[n % 3]], writes=[wb[n % 2]])

            w_load(0)
            w_load(1)
            w_cast(0)
            cur_blk = -1
            for n, (blk, kind, g, h0) in enumerate(alltiles):
                if blk != cur_blk:
                    cur_blk = blk
                    if blk == 0:
                        io["halo_load"](xT)
                    else:
                        S.dma("sp", xT[:], XTo_in[:, (blk - 1) * 2048:blk * 2048].rearrange("(c p) t -> p c t", p=128),
                              reads=[XTo_in_d], writes=[xT])
                    S.dma("sp", Cb[:], Ctab[:, blk * 2048:(blk + 1) * 2048], reads=[Cd], writes=[Cb])
                    S.dma("sp", Sb[:], Stab[:, blk * 2048:(blk + 1) * 2048], reads=[Sd], writes=[Sb])
                if n + 2 < len(alltiles):
                    w_load(n + 2)
                if n + 1 < len(alltiles):
                    w_cast(n + 1)
                wi = n % 2
                if True:
                    if blk == 0:
                        ms = [3] if g < 2 else [0, 1, 2, 3]
                        tts = [15] if g == 0 else ([12, 13, 14, 15] if g == 1 else list(range(16)))
                    else:
                        ms = [0, 1, 2, 3]
                        tts = list(range(16))
                    if kind in (0, 1, 3):
                        for hh in range(2):
                            h = h0 + hh
                            qi = qk % 2
                            qk += 1
                            for m in ms:
                                p = pp[ppk % 4]
                                ppk += 1
                                for c in range(16):
                                    rhs = xT[:, c, m * 512:(m + 1) * 512]
                                    S.op("pe", lambda e, p=p, wi=wi, c=c, hh=hh, rhs=rhs: e.matmul(
                                        p[:], wb[wi][:, c, hh * 128:(hh + 1) * 128], rhs, start=(c == 0), stop=(c == 15)),
                                        reads=[wb[wi], xT], writes=[p], signal=(c == 15))
                                if kind == 3:
                                    dst = qst[qi][:, m * 512:(m + 1) * 512]
                                    S.op("act", lambda e, p=p, dst=dst: e.activation(out=dst, in_=p[:], func=AF.Copy),
                                         reads=[p], writes=[qst[qi]])
                                else:
                                    ti = ppk % 2
                                    ms_ = slice(m * 512, (m + 1) * 512)
                                    S.op("dve", lambda e, p=p, ti=ti, ms_=ms_: e.tensor_tensor(out=t1[ti][:], in0=p[:], in1=Cb[:, ms_], op=ALU.mult),
                                         reads=[p, Cb], writes=[t1[ti]])
                                    S.op("dve", lambda e, p=p, ti=ti, ms_=ms_: e.tensor_tensor(out=t2[ti][0:64, :], in0=p[64:128, :], in1=Sb[0:64, ms_], op=ALU.mult),
                                         reads=[p, Sb], writes=[t2[ti]])
                                    S.op("dve", lambda e, p=p, ti=ti, ms_=ms_: e.tensor_tensor(out=t2[ti][64:128, :], in0=p[0:64, :], in1=Sb[64:128, ms_], op=ALU.mult),
                                         reads=[p, Sb], writes=[t2[ti]])
                                    if g == 0:
                                        dst = qst[qi][:, ms_]
                                        a0, a1 = t1[ti][:], t2[ti][:]
                                    elif g == 1:
                                        dst = sb_ap(qst[qi], 512 * m, [[2048, 128], [1, 128], [128, 4]])
                                        a0 = sb_ap(t1[ti], 0, [[512, 128], [4, 128], [1, 4]])
                                        a1 = sb_ap(t2[ti], 0, [[512, 128], [4, 128], [1, 4]])
                                    else:
                                        dst = sb_ap(qst[qi], 32 * m, [[2048, 128], [1, 32], [128, 16]])
                                        a0 = sb_ap(t1[ti], 0, [[512, 128], [16, 32], [1, 16]])
                                        a1 = sb_ap(t2[ti], 0, [[512, 128], [16, 32], [1, 16]])
                                    S.op("pool", lambda e, dst=dst, a0=a0, a1=a1: e.tensor_tensor(out=dst, in0=a0, in1=a1, op=ALU.add),
                                         reads=[t1[ti], t2[ti]], writes=[qst[qi]])
                            c_lo, c_hi = ms[0] * 512, (ms[-1] + 1) * 512
                            if kind == 0:
                                S.dma("pool", QT[g * 16 + h, :, (blk - 1) * 2048 + c_lo:(blk - 1) * 2048 + c_hi], qst[qi][:, c_lo:c_hi],
                                      reads=[qst[qi]], writes=[QTd])
                            elif kind == 1:
                                S.dma("pool", KT[g * 16 + h, :, blk * 2048 + c_lo:blk * 2048 + c_hi], qst[qi][:, c_lo:c_hi],
                                      reads=[qst[qi]], writes=[KTd])
                            else:
                                S.dma("pool", ZT[h * 128:(h + 1) * 128, (blk - 1) * 2048:blk * 2048], qst[qi][:],
                                      reads=[qst[qi]], writes=[ZTd])
                    else:
                        for tt in tts:
                            off, dims = dil_tile_dims(g, tt)
                            p = pv[vk % 2]
                            vi = vk % 2
                            vk += 1
                            for c in range(16):
                                lhsT = sb_ap(xT, c * 2048 + off, [[16 * 2048, 128]] + dims)
                                S.op("pe", lambda e, p=p, wi=wi, c=c, lhsT=lhsT: e.matmul(
                                    p[:], lhsT, wb[wi][:, c, :], start=(c == 0), stop=(c == 15)),
                                    reads=[wb[wi], xT], writes=[p], signal=(c == 15))
                            S.op("act", lambda e, p=p, vi=vi: e.activation(out=vst[vi][:], in_=p[:], func=AF.Copy),
                                 reads=[p], writes=[vst[vi]])
                            S.dma("act", VV[g, blk * 2048 + tt * 128:blk * 2048 + (tt + 1) * 128, h0 * 128:h0 * 128 + 256], vst[vi][:],
                                  reads=[vst[vi]], writes=[VVd])
        S.barrier()
        with ExitStack() as es:
            S.es = es
            msk = S.sb("msk", [128, 512], BF16)
            S.dma("sp", msk[:], masks_in, writes=[msk])
            accO = S.sb("accO", [128, NTOK])
            accL = S.sb("accL", [128, NTOK])
            qt = [S.sb("qt%d" % i, [128, 4096], BF16) for i in range(2)]
            kt = [S.sb("kt%d" % i, [128, 6144], BF16) for i in range(2)]
            vt = [S.sb("vt%d" % i, [128, 48, 128], BF16) for i in range(2)]
            zt = S.sb("zt", [128, NTOK], BF16)
            gst = S.sb("gst", [128, NTOK], BF16)
            pt = [S.sb("pt%d" % i, [128, 256], BF16) for i in range(4)]
            rl = [S.sb("rl%d" % i, [128, 512]) for i in range(2)]
            ot = [S.sb("ot%d" % i, [128, 512]) for i in range(2)]
            sz = [S.sb("sz%d" % i, [128, 512]) for i in range(2)]
            ps_s = [S.ps("ps_s%d" % i, [128, 256]) for i in range(4)]
            po = [S.ps("po%d" % i, [128, 512]) for i in range(2)]
            pl = [S.ps("pl%d" % i, [128, 512]) for i in range(2)]
            scale = 128.0 ** -0.5
            hg = [(h, g) for h in range(16) for g in range(3)]

            def load_hg(idx):
                h, g = hg[idx]
                bi = idx % 2
                S.dma("sp", qt[bi][:], QT[g * 16 + h], reads=[QTd], writes=[qt[bi]])
                S.dma("sp", kt[bi][:], KT[g * 16 + h], reads=[KTd], writes=[kt[bi]])
                S.dma("sp", vt[bi][:], VV[g, :, h * 128:(h + 1) * 128].rearrange("(t p) d -> p t d", p=128),
                      reads=[VVd], writes=[vt[bi]])

            load_hg(0)
            sk = 0
            ok = 0
            for h in range(16):
                S.dma("sp", zt[:], ZT[h * 128:(h + 1) * 128, :], reads=[ZTd], writes=[zt])
                for g in range(3):
                    idx = h * 3 + g
                    bi = idx % 2
                    if idx + 1 < len(hg):
                        load_hg(idx + 1)
                    Pg = 128 * DIL[g]
                    qbs = []
                    for sbk in range(2):
                        for m in range(4):
                            for qb in range(4):
                                col0 = sbk * 2048 + m * 512 + qb * 128
                                qbs.append((sbk, m, qb, col0, 2048 + col0, 2048 + col0 - Pg))

                    def emitS(i):
                        (sbk, m, qb, col0, kc, prev) = qbs[i]
                        si = (sk + i) % 4
                        S.op("pe", lambda e: e.matmul(ps_s[si][:, 0:128], kt[bi][:, prev:prev + 128], qt[bi][:, col0:col0 + 128],
                                                      start=True, stop=True),
                             reads=[kt[bi], qt[bi]], writes=[ps_s[si]], signal=False)
                        S.op("pe", lambda e: e.matmul(ps_s[si][:, 128:256], kt[bi][:, kc:kc + 128], qt[bi][:, col0:col0 + 128],
                                                      start=True, stop=True),
                             reads=[kt[bi], qt[bi]], writes=[ps_s[si]])

                    emitS(0)
                    for i, (sbk, m, qb, col0, kc, prev) in enumerate(qbs):
                        if i + 1 < len(qbs):
                            emitS(i + 1)
                        si = (sk + i) % 4
                        if qb == 0:
                            oi = ok % 2
                            ok += 1
                        S.op("act", lambda e, si=si: e.activation(out=pt[si][:], in_=ps_s[si][:], func=AF.Exp, scale=scale),
                             reads=[ps_s[si]], writes=[pt[si]])
                        moff = 256 if prev < 2048 else 0
                        S.op("dve", lambda e, si=si, moff=moff: e.tensor_tensor(out=pt[si][:], in0=pt[si][:], in1=msk[:, moff:moff + 256], op=ALU.mult),
                             reads=[pt[si], msk], writes=[pt[si]])
                        oc = slice(qb * 128, (qb + 1) * 128)
                        S.op("pe", lambda e, oi=oi, si=si, prev=prev, oc=oc: e.matmul(
                            po[oi][:, oc], vt[bi][:, prev // 128, :], pt[si][:, 0:128], start=True, stop=False),
                            reads=[vt[bi], pt[si]], writes=[po[oi]], signal=False)
                        S.op("pe", lambda e, oi=oi, si=si, kc=kc, oc=oc: e.matmul(
                            po[oi][:, oc], vt[bi][:, kc // 128, :], pt[si][:, 128:256], start=False, stop=True),
                            reads=[vt[bi], pt[si]], writes=[po[oi]], signal=False)
                        S.op("pe", lambda e, oi=oi, si=si, oc=oc: e.matmul(
                            pl[oi][:, oc], ones[:], pt[si][:, 0:128], start=True, stop=False),
                            reads=[ones, pt[si]], writes=[pl[oi]], signal=False)
                        S.op("pe", lambda e, oi=oi, si=si, oc=oc: e.matmul(
                            pl[oi][:, oc], ones[:], pt[si][:, 128:256], start=False, stop=True),
                            reads=[ones, pt[si], vt[bi]], writes=[pl[oi], po[oi]])
                        if qb == 3:
                            off, dims = dil_dims(g, m)
                            dO = sb_ap(accO, sbk * 2048 + off, [[NTOK, 128]] + dims)
                            dL = sb_ap(accL, sbk * 2048 + off, [[NTOK, 128]] + dims)
                            if g == 0:
                                S.op("dve", lambda e, oi=oi, dO=dO: e.tensor_copy(out=dO, in_=po[oi][:]), reads=[po[oi]], writes=[accO])
                                S.op("act", lambda e, oi=oi, dL=dL: e.activation(out=dL, in_=pl[oi][:], func=AF.Copy), reads=[pl[oi]], writes=[accL])
                            else:
                                S.op("dve", lambda e, oi=oi, dO=dO: e.tensor_tensor(out=dO, in0=po[oi][:], in1=dO, op=ALU.add),
                                     reads=[po[oi], accO], writes=[accO])
                                S.op("dve", lambda e, oi=oi, dL=dL: e.tensor_tensor(out=dL, in0=pl[oi][:], in1=dL, op=ALU.add),
                                     reads=[pl[oi], accL], writes=[accL])
                    sk += len(qbs)
                for c8 in range(8):
                    i = c8 % 2
                    cs = slice(c8 * 512, (c8 + 1) * 512)
                    S.op("dve", lambda e, i=i, cs=cs: e.reciprocal(out=rl[i][:], in_=accL[:, cs]), reads=[accL], writes=[rl[i]])
                    S.op("pool", lambda e, i=i, cs=cs: e.tensor_tensor(out=ot[i][:], in0=accO[:, cs], in1=rl[i][:], op=ALU.mult),
                         reads=[accO, rl[i]], writes=[ot[i]])
                    S.op("act", lambda e, i=i, cs=cs: e.activation(out=sz[i][:], in_=zt[:, cs], func=AF.Silu), reads=[zt], writes=[sz[i]])
                    S.op("pool", lambda e, i=i, cs=cs: e.tensor_tensor(out=gst[:, cs], in0=ot[i][:], in1=sz[i][:], op=ALU.mult),
                         reads=[ot[i], sz[i]], writes=[gst])
                S.dma("act", GT[h * 128:(h + 1) * 128, :], gst[:], reads=[gst], writes=[GTd])
        S.barrier()
        with ExitStack() as es:
            S.es = es
            GTv = GT.rearrange("(c p) t -> p c t", p=128)
            emit_epilogue(S, ident, lambda t, dst: S.dma("sp", dst[:], GTv[:, :, t * 128:(t + 1) * 128], reads=[GTd], writes=[dst]),
                          resid, resid_d, wout, lng, lnb, ro, rod, XTo, xtd)
        S.barrier()


def emit_mla_a(S, ident, ones, io):
    w_in_c, w_uq_c, w_kk_c, w_kv_c, qg_in, kg_in, cmask_in = (io[k] for k in ("w_in_c", "w_uq_c", "w_kk_c", "w_kv_c", "qg", "kg", "cmask"))
    GTP, GTPd = io["GTP"], io["GTP_d"]
    QN, QP, KN, KP, VV, SZT, Ctab, Stab = (io[k] for k in ("QN", "QP", "KN", "KP", "VVm", "SZT", "Ctab64", "Stab64"))
    QNd, QPd, KNd, KPd, VVd, SZd, Cd, Sd = (io[k + "_d"] for k in ("QN", "QP", "KN", "KP", "VVm", "SZT", "Ctab64", "Stab64"))
    if True:
        with ExitStack() as es:
            S.es = es
            wi = S.sb("wi", [128, 16, 1600], BF16)
            wuq = S.sb("wuq", [128, 4, 768], BF16)
            wkk = S.sb("wkk", [128, 4, 512], BF16)
            wkv = S.sb("wkv", [128, 4, 512], BF16)
            qg = S.sb("qg", [128, 4])
            kg = S.sb("kg", [128, 4])
            wst = [S.sb("wst%d" % i, [128, 4, 800]) for i in range(2)]
            S.dma("sp", qg[:], qg_in, writes=[qg])
            S.dma("sp", kg[:], kg_in, writes=[kg])
            w_inv = w_in_c.rearrange("(c p) n -> p c n", p=128)
            k = 0
            for c4 in range(4):
                for half in range(2):
                    i = k % 2
                    k += 1
                    S.dma("sp", wst[i][:], w_inv[:, c4 * 4:(c4 + 1) * 4, half * 800:(half + 1) * 800], writes=[wst[i]])
                    S.op("pool", lambda e, i=i, c4=c4, half=half: e.tensor_copy(
                        out=wi[:, c4 * 4:(c4 + 1) * 4, half * 800:(half + 1) * 800], in_=wst[i][:]), reads=[wst[i]], writes=[wi])
            for (src, dstw, gain, ncol) in ((w_uq_c, wuq, qg, 768), (w_kk_c, wkk, kg, 512), (w_kv_c, wkv, kg, 512)):
                i = k % 2
                k += 1
                S.dma("sp", wst[i][:, :, 0:ncol], src.rearrange("(c p) n -> p c n", p=128), writes=[wst[i]])
                for c in range(4):
                    S.op("dve", lambda e, i=i, c=c, dstw=dstw, gain=gain, ncol=ncol: e.tensor_scalar(
                        out=dstw[:, c, :], in0=wst[i][:, c, 0:ncol], scalar1=gain[:, c:c + 1], scalar2=None, op0=ALU.mult),
                        reads=[wst[i], gain], writes=[dstw])
            xT = [S.sb("xT%d" % i, [128, 16, 512], BF16) for i in range(2)]
            Cb = [S.sb("Cb%d" % i, [64, 512]) for i in range(2)]
            Sb = [S.sb("Sb%d" % i, [64, 512]) for i in range(2)]
            cqb = S.sb("cqb", [128, 4, 512], BF16)
            sq = S.sb("sq", [128, 4, 512], BF16)
            ckvb = S.sb("ckvb", [128, 4, 512], BF16)
            sq2 = S.sb("sq2", [128, 4, 512], BF16)
            rq = S.sb("rq", [128, 512])
            rk = S.sb("rk", [128, 512])
            rtok = S.sb("rtok", [128, 4])
            st = [S.sb("st%d" % i, [128, 512], BF16) for i in range(3)]
            t1 = S.sb("t1", [64, 512])
            ta = S.sb("ta", [64, 512])
            tb = S.sb("tb", [64, 512])
            pa = [S.ps("pa%d" % i, [128, 512]) for i in range(3)]
            pb = S.ps("pb", [128, 512])
            pc = [S.ps("pc%d" % i, [64, 512]) for i in range(2)]
            pd = S.ps("pd", [128, 4])
            cnt = {"pa": 0, "st": 0, "pc": 0}

            def nxt(key, n):
                cnt[key] += 1
                return (cnt[key] - 1) % n

            def big_mm(p, lhs_fn, xi):
                for c in range(16):
                    S.op("pe", lambda e, c=c: e.matmul(p[:], lhs_fn(c), xT[xi][:, c, :], start=(c == 0), stop=(c == 15)),
                         reads=[wi, xT[xi]], writes=[p], signal=(c == 15))

            def rstd_from(ps_buf, dst, width):
                S.op("dve", lambda e: e.tensor_scalar(out=dst[:, 0:width], in0=ps_buf[:, 0:width], scalar1=1.0 / 512, scalar2=RMS_EPS,
                                                      op0=ALU.mult, op1=ALU.add), reads=[ps_buf], writes=[dst])
                S.op("act", lambda e: e.activation(out=dst[:, 0:width], in_=dst[:, 0:width], func=AF.Sqrt), reads=[dst], writes=[dst])
                S.op("dve", lambda e: e.reciprocal(out=dst[:, 0:width], in_=dst[:, 0:width]), reads=[dst], writes=[dst])

            def rope64(srcbuf, xi, dst_ap, dstd, post=None):
                si = nxt("st", 3)
                S.op("dve", lambda e: e.tensor_tensor(out=ta[:], in0=srcbuf[0:64, :], in1=Cb[xi][:], op=ALU.mult),
                     reads=[srcbuf, Cb[xi]], writes=[ta])
                S.op("dve", lambda e: e.tensor_tensor(out=tb[0:32, :], in0=srcbuf[32:64, :], in1=Sb[xi][0:32, :], op=ALU.mult),
                     reads=[srcbuf, Sb[xi]], writes=[tb])
                S.op("dve", lambda e: e.tensor_tensor(out=tb[32:64, :], in0=srcbuf[0:32, :], in1=Sb[xi][32:64, :], op=ALU.mult),
                     reads=[srcbuf, Sb[xi]], writes=[tb])
                if post is None:
                    S.op("pool", lambda e: e.tensor_tensor(out=st[si][0:64, :], in0=ta[:], in1=tb[:], op=ALU.add),
                         reads=[ta, tb], writes=[st[si]])
                else:
                    S.op("pool", lambda e: e.tensor_tensor(out=t1[:], in0=ta[:], in1=tb[:], op=ALU.add),
                         reads=[ta, tb], writes=[t1])
                    S.op("dve", lambda e: e.tensor_tensor(out=st[si][0:64, :], in0=t1[:], in1=post[0:64, :], op=ALU.mult),
                         reads=[t1, post], writes=[st[si]])
                S.dma("act", dst_ap, st[si][0:64, :], reads=[st[si]], writes=[dstd])

            def load_blk(b):
                xi = b % 2
                io["xt_load"](b, xT[xi])
                S.dma("sp", Cb[xi][:], Ctab[:, b * 512:(b + 1) * 512], reads=[Cd], writes=[Cb[xi]])
                S.dma("sp", Sb[xi][:], Stab[:, b * 512:(b + 1) * 512], reads=[Sd], writes=[Sb[xi]])

            load_blk(0)
            for b in range(SEQ // 512):
                xi = b % 2
                if b + 1 < SEQ // 512:
                    load_blk(b + 1)
                bs = slice(b * 512, (b + 1) * 512)
                for (coff, cb_, sq_, rr) in ((0, cqb, sq, rq), (512, ckvb, sq2, rk)):
                    for f in range(4):
                        p = pa[nxt("pa", 3)]
                        big_mm(p, lambda c, f=f, coff=coff: wi[:, c, coff + f * 128:coff + (f + 1) * 128], xi)
                        S.op("act", lambda e, p=p, f=f, cb_=cb_: e.activation(out=cb_[:, f, :], in_=p[:], func=AF.Copy),
                             reads=[p], writes=[cb_])
                        S.op("act", lambda e, p=p, f=f, sq_=sq_: e.activation(out=sq_[:, f, :], in_=p[:], func=AF.Square),
                             reads=[p], writes=[sq_])
                    for f in range(4):
                        S.op("pe", lambda e, f=f, sq_=sq_: e.matmul(pb[:], ones[:], sq_[:, f, :], start=(f == 0), stop=(f == 3)),
                             reads=[ones, sq_], writes=[pb], signal=(f == 3))
                    rstd_from(pb, rr, 512)
                for tt in range(4):
                    for f in range(4):
                        S.op("pe", lambda e, f=f, tt=tt: e.matmul(pd[:, tt:tt + 1], sq2[:, f, tt * 128:(tt + 1) * 128], ones[:, 0:1],
                                                                  start=(f == 0), stop=(f == 3)),
                             reads=[ones, sq2], writes=[pd], signal=(f == 3 and tt == 3))
                rstd_from(pd, rtok, 4)
                for h in range(4):
                    p = pa[nxt("pa", 3)]
                    for f in range(4):
                        S.op("pe", lambda e, p=p, f=f, h=h: e.matmul(p[:], wuq[:, f, h * 192:h * 192 + 128], cqb[:, f, :],
                                                                     start=(f == 0), stop=(f == 3)),
                             reads=[wuq, cqb], writes=[p], signal=(f == 3))
                    si = nxt("st", 3)
                    S.op("dve", lambda e, p=p, si=si: e.tensor_tensor(out=st[si][:], in0=p[:], in1=rq[:], op=ALU.mult),
                         reads=[p, rq], writes=[st[si]])
                    S.dma("act", QN[h, :, bs], st[si][:], reads=[st[si]], writes=[QNd])
                    p2 = pc[nxt("pc", 2)]
                    for f in range(4):
                        S.op("pe", lambda e, p2=p2, f=f, h=h: e.matmul(p2[:], wuq[:, f, h * 192 + 128:h * 192 + 192], cqb[:, f, :],
                                                                       start=(f == 0), stop=(f == 3)),
                             reads=[wuq, cqb], writes=[p2], signal=(f == 3))
                    rope64(p2, xi, QP[h, :, bs], QPd, post=rq)
                p2 = pc[nxt("pc", 2)]
                for c in range(16):
                    S.op("pe", lambda e, p2=p2, c=c: e.matmul(p2[:], wi[:, c, 1024:1088], xT[xi][:, c, :], start=(c == 0), stop=(c == 15)),
                         reads=[wi, xT[xi]], writes=[p2], signal=(c == 15))
                rope64(p2, xi, KP[:, bs], KPd)
                for h in range(4):
                    p = pa[nxt("pa", 3)]
                    for f in range(4):
                        S.op("pe", lambda e, p=p, f=f, h=h: e.matmul(p[:], wkk[:, f, h * 128:(h + 1) * 128], ckvb[:, f, :],
                                                                     start=(f == 0), stop=(f == 3)),
                             reads=[wkk, ckvb], writes=[p], signal=(f == 3))
                    si = nxt("st", 3)
                    S.op("dve", lambda e, p=p, si=si: e.tensor_tensor(out=st[si][:], in0=p[:], in1=rk[:], op=ALU.mult),
                         reads=[p, rk], writes=[st[si]])
                    S.dma("act", KN[h, :, bs], st[si][:], reads=[st[si]], writes=[KNd])
                for tt in range(4):
                    p = pa[nxt("pa", 3)]
                    for f in range(4):
                        S.op("pe", lambda e, p=p, f=f, tt=tt: e.matmul(p[:], ckvb[:, f, tt * 128:(tt + 1) * 128], wkv[:, f, :],
                                                                       start=(f == 0), stop=(f == 3)),
                             reads=[wkv, ckvb], writes=[p], signal=(f == 3))
                    si = nxt("st", 3)
                    S.op("dve", lambda e, p=p, si=si, tt=tt: e.tensor_scalar(out=st[si][:], in0=p[:], scalar1=rtok[:, tt:tt + 1], scalar2=None,
                                                                             op0=ALU.mult), reads=[p, rtok], writes=[st[si]])
                    S.dma("act", VV[b * 512 + tt * 128:b * 512 + (tt + 1) * 128, :], st[si][:], reads=[st[si]], writes=[VVd])
                for f in range(4):
                    p = pa[nxt("pa", 3)]
                    big_mm(p, lambda c, f=f: wi[:, c, 1088 + f * 128:1088 + (f + 1) * 128], xi)
                    si = nxt("st", 3)
                    S.op("act", lambda e, p=p, si=si: e.activation(out=st[si][:], in_=p[:], func=AF.Silu), reads=[p], writes=[st[si]])
                    S.dma("act", SZT[f * 128:(f + 1) * 128, bs], st[si][:], reads=[st[si]], writes=[SZd])
        S.barrier()
        with ExitStack() as es:
            S.es = es
            cm = S.sb("cm", [128, 2048], BF16)
            S.dma("sp", cm[:], cmask_in, writes=[cm])
            kn = S.sb("kn", [128, SEQ], BF16)
            kp = S.sb("kp", [64, SEQ], BF16)
            vt = S.sb("vt", [128, 128, 128], BF16)
            S.dma("sp", kp[:], KP, reads=[KPd], writes=[kp])
            qn = [S.sb("qn%d" % i, [128, 512], BF16) for i in range(2)]
            qp = [S.sb("qp%d" % i, [64, 512], BF16) for i in range(2)]
            szb = [S.sb("szb%d" % i, [128, 512], BF16) for i in range(2)]
            pt = [S.sb("pt%d" % i, [128, 512], BF16) for i in range(3)]
            rl = [S.sb("rl%d" % i, [128, 512]) for i in range(2)]
            ot = [S.sb("ot%d" % i, [128, 512]) for i in range(2)]
            gst = [S.sb("gst%d" % i, [128, 512], BF16) for i in range(2)]
            lacc = [S.sb("lacc%d" % i, [128, 512]) for i in range(2)]
            ones_f = S.sb("ones_f", [128, 128])
            S.op("pool", lambda e: e.memset(ones_f[:], 1.0), writes=[ones_f])
            ps_s = [S.ps("ps_s%d" % i, [128, 512]) for i in range(3)]
            po = [S.ps("po%d" % i, [128, 512]) for i in range(2)]
            pl = [S.ps("pl%d" % i, [128, 512]) for i in range(2)]
            scale = 192.0 ** -0.5
            sk = [0]
            work = [(h, qc) for h in range(4) for qc in range(SEQ // 512)]

            def load_q(idx):
                h, qc = work[idx]
                i = idx % 2
                cs = slice(qc * 512, (qc + 1) * 512)
                S.dma("sp", qn[i][:], QN[h, :, cs], reads=[QNd], writes=[qn[i]])
                S.dma("sp", qp[i][:], QP[h, :, cs], reads=[QPd], writes=[qp[i]])
                S.dma("sp", szb[i][:], SZT[h * 128:(h + 1) * 128, cs], reads=[SZd], writes=[szb[i]])

            for idx, (h, qc) in enumerate(work):
                i = idx % 2
                if qc == 0:
                    S.dma("sp", kn[:], KN[h], reads=[KNd], writes=[kn])
                    for v4 in range(4):
                        S.dma("sp", vt[:, v4 * 32:(v4 + 1) * 32, :],
                              VV[v4 * 4096:(v4 + 1) * 4096, h * 128:(h + 1) * 128].rearrange("(t p) d -> p t d", p=128),
                              reads=[VVd], writes=[vt])
                    load_q(idx)
                if idx + 1 < len(work) and work[idx + 1][1] != 0:
                    load_q(idx + 1)
                nk = 4 * (qc + 1)

                def Sm(kt):
                    si = (sk[0] + kt) % 3
                    S.op("pe", lambda e: e.matmul(ps_s[si][:], kn[:, kt * 128:(kt + 1) * 128], qn[i][:], start=True, stop=False),
                         reads=[kn, qn[i]], writes=[ps_s[si]], signal=False)
                    S.op("pe", lambda e: e.matmul(ps_s[si][:], kp[:, kt * 128:(kt + 1) * 128], qp[i][:], start=False, stop=True),
                         reads=[kp, qp[i], kn, qn[i]], writes=[ps_s[si]])

                Sm(0)
                for kt in range(nk):
                    if kt + 1 < nk:
                        Sm(kt + 1)
                    si = (sk[0] + kt) % 3
                    S.op("act", lambda e, si=si: e.activation(out=pt[si][:], in_=ps_s[si][:], func=AF.Exp, scale=scale),
                         reads=[ps_s[si]], writes=[pt[si]])
                    d = kt - 4 * qc
                    if d >= 0:
                        S.op("dve", lambda e, si=si, d=d: e.tensor_tensor(out=pt[si][:], in0=pt[si][:], in1=cm[:, d * 512:(d + 1) * 512], op=ALU.mult),
                             reads=[pt[si], cm], writes=[pt[si]])
                    if kt % 2 == 0:
                        S.op("pe", lambda e, si=si, kt=kt: e.matmul(po[i][:], vt[:, kt, :], pt[si][:], start=(kt == 0), stop=(kt == nk - 1)),
                             reads=[vt, pt[si]], writes=[po[i]], signal=False)
                        S.op("pe", lambda e, si=si, kt=kt: e.matmul(pl[i][:], ones[:], pt[si][:], start=(kt == 0), stop=False),
                             reads=[ones, pt[si], vt], writes=[pl[i], po[i]])
                    else:
                        S.op("pe", lambda e, si=si, kt=kt: e.matmul(po[i][:], vt[:, kt, :], pt[si][:], start=(kt == 0), stop=(kt == nk - 1)),
                             reads=[vt, pt[si]], writes=[po[i]])
                        if kt == 1:
                            S.op("pool", lambda e, si=si: e.tensor_copy(out=lacc[i][:], in_=pt[si][:]), reads=[pt[si]], writes=[lacc[i]])
                        else:
                            S.op("pool", lambda e, si=si: e.tensor_tensor(out=lacc[i][:], in0=lacc[i][:], in1=pt[si][:], op=ALU.add),
                                 reads=[lacc[i], pt[si]], writes=[lacc[i]])
                sk[0] += nk
                S.op("pe", lambda e: e.matmul(pl[i][:], ones_f[:], lacc[i][:], start=False, stop=True),
                     reads=[ones_f, lacc[i]], writes=[pl[i]])
                S.op("dve", lambda e: e.reciprocal(out=rl[i][:], in_=pl[i][:]), reads=[pl[i]], writes=[rl[i]])
                S.op("dve", lambda e: e.tensor_tensor(out=ot[i][:], in0=po[i][:], in1=rl[i][:], op=ALU.mult),
                     reads=[po[i], rl[i]], writes=[ot[i]])
                S.op("pool", lambda e: e.tensor_tensor(out=gst[i][:], in0=ot[i][:], in1=szb[i][:], op=ALU.mult),
                     reads=[ot[i], szb[i]], writes=[gst[i]])
                S.dma("act", GTP[(qc // 8) * 512 + h * 128:(qc // 8) * 512 + (h + 1) * 128, (qc % 8) * 512:(qc % 8 + 1) * 512], gst[i][:],
                      reads=[gst[i]], writes=[GTPd])
                if qc == SEQ // 512 - 1:
                    io["gt_head_done"](h)
        S.barrier()


def _inv_freq(dim):
    return np.asarray(1.0 / (10000.0 ** (jnp.arange(0, dim, 2, dtype=jnp.float32) / dim)), dtype=np.float32)


def _dil_consts():
    inv = _inv_freq(128)
    inv128 = np.concatenate([inv, inv]).reshape(128, 1).astype(np.float32)
    sgn = np.concatenate([-np.ones(64), np.ones(64)]).reshape(128, 1).astype(np.float32)
    k = np.arange(128)[:, None]
    q = np.arange(128)[None, :]
    m_prev = (k >= q).astype(np.float32)
    m_cur = (k <= q).astype(np.float32)
    return inv128, sgn, m_prev, m_cur


_PROGS = {}


def build_fused():
    nc = bass.Bass("TRN2", target_bir_lowering=False)

    def I(name, shape, dt=F32):
        return nc.dram_tensor(name, shape, dt, kind="ExternalInput").ap()

    def T(name, shape, dt):
        return nc.dram_tensor(name, shape, dt, kind="Internal").ap()

    x_own = I("x_own", [NTOK, D])
    x_halo = I("x_halo", [2048, D])
    pos_d = I("pos_d", [1, 6144], I32)
    pos_a = I("pos_a", [1, SEQ], I32)
    inv128, sgn128 = I("inv128", [128, 1]), I("sgn128", [128, 1])
    inv64, sgn64 = I("inv64", [64, 1]), I("sgn64", [64, 1])
    masks = I("masks", [128, 512], BF16)
    cmask = I("cmask", [128, 2048], BF16)
    dsa_w_in = [I("dsa_w_in%d" % j, [D, 20480]) for j in range(2)]
    dsa_w_out = [I("dsa_w_out%d" % j, [D, D]) for j in range(2)]
    mla_w_in_c = [I("mla_w_in_c%d" % j, [D, 1600]) for j in range(2)]
    mla_w_uq_c = [I("mla_w_uq_c%d" % j, [512, 768]) for j in range(2)]
    mla_w_kk_c = [I("mla_w_kk_c%d" % j, [512, 512]) for j in range(2)]
    mla_w_kv_c = [I("mla_w_kv_c%d" % j, [512, 512]) for j in range(2)]
    mla_qg = [I("mla_qg%d" % j, [128, 4]) for j in range(2)]
    mla_kg = [I("mla_kg%d" % j, [128, 4]) for j in range(2)]
    mla_w_out = [I("mla_w_out%d" % j, [D, D]) for j in range(2)]
    lng = [I("lng%d" % l, [1, D]) for l in range(DEPTH)]
    lnb = [I("lnb%d" % l, [1, D]) for l in range(DEPTH)]
    out = nc.dram_tensor("out", [NTOK, D], F32, kind="ExternalOutput").ap()
    XTS_t = nc.dram_tensor("XT_send", [D, NTOK], BF16)
    XTALL_t = nc.dram_tensor("XT_allr", [4 * D, NTOK], BF16)
    GTS_t = nc.dram_tensor("GT_send", [2048, NTOK], BF16)
    GTALL_t = nc.dram_tensor("GT_allr", [4 * 2048, NTOK], BF16)
    XTS, XTALL, GTS, GTALL = XTS_t.ap(), XTALL_t.ap(), GTS_t.ap(), GTALL_t.ap()
    XTH0 = T("XT_halo0", [D, 2048], BF16)
    R = [T("R%d" % i, [NTOK, D], F32) for i in range(3)]
    scr = {
        "QT": T("QT", [48, 128, 4096], BF16), "KT": T("KT", [48, 128, 6144], BF16), "VV": T("VV", [3, 6144, D], BF16),
        "ZT": T("ZT", [D, NTOK], BF16), "GT": T("GT", [D, NTOK], BF16),
        "Ctab": T("Ctab", [128, 6144], F32), "Stab": T("Stab", [128, 6144], F32),
        "QN": T("QN", [4, 128, SEQ], BF16), "QP": T("QP", [4, 64, SEQ], BF16), "KN": T("KN", [4, 128, SEQ], BF16),
        "KP": T("KP", [64, SEQ], BF16), "VVm": T("VVm", [SEQ, 512], BF16), "SZT": T("SZT", [512, SEQ], BF16),
        "Ctab64": T("Ctab64", [64, SEQ], F32), "Stab64": T("Stab64", [64, SEQ], F32),
    }
    with ExitStack() as es0:
        S = Sched(nc, es0)
        ident, ones = make_consts(S)
        io0 = dict(scr)
        for k in list(scr):
            io0[k + "_d"] = S.dram(scr[k])
        XTS_d, XTALL_d, GTS_d, GTALL_d, XTH0_d = S.dram(XTS), S.dram(XTALL), S.dram(GTS), S.dram(GTALL), S.dram(XTH0)
        R_d = [S.dram(r_) for r_ in R]
        out_d = S.dram(out)
        ext = Buf(None)
        pid = nc.gpsimd.partition_id()
        rown = (pid % 4) * 2048
        rprev = (pid + 3) % 4
        XTALL4 = XTALL.rearrange("(c s p) t -> c s p t", c=16, s=4)
        for (pp_, n_, iv, sg, ck, sk_, P) in ((pos_d, 6144, inv128, sgn128, "Ctab", "Stab", 128),
                                             (pos_a, SEQ, inv64, sgn64, "Ctab64", "Stab64", 64)):
            with ExitStack() as es:
                S.es = es
                inv = S.sb("inv", [P, 1])
                sgn = S.sb("sgn", [P, 1])
                S.dma("sp", inv[:], iv, writes=[inv])
                S.dma("sp", sgn[:], sg, writes=[sgn])
                rope_tables(S, pp_, n_, inv, sgn, io0[ck + "_d"], io0[sk_ + "_d"], P)
            S.barrier()
        emit_p0(S, ident, x_own, NTOK, XTS, XTS_d)
        emit_p0(S, ident, x_halo, 2048, XTH0, XTH0_d)
        XTH0v = XTH0.rearrange("(c p) t -> p c t", p=128)

        def xt_load(b, dst):
            rr, cb = b // 8, b % 8
            S.dma("sp", dst[:], XTALL4[:, rr, :, cb * 512:(cb + 1) * 512].rearrange("c p t -> p c t"),
                  reads=[XTALL_d], writes=[dst])

        GTloc, GTloc_d = scr["GT"], io0["GT_d"]
        GTlv = GTloc.rearrange("(c p) t -> p c t", p=128)

        def gt_gather():
            S.dma("pool", GTloc, GTALL[bass.ds(rown, 2048), :], reads=[GTALL_d], writes=[GTloc_d])

        def gt_head_done(h):
            for tr in range(4):
                k = tr * 4 + h
                S.collective(GTS[k * 128:(k + 1) * 128, :], GTALL[k * 512:(k + 1) * 512, :], GTS_d, GTALL_d)

        def gt_load_loc(t, dst):
            S.dma("sp", dst[:], GTlv[:, :, t * 128:(t + 1) * 128], reads=[GTloc_d], writes=[dst])

        def halo_static(xT):
            S.dma("sp", xT[:], XTH0v, reads=[XTH0_d], writes=[xT])

        def halo_dyn(xT):
            src = XTALL4[:, bass.ds(rprev, 1), :, 2048:4096].rearrange("c o p t -> p (c o) t")
            S.dma("pool", xT[:], src, reads=[XTALL_d], writes=[xT])

        resid_in, resid_in_d = x_own, ext
        for layer in range(DEPTH):
            j = layer // 2
            last = layer == DEPTH - 1
            ro, rod = (out, out_d) if last else (R[layer], R_d[layer])
            if layer % 2 == 0:
                io = dict(io0)
                io.update({"XT_own": XTS, "XT_own_d": XTS_d, "resid": resid_in, "resid_d": resid_in_d,
                           "w_in": dsa_w_in[j], "wout": dsa_w_out[j], "lng": lng[layer], "lnb": lnb[layer], "masks": masks,
                           "ro": ro, "rod": rod, "XTo": XTS, "xtd": XTS_d,
                           "halo_load": halo_static if layer == 0 else halo_dyn})
                emit_dil(S, ident, ones, io)
            else:
                io = dict(io0)
                io.update({"w_in_c": mla_w_in_c[j], "w_uq_c": mla_w_uq_c[j], "w_kk_c": mla_w_kk_c[j], "w_kv_c": mla_w_kv_c[j],
                           "qg": mla_qg[j], "kg": mla_kg[j], "cmask": cmask, "GTP": GTS, "GTP_d": GTS_d, "xt_load": xt_load,
                           "gt_head_done": gt_head_done})
                emit_mla_a(S, ident, ones, io)
                gt_gather()
                with ExitStack() as es:
                    S.es = es
                    emit_epilogue(S, ident, gt_load_loc, resid_in, resid_in_d, mla_w_out[j], lng[layer], lnb[layer],
                                  ro, rod, None if last else XTS, None if last else XTS_d, perm=MLA_PERM)
                S.barrier()
            if not last:
                S.allgather16(XTS_t, XTALL_t, XTS_d, XTALL_d)
            resid_in, resid_in_d = ro, rod
        S.drain([out_d])
    return nc


def kernel(x, positions, dsa_w_in, dsa_w_out, mla_w_in, mla_q_norm, mla_w_uq, mla_kv_norm, mla_w_ukv,
           mla_w_out, ln_g, ln_b):
    x = np.asarray(x)
    positions = np.asarray(positions)
    args = [np.asarray(a) for a in (dsa_w_in, dsa_w_out, mla_w_in, mla_q_norm, mla_w_uq, mla_kv_norm, mla_w_ukv, mla_w_out, ln_g, ln_b)]
    dsa_w_in, dsa_w_out, mla_w_in, mla_q_norm, mla_w_uq, mla_kv_norm, mla_w_ukv, mla_w_out, ln_g, ln_b = args
    if "fused" not in _PROGS:
        _PROGS["fused"] = build_fused()
    nc = _PROGS["fused"]
    bf = ml_dtypes.bfloat16
    inv128, sgn128, m_prev, m_cur = _dil_consts()
    inv = _inv_freq(64)
    inv64 = np.concatenate([inv, inv]).reshape(64, 1).astype(np.float32)
    sgn64 = np.concatenate([-np.ones(32), np.ones(32)]).reshape(64, 1).astype(np.float32)
    kk = np.arange(128)[:, None]
    qq = np.arange(512)[None, :]
    cmask = np.concatenate([((d * 128 + kk) <= qq).astype(np.float32) for d in range(4)], axis=1).astype(bf)
    in_maps = []
    for c in range(NCORES):
        bb, r = c // 4, c % 4
        m = {"x_own": np.ascontiguousarray(x[bb, r * NTOK:(r + 1) * NTOK])}
        if r == 0:
            m["x_halo"] = np.zeros((2048, D), np.float32)
            hpos = np.zeros((2048,), np.int32)
            m_halo = np.zeros_like(m_prev)
        else:
            m["x_halo"] = np.ascontiguousarray(x[bb, r * NTOK - 2048:r * NTOK])
            hpos = positions[bb, r * NTOK - 2048:r * NTOK]
            m_halo = m_prev
        m["masks"] = np.concatenate([m_prev, m_cur, m_halo, m_cur], axis=1).astype(bf)
        m["pos_d"] = np.concatenate([hpos, positions[bb, r * NTOK:(r + 1) * NTOK]]).reshape(1, 6144).astype(np.int32)
        m["pos_a"] = np.ascontiguousarray(positions[bb].reshape(1, SEQ)).astype(np.int32)
        m.update({"inv128": inv128, "sgn128": sgn128, "inv64": inv64, "sgn64": sgn64, "cmask": cmask})
        for j in range(2):
            m["dsa_w_in%d" % j] = dsa_w_in[j]
            m["dsa_w_out%d" % j] = dsa_w_out[j]
            w_in = mla_w_in[j]
            m["mla_w_in_c%d" % j] = np.ascontiguousarray(np.concatenate([w_in[:, 0:1088], w_in[:, 1088 + r * 512:1088 + (r + 1) * 512]], axis=1))
            m["mla_w_uq_c%d" % j] = np.ascontiguousarray(mla_w_uq[j][:, r * 768:(r + 1) * 768])
            wk4 = mla_w_ukv[j].reshape(512, 16, 256)[:, 4 * r:4 * r + 4]
            m["mla_w_kk_c%d" % j] = np.ascontiguousarray(wk4[:, :, 0:128].reshape(512, 512))
            m["mla_w_kv_c%d" % j] = np.ascontiguousarray(wk4[:, :, 128:256].reshape(512, 512))
            m["mla_qg%d" % j] = np.ascontiguousarray(mla_q_norm[j].reshape(4, 128).T).astype(np.float32)
            m["mla_kg%d" % j] = np.ascontiguousarray(mla_kv_norm[j].reshape(4, 128).T).astype(np.float32)
            m["mla_w_out%d" % j] = mla_w_out[j]
        for l in range(DEPTH):
            m["lng%d" % l] = np.ascontiguousarray(ln_g[l].reshape(1, D))
            m["lnb%d" % l] = np.ascontiguousarray(ln_b[l].reshape(1, D))
        in_maps.append(m)
    res = run_bass_kernel_spmd(nc, in_maps, core_ids=list(range(NCORES)))
    out = np.empty((2, SEQ, D), np.float32)
    for c in range(NCORES):
        out[c // 4, (c % 4) * NTOK:(c % 4 + 1) * NTOK] = res.results[c]["out"]
    return out
```

```python
import math
from contextlib import ExitStack

import numpy as np
import ml_dtypes
import jax.numpy as jnp

import concourse.bass as bass
import concourse.mybir as mybir
from concourse.bass_utils import run_bass_kernel_spmd

F32 = mybir.dt.float32
BF16 = mybir.dt.bfloat16
I32 = mybir.dt.int32
AF = mybir.ActivationFunctionType
ALU = mybir.AluOpType

D = 2048
SEQ = 16384
NTOK = 4096
DEPTH = 4
ALPHA = (2 * DEPTH) ** 0.25
LN_EPS = 1e-5
RMS_EPS = 1e-6
DIL = (1, 4, 16)
NCORES = 8
TWO_PI = 2.0 * math.pi
CW1 = 6.28125
CW2 = TWO_PI - CW1
MAGIC = 12582912.0


class Buf:
    __slots__ = ("t", "w", "r")

    def __init__(self, t):
        self.t = t
        self.w = {}
        self.r = {}

    def __getitem__(self, k):
        return self.t[k]


class Sched:
    def __init__(self, nc, es):
        self.nc = nc
        self.es = es
        self.es_sem = es
        self.eng = {"pe": nc.tensor, "act": nc.scalar, "dve": nc.vector, "pool": nc.gpsimd, "sp": nc.sync}
        self.esem = {}
        self.ecnt = {}
        self.known = {e: {} for e in self.eng}
        self.nsem = 0
        for e in ("pe", "act", "dve", "pool"):
            self._roll(e)
        self.dpool = {}
        self.dpos = {}
        for q, n in (("sp", 20), ("act", 6), ("pool", 8)):
            self.dpool[q] = [[self._newsem(), 0] for _ in range(n)]
            self.dpos[q] = 0

    def _newsem(self):
        self.nsem += 1
        return self.es_sem.enter_context(self.nc.semaphore("s%d" % self.nsem))

    def _roll(self, e):
        self.esem[e] = self._newsem()
        self.ecnt[e] = 0

    def sb(self, name, shape, dt=F32):
        self.nsem += 1
        return Buf(self.es.enter_context(self.nc.sbuf_tensor("sb%d_%s" % (self.nsem, name), shape, dt)))

    def ps(self, name, shape, dt=F32):
        self.nsem += 1
        return Buf(self.es.enter_context(self.nc.psum_tensor("ps%d_%s" % (self.nsem, name), shape, dt)))

    def dram(self, t):
        return Buf(t)

    def _waits(self, e, reads, writes):
        deps = {}

        def add(tok):
            if tok is None:
                return
            s, v = tok
            if deps.get(id(s), (None, 0))[1] < v:
                deps[id(s)] = (s, v)

        for b in reads:
            for s, v in b.w.values():
                add((s, v))
        for b in writes:
            for s, v in b.w.values():
                add((s, v))
            for s, v in b.r.values():
                add((s, v))
        kn = self.known[e]
        for sid, (s, v) in deps.items():
            if e == "pe" and s is self.esem["pe"]:
                continue
            if kn.get(sid, 0) >= v:
                continue
            self.eng[e].wait_ge(s, v)
            kn[sid] = v

    def _commit(self, tok, reads, writes):
        s, v = tok
        for b in reads:
            b.r[id(s)] = (s, v)
        for b in writes:
            b.w[id(s)] = tok
            b.r = {}

    def op(self, e, fn, reads=(), writes=(), signal=True):
        self._waits(e, reads, writes)
        inst = fn(self.eng[e])
        if signal:
            if self.ecnt[e] >= 30000:
                self._roll(e)
            self.ecnt[e] += 1
            inst.then_inc(self.esem[e], 1)
            self._commit((self.esem[e], self.ecnt[e]), reads, writes)
        return inst

    def dma(self, q, out, in_, reads=(), writes=(), **kw):
        pool = self.dpool[q]
        i = self.dpos[q]
        self.dpos[q] = (i + 1) % len(pool)
        slot = pool[i]
        if slot[1] >= 1800:
            slot[0] = self._newsem()
            slot[1] = 0
        s = slot[0]
        kn = self.known[q]
        if slot[1] > 0 and kn.get(id(s), 0) < slot[1] * 16:
            self.eng[q].wait_ge(s, slot[1] * 16)
            kn[id(s)] = slot[1] * 16
        self._waits(q, reads, writes)
        slot[1] += 1
        self.eng[q].dma_start(out=out, in_=in_, **kw).then_inc(s, 16)
        self._commit((s, slot[1] * 16), reads, writes)

    def collective(self, send_ap, recv_ap, send_d, recv_d):
        if not hasattr(self, "csem"):
            self.csem = self._newsem()
            self.ccnt = 0
        self._waits("pool", [send_d], [recv_d])
        self.ccnt += 1
        self.nc.gpsimd.collective_compute(
            "AllGather", ALU.bypass, replica_groups=[[0, 1, 2, 3], [4, 5, 6, 7]],
            ins=[send_ap], outs=[recv_ap]).then_inc(self.csem)
        self._commit((self.csem, self.ccnt), [send_d], [recv_d])

    def allgather16(self, send_t, recv_t, send_d, recv_d):
        sa = send_t.ap()
        ra = recv_t.ap()
        for k in range(16):
            self.collective(sa[k * 128:(k + 1) * 128, :], ra[k * 512:(k + 1) * 512, :], send_d, recv_d)

    def barrier(self):
        toks = []
        for e in ("pe", "act", "dve", "pool"):
            if self.ecnt[e] > 0:
                toks.append((self.esem[e], self.ecnt[e]))
        for q in self.dpool:
            for s_, n in self.dpool[q]:
                if n > 0:
                    toks.append((s_, n * 16))
        if hasattr(self, "csem") and self.ccnt > 0:
            toks.append((self.csem, self.ccnt))
        for e in self.eng:
            kn = self.known[e]
            for s_, v in toks:
                if e == "pe" and s_ is self.esem["pe"]:
                    continue
                if kn.get(id(s_), 0) >= v:
                    continue
                self.eng[e].wait_ge(s_, v)
                kn[id(s_)] = v

    def drain(self, bufs):
        self._waits("sp", bufs, ())


def ap3(t, off, dims):
    return bass.AP(t.tensor if hasattr(t, "tensor") else t, off, dims)


def sb_ap(buf, off, dims):
    base = buf.t[:]
    return bass.AP(base.tensor, base.offset + off, dims)


def make_consts(S):
    nc = S.nc
    ident = S.sb("ident", [128, 128], BF16)
    ones = S.sb("ones", [128, 128], BF16)
    S.op("pool", lambda g: g.memset(ident[:], 1.0), writes=[ident])
    S.op("pool", lambda g: g.affine_select(out=ident[:], in_=ident[:], pattern=[[-1, 128]],
                                           compare_op=ALU.is_equal, fill=0.0, base=0, channel_multiplier=1),
         reads=[ident], writes=[ident])
    S.op("pool", lambda g: g.memset(ones[:], 1.0), writes=[ones])
    return ident, ones


def emit_to_xt(S, src_bf, ident, ptr, xts, XTd, XT_ap_fn, t, q="sp"):
    for c in range(16):
        S.op("pe", lambda e, c=c: e.transpose(ptr[:, c * 128:(c + 1) * 128], src_bf[:, c * 128:(c + 1) * 128], ident[:]),
             reads=[src_bf, ident], writes=[ptr], signal=(c == 15))
    S.op("dve", lambda e: e.tensor_copy(out=xts[:], in_=ptr[:]), reads=[ptr], writes=[xts])
    S.dma(q, XT_ap_fn(t), xts[:].rearrange("p (c t) -> p c t", c=16), reads=[xts], writes=[XTd])


def rope_tables(S, pos_ap, ntok, inv, sgn, Cd, Sd, nparts, chunk=2048):
    P = nparts
    pi_ = S.sb("rt_pi", [P, chunk], I32)
    pf = S.sb("rt_pf", [P, chunk])
    ang = S.sb("rt_ang", [P, chunk])
    k = S.sb("rt_k", [P, chunk])
    r = S.sb("rt_r", [P, chunk])
    o = S.sb("rt_o", [P, chunk])
    for c0 in range(0, ntok, chunk):
        src = bass.AP(pos_ap.tensor, pos_ap.offset + c0, [[0, P], [1, chunk]])
        S.dma("sp", pi_[:], src, writes=[pi_])
        S.op("dve", lambda e: e.tensor_copy(out=pf[:], in_=pi_[:]), reads=[pi_], writes=[pf])
        S.op("dve", lambda e: e.tensor_scalar(out=ang[:], in0=pf[:], scalar1=inv[:, 0:1], scalar2=None, op0=ALU.mult),
             reads=[pf, inv], writes=[ang])
        for which, dst in ((0, Sd), (1, Cd)):
            S.op("dve", lambda e: e.tensor_scalar(out=k[:], in0=ang[:], scalar1=1.0 / TWO_PI, scalar2=MAGIC,
                                                  op0=ALU.mult, op1=ALU.add), reads=[ang], writes=[k])
            S.op("dve", lambda e: e.tensor_scalar(out=k[:], in0=k[:], scalar1=-MAGIC, scalar2=None, op0=ALU.add),
                 reads=[k], writes=[k])
            S.op("dve", lambda e: e.scalar_tensor_tensor(out=r[:], in0=k[:], scalar=-CW1, in1=ang[:], op0=ALU.mult, op1=ALU.add),
                 reads=[k, ang], writes=[r])
            S.op("dve", lambda e: e.scalar_tensor_tensor(out=r[:], in0=k[:], scalar=-CW2, in1=r[:], op0=ALU.mult, op1=ALU.add),
                 reads=[k, r], writes=[r])
            S.op("dve", lambda e: e.tensor_scalar(out=r[:], in0=r[:], scalar1=3.1415925, scalar2=-3.1415925, op0=ALU.min, op1=ALU.max),
                 reads=[r], writes=[r])
            if which == 1:
                S.op("dve", lambda e: e.scalar_tensor_tensor(out=r[:], in0=r[:], scalar=-1.0, in1=r[:], op0=ALU.mult, op1=ALU.max),
                     reads=[r], writes=[r])
                S.op("dve", lambda e: e.tensor_scalar(out=r[:], in0=r[:], scalar1=-1.0, scalar2=math.pi / 2, op0=ALU.mult, op1=ALU.add),
                     reads=[r], writes=[r])
            S.op("act", lambda e: e.activation(out=o[:], in_=r[:], func=AF.Sin), reads=[r], writes=[o])
            if which == 0:
                S.op("dve", lambda e: e.tensor_scalar(out=o[:], in0=o[:], scalar1=sgn[:, 0:1], scalar2=None, op0=ALU.mult),
                     reads=[o, sgn], writes=[o])
            S.dma("sp", dst.t[:, c0:c0 + chunk], o[:], reads=[o], writes=[dst])


def dil_dims(g, m):
    if g == 0:
        return 512 * m, [[1, 512]]
    if g == 1:
        return 512 * m, [[1, 4], [4, 128]]
    return 4 * m, [[1, 4], [16, 128]]


def dil_tile_dims(g, tt):
    if g == 0:
        return 128 * tt, [[1, 128]]
    if g == 1:
        return 512 * (tt // 4) + (tt % 4), [[4, 128]]
    return tt, [[16, 128]]


def emit_p0(S, ident, x, ntok, XT, XTd):
    with ExitStack() as es:
        S.es = es
        xf = [S.sb("xf%d" % i, [128, D]) for i in range(2)]
        xb = [S.sb("xb%d" % i, [128, D], BF16) for i in range(2)]
        ptr = [S.ps("ptr%d" % i, [128, D], BF16) for i in range(2)]
        xts = [S.sb("xts%d" % i, [128, D], BF16) for i in range(2)]
        XTv = XT.rearrange("(c p) t -> p c t", p=128)
        for t in range(ntok // 128):
            i = t % 2
            S.dma("sp", xf[i][:], x[t * 128:(t + 1) * 128, :], writes=[xf[i]])
            S.op("pool", lambda e, i=i: e.tensor_copy(out=xb[i][:], in_=xf[i][:]), reads=[xf[i]], writes=[xb[i]])
            emit_to_xt(S, xb[i], ident, ptr[i], xts[i], XTd, lambda t: XTv[:, :, t * 128:(t + 1) * 128], t)
    S.barrier()


MLA_PERM = [(c % 4) * 4 + c // 4 for c in range(16)]


def emit_epilogue(S, ident, gt_load, resid, resid_d, wout, lng, lnb, resid_out, resid_out_d, XT_out, XT_out_d, perm=None):
    perm = perm or list(range(16))
    nc = S.nc
    wo = S.sb("wo", [128, 16, D], BF16)
    wst = [S.sb("wst%d" % i, [128, 4, 512]) for i in range(2)]
    woutv = wout.rearrange("(c p) n -> p c n", p=128)
    k = 0
    for c4 in range(4):
        for n in range(4):
            i = k % 2
            k += 1
            S.dma("sp", wst[i][:], woutv[:, c4 * 4:(c4 + 1) * 4, n * 512:(n + 1) * 512], writes=[wst[i]])
            S.op("pool", lambda e, i=i, c4=c4, n=n: e.tensor_copy(out=wo[:, c4 * 4:(c4 + 1) * 4, n * 512:(n + 1) * 512], in_=wst[i][:]),
                 reads=[wst[i]], writes=[wo])
    g_b = S.sb("lng_b", [128, D])
    b_b = S.sb("lnb_b", [128, D])
    S.dma("sp", g_b[:], bass.AP(lng.tensor, lng.offset, [[0, 128], [1, D]]), writes=[g_b])
    S.dma("sp", b_b[:], bass.AP(lnb.tensor, lnb.offset, [[0, 128], [1, D]]), writes=[b_b])
    gt = [S.sb("gt%d" % i, [128, 16, 128], BF16) for i in range(2)]
    rs = [S.sb("rs%d" % i, [128, D]) for i in range(2)]
    v = [S.sb("v%d" % i, [128, D]) for i in range(2)]
    xo = [S.sb("xo%d" % i, [128, D]) for i in range(2)]
    xb = [S.sb("exb%d" % i, [128, D], BF16) for i in range(2)]
    st6_ = [S.sb("st6_%d" % i, [128, 4, 6]) for i in range(2)]
    mv_ = [S.sb("mv%d" % i, [128, 2]) for i in range(2)]
    rstd_ = [S.sb("rstd%d" % i, [128, 1]) for i in range(2)]
    py = [S.ps("py%d" % n, [128, 512]) for n in range(4)]
    ptr = S.ps("eptr", [128, D], BF16)
    xts = [S.sb("exts%d" % i, [128, D], BF16) for i in range(2)]
    XTv = XT_out.rearrange("(c p) t -> p c t", p=128) if XT_out is not None else None
    def _loads(t):
        gt_load(t, gt[t % 2])
        S.dma("sp", rs[t % 2][:], resid[t * 128:(t + 1) * 128, :], reads=[resid_d], writes=[rs[t % 2]])

    _loads(0)
    for t in range(NTOK // 128):
        i = t % 2
        if t + 1 < NTOK // 128:
            _loads(t + 1)
        for n in range(4):
            for c in range(16):
                S.op("pe", lambda e, c=c, n=n, i=i: e.matmul(py[n][:], gt[i][:, c, :], wo[:, perm[c], n * 512:(n + 1) * 512],
                                                             start=(c == 0), stop=(c == 15)),
                     reads=[gt[i], wo], writes=[py[n]], signal=(c == 15))
            S.op("dve", lambda e, n=n, i=i: e.scalar_tensor_tensor(out=v[i][:, n * 512:(n + 1) * 512], in0=rs[i][:, n * 512:(n + 1) * 512],
                                                                    scalar=ALPHA, in1=py[n][:], op0=ALU.mult, op1=ALU.add),
                 reads=[rs[i], py[n]], writes=[v[i]])
        st6, mv, rstd = st6_[i], mv_[i], rstd_[i]
        for n in range(4):
            S.op("dve", lambda e, n=n, i=i: e.bn_stats(out=st6[:, n, :], in_=v[i][:, n * 512:(n + 1) * 512]),
                 reads=[v[i]], writes=[st6])
        S.op("dve", lambda e: e.bn_aggr(out=mv[:], in_=st6[:]), reads=[st6], writes=[mv])
        S.op("dve", lambda e: e.tensor_scalar(out=rstd[:], in0=mv[:, 1:2], scalar1=LN_EPS, scalar2=None, op0=ALU.add),
             reads=[mv], writes=[rstd])
        S.op("act", lambda e: e.activation(out=rstd[:], in_=rstd[:], func=AF.Sqrt), reads=[rstd], writes=[rstd])
        S.op("dve", lambda e: e.reciprocal(out=rstd[:], in_=rstd[:]), reads=[rstd], writes=[rstd])
        S.op("dve", lambda e, i=i: e.scalar_tensor_tensor(out=v[i][:], in0=v[i][:], scalar=mv[:, 0:1], in1=g_b[:],
                                                          op0=ALU.subtract, op1=ALU.mult), reads=[v[i], mv, g_b], writes=[v[i]])
        S.op("dve", lambda e, i=i: e.scalar_tensor_tensor(out=xo[i][:], in0=v[i][:], scalar=rstd[:, 0:1], in1=b_b[:],
                                                          op0=ALU.mult, op1=ALU.add), reads=[v[i], rstd, b_b], writes=[xo[i]])
        S.dma("act", resid_out[t * 128:(t + 1) * 128, :], xo[i][:], reads=[xo[i]], writes=[resid_out_d])
        if XT_out is not None:
            S.op("pool", lambda e, i=i: e.tensor_copy(out=xb[i][:], in_=xo[i][:]), reads=[xo[i]], writes=[xb[i]])
            emit_to_xt(S, xb[i], ident, ptr, xts[i], XT_out_d, lambda t: XTv[:, :, t * 128:(t + 1) * 128], t, q="act")


def emit_dil(S, ident, ones, io):
    XTo_in, XTo_in_d = io["XT_own"], io["XT_own_d"]
    resid, resid_d = io["resid"], io["resid_d"]
    w_in, wout, lng, lnb, masks_in = io["w_in"], io["wout"], io["lng"], io["lnb"], io["masks"]
    ro, rod, XTo, xtd = io["ro"], io["rod"], io["XTo"], io["xtd"]
    QT, KT, VV, ZT, GT, Ctab, Stab = (io[k] for k in ("QT", "KT", "VV", "ZT", "GT", "Ctab", "Stab"))
    QTd, KTd, VVd, ZTd, GTd, Cd, Sd = (io[k + "_d"] for k in ("QT", "KT", "VV", "ZT", "GT", "Ctab", "Stab"))
    if True:
        with ExitStack() as es:
            S.es = es
            xT = S.sb("xT", [128, 16, 2048], BF16)
            Cb = S.sb("Cb", [128, 2048])
            Sb = S.sb("Sb", [128, 2048])
            wf = [S.sb("wf%d" % i, [128, 16, 256]) for i in range(3)]
            wb = [S.sb("wb%d" % i, [128, 16, 256], BF16) for i in range(2)]
            qst = [S.sb("qst%d" % i, [128, 2048], BF16) for i in range(2)]
            vst = [S.sb("vst%d" % i, [128, 256], BF16) for i in range(2)]
            t1 = [S.sb("t1_%d" % i, [128, 512]) for i in range(2)]
            t2 = [S.sb("t2_%d" % i, [128, 512]) for i in range(2)]
            pp = [S.ps("pp%d" % i, [128, 512]) for i in range(4)]
            pv = [S.ps("pv%d" % i, [128, 256]) for i in range(2)]
            w_inv = w_in.rearrange("(c p) n -> p c n", p=128)
            ppk = 0
            qk = 0
            vk = 0
            alltiles = []
            for blk in range(3):
                for g in range(3):
                    for kind in range(3):
                        if blk == 0 and kind == 0:
                            continue
                        for h0 in range(0, 16, 2):
                            alltiles.append((blk, kind, g, h0))
                if blk > 0:
                    for h0 in range(0, 16, 2):
                        alltiles.append((blk, 3, 0, h0))

            def w_load(n):
                (_, kind, g, h0) = alltiles[n]
                col0 = 18432 + h0 * 128 if kind == 3 else ((g * 3 + kind) * 16 + h0) * 128
                S.dma("sp", wf[n % 3][:], w_inv[:, :, col0:col0 + 256], writes=[wf[n % 3]])

            def w_cast(n):
                S.op("act", lambda e: e.activation(out=wb[n % 2][:], in_=wf[n % 3][:], func=AF.Copy),
                     reads=[wf[n % 3]], writes=[wb[n % 2]])

            w_load(0)
            w_load(1)
            w_cast(0)
            cur_blk = -1
            for n, (blk, kind, g, h0) in enumerate(alltiles):
                if blk != cur_blk:
                    cur_blk = blk
                    if blk == 0:
                        io["halo_load"](xT)
                    else:
                        S.dma("sp", xT[:], XTo_in[:, (blk - 1) * 2048:blk * 2048].rearrange("(c p) t -> p c t", p=128),
                              reads=[XTo_in_d], writes=[xT])
                    S.dma("sp", Cb[:], Ctab[:, blk * 2048:(blk + 1) * 2048], reads=[Cd], writes=[Cb])
                    S.dma("sp", Sb[:], Stab[:, blk * 2048:(blk + 1) * 2048], reads=[Sd], writes=[Sb])
                if n + 2 < len(alltiles):
                    w_load(n + 2)
                if n + 1 < len(alltiles):
                    w_cast(n + 1)
                wi = n % 2
                if True:
                    if blk == 0:
                        ms = [3] if g < 2 else [0, 1, 2, 3]
                        tts = [15] if g == 0 else ([12, 13, 14, 15] if g == 1 else list(range(16)))
                    else:
                        ms = [0, 1, 2, 3]
                        tts = list(range(16))
                    if kind in (0, 1, 3):
                        for hh in range(2):
                            h = h0 + hh
                            qi = qk % 2
                            qk += 1
                            for m in ms:
                                p = pp[ppk % 4]
                                ppk += 1
                                for c in range(16):
                                    rhs = xT[:, c, m * 512:(m + 1) * 512]
                                    S.op("pe", lambda e, p=p, wi=wi, c=c, hh=hh, rhs=rhs: e.matmul(
                                        p[:], wb[wi][:, c, hh * 128:(hh + 1) * 128], rhs, start=(c == 0), stop=(c == 15)),
                                        reads=[wb[wi], xT], writes=[p], signal=(c == 15))
                                if kind == 3:
                                    dst = qst[qi][:, m * 512:(m + 1) * 512]
                                    S.op("act", lambda e, p=p, dst=dst: e.activation(out=dst, in_=p[:], func=AF.Copy),
                                         reads=[p], writes=[qst[qi]])
                                else:
                                    ti = ppk % 2
                                    ms_ = slice(m * 512, (m + 1) * 512)
                                    S.op("dve", lambda e, p=p, ti=ti, ms_=ms_: e.tensor_tensor(out=t1[ti][:], in0=p[:], in1=Cb[:, ms_], op=ALU.mult),
                                         reads=[p, Cb], writes=[t1[ti]])
                                    S.op("dve", lambda e, p=p, ti=ti, ms_=ms_: e.tensor_tensor(out=t2[ti][0:64, :], in0=p[64:128, :], in1=Sb[0:64, ms_], op=ALU.mult),
                                         reads=[p, Sb], writes=[t2[ti]])
                                    S.op("dve", lambda e, p=p, ti=ti, ms_=ms_: e.tensor_tensor(out=t2[ti][64:128, :], in0=p[0:64, :], in1=Sb[64:128, ms_], op=ALU.mult),
                                         reads=[p, Sb], writes=[t2[ti]])
                                    if g == 0:
                                        dst = qst[qi][:, ms_]
                                        a0, a1 = t1[ti][:], t2[ti][:]
                                    elif g == 1:
                                        dst = sb_ap(qst[qi], 512 * m, [[2048, 128], [1, 128], [128, 4]])
                                        a0 = sb_ap(t1[ti], 0, [[512, 128], [4, 128], [1, 4]])
                                        a1 = sb_ap(t2[ti], 0, [[512, 128], [4, 128], [1, 4]])
                                    else:
                                        dst = sb_ap(qst[qi], 32 * m, [[2048, 128], [1, 32], [128, 16]])
                                        a0 = sb_ap(t1[ti], 0, [[512, 128], [16, 32], [1, 16]])
                                        a1 = sb_ap(t2[ti], 0, [[512, 128], [16, 32], [1, 16]])
                                    S.op("pool", lambda e, dst=dst, a0=a0, a1=a1: e.tensor_tensor(out=dst, in0=a0, in1=a1, op=ALU.add),
                                         reads=[t1[ti], t2[ti]], writes=[qst[qi]])
                            c_lo, c_hi = ms[0] * 512, (ms[-1] + 1) * 512
                            if kind == 0:
                                S.dma("pool", QT[g * 16 + h, :, (blk - 1) * 2048 + c_lo:(blk - 1) * 2048 + c_hi], qst[qi][:, c_lo:c_hi],
                                      reads=[qst[qi]], writes=[QTd])
                            elif kind == 1:
                                S.dma("pool", KT[g * 16 + h, :, blk * 2048 + c_lo:blk * 2048 + c_hi], qst[qi][:, c_lo:c_hi],
                                      reads=[qst[qi]], writes=[KTd])
                            else:
                                S.dma("pool", ZT[h * 128:(h + 1) * 128, (blk - 1) * 2048:blk * 2048], qst[qi][:],
                                      reads=[qst[qi]], writes=[ZTd])
                    else:
                        for tt in tts:
                            off, dims = dil_tile_dims(g, tt)
                            p = pv[vk % 2]
                            vi = vk % 2
                            vk += 1
                            for c in range(16):
                                lhsT = sb_ap(xT, c * 2048 + off, [[16 * 2048, 128]] + dims)
                                S.op("pe", lambda e, p=p, wi=wi, c=c, lhsT=lhsT: e.matmul(
                                    p[:], lhsT, wb[wi][:, c, :], start=(c == 0), stop=(c == 15)),
                                    reads=[wb[wi], xT], writes=[p], signal=(c == 15))
                            S.op("act", lambda e, p=p, vi=vi: e.activation(out=vst[vi][:], in_=p[:], func=AF.Copy),
                                 reads=[p], writes=[vst[vi]])
                            S.dma("act", VV[g, blk * 2048 + tt * 128:blk * 2048 + (tt + 1) * 128, h0 * 128:h0 * 128 + 256], vst[vi][:],
                                  reads=[vst[vi]], writes=[VVd])
        S.barrier()
        with ExitStack() as es:
            S.es = es
            msk = S.sb("msk", [128, 512], BF16)
            S.dma("sp", msk[:], masks_in, writes=[msk])
            accO = S.sb("accO", [128, NTOK])
            accL = S.sb("accL", [128, NTOK])
            qt = [S.sb("qt%d" % i, [128, 4096], BF16) for i in range(2)]
            kt = [S.sb("kt%d" % i, [128, 6144], BF16) for i in range(2)]
            vt = [S.sb("vt%d" % i, [128, 48, 128], BF16) for i in range(2)]
            zt = S.sb("zt", [128, NTOK], BF16)
            gst = S.sb("gst", [128, NTOK], BF16)
            pt = [S.sb("pt%d" % i, [128, 256], BF16) for i in range(4)]
            rl = [S.sb("rl%d" % i, [128, 512]) for i in range(2)]
            ot = [S.sb("ot%d" % i, [128, 512]) for i in range(2)]
            sz = [S.sb("sz%d" % i, [128, 512]) for i in range(2)]
            ps_s = [S.ps("ps_s%d" % i, [128, 256]) for i in range(4)]
            po = [S.ps("po%d" % i, [128, 512]) for i in range(2)]
            pl = [S.ps("pl%d" % i, [128, 512]) for i in range(2)]
            scale = 128.0 ** -0.5
            hg = [(h, g) for h in range(16) for g in range(3)]

            def load_hg(idx):
                h, g = hg[idx]
                bi = idx % 2
                S.dma("sp", qt[bi][:], QT[g * 16 + h], reads=[QTd], writes=[qt[bi]])
                S.dma("sp", kt[bi][:], KT[g * 16 + h], reads=[KTd], writes=[kt[bi]])
                S.dma("sp", vt[bi][:], VV[g, :, h * 128:(h + 1) * 128].rearrange("(t p) d -> p t d", p=128),
                      reads=[VVd], writes=[vt[bi]])

            load_hg(0)
            sk = 0
            ok = 0
            for h in range(16):
                S.dma("sp", zt[:], ZT[h * 128:(h + 1) * 128, :], reads=[ZTd], writes=[zt])
                for g in range(3):
                    idx = h * 3 + g
                    bi = idx % 2
                    if idx + 1 < len(hg):
                        load_hg(idx + 1)
                    Pg = 128 * DIL[g]
                    qbs = []
                    for sbk in range(2):
                        for m in range(4):
                            for qb in range(4):
                                col0 = sbk * 2048 + m * 512 + qb * 128
                                qbs.append((sbk, m, qb, col0, 2048 + col0, 2048 + col0 - Pg))

                    def emitS(i):
                        (sbk, m, qb, col0, kc, prev) = qbs[i]
                        si = (sk + i) % 4
                        S.op("pe", lambda e: e.matmul(ps_s[si][:, 0:128], kt[bi][:, prev:prev + 128], qt[bi][:, col0:col0 + 128],
                                                      start=True, stop=True),
                             reads=[kt[bi], qt[bi]], writes=[ps_s[si]], signal=False)
                        S.op("pe", lambda e: e.matmul(ps_s[si][:, 128:256], kt[bi][:, kc:kc + 128], qt[bi][:, col0:col0 + 128],
                                                      start=True, stop=True),
                             reads=[kt[bi], qt[bi]], writes=[ps_s[si]])

                    emitS(0)
                    for i, (sbk, m, qb, col0, kc, prev) in enumerate(qbs):
                        if i + 1 < len(qbs):
                            emitS(i + 1)
                        si = (sk + i) % 4
                        if qb == 0:
                            oi = ok % 2
                            ok += 1
                        S.op("act", lambda e, si=si: e.activation(out=pt[si][:], in_=ps_s[si][:], func=AF.Exp, scale=scale),
                             reads=[ps_s[si]], writes=[pt[si]])
                        moff = 256 if prev < 2048 else 0
                        S.op("dve", lambda e, si=si, moff=moff: e.tensor_tensor(out=pt[si][:], in0=pt[si][:], in1=msk[:, moff:moff + 256], op=ALU.mult),
                             reads=[pt[si], msk], writes=[pt[si]])
                        oc = slice(qb * 128, (qb + 1) * 128)
                        S.op("pe", lambda e, oi=oi, si=si, prev=prev, oc=oc: e.matmul(
                            po[oi][:, oc], vt[bi][:, prev // 128, :], pt[si][:, 0:128], start=True, stop=False),
                            reads=[vt[bi], pt[si]], writes=[po[oi]], signal=False)
                        S.op("pe", lambda e, oi=oi, si=si, kc=kc, oc=oc: e.matmul(
                            po[oi][:, oc], vt[bi][:, kc // 128, :], pt[si][:, 128:256], start=False, stop=True),
                            reads=[vt[bi], pt[si]], writes=[po[oi]], signal=False)
                        S.op("pe", lambda e, oi=oi, si=si, oc=oc: e.matmul(
                            pl[oi][:, oc], ones[:], pt[si][:, 0:128], start=True, stop=False),
                            reads=[ones, pt[si]], writes=[pl[oi]], signal=False)
                        S.op("pe", lambda e, oi=oi, si=si, oc=oc: e.matmul(
                            pl[oi][:, oc], ones[:], pt[si][:, 128:256], start=False, stop=True),
                            reads=[ones, pt[si], vt[bi]], writes=[pl[oi], po[oi]])
                        if qb == 3:
                            off, dims = dil_dims(g, m)
                            dO = sb_ap(accO, sbk * 2048 + off, [[NTOK, 128]] + dims)
                            dL = sb_ap(accL, sbk * 2048 + off, [[NTOK, 128]] + dims)
                            if g == 0:
                                S.op("dve", lambda e, oi=oi, dO=dO: e.tensor_copy(out=dO, in_=po[oi][:]), reads=[po[oi]], writes=[accO])
                                S.op("act", lambda e, oi=oi, dL=dL: e.activation(out=dL, in_=pl[oi][:], func=AF.Copy), reads=[pl[oi]], writes=[accL])
                            else:
                                S.op("dve", lambda e, oi=oi, dO=dO: e.tensor_tensor(out=dO, in0=po[oi][:], in1=dO, op=ALU.add),
                                     reads=[po[oi], accO], writes=[accO])
                                S.op("dve", lambda e, oi=oi, dL=dL: e.tensor_tensor(out=dL, in0=pl[oi][:], in1=dL, op=ALU.add),
                                     reads=[pl[oi], accL], writes=[accL])
                    sk += len(qbs)
                for c8 in range(8):
                    i = c8 % 2
                    cs = slice(c8 * 512, (c8 + 1) * 512)
                    S.op("dve", lambda e, i=i, cs=cs: e.reciprocal(out=rl[i][:], in_=accL[:, cs]), reads=[accL], writes=[rl[i]])
                    S.op("pool", lambda e, i=i, cs=cs: e.tensor_tensor(out=ot[i][:], in0=accO[:, cs], in1=rl[i][:], op=ALU.mult),
                         reads=[accO, rl[i]], writes=[ot[i]])
                    S.op("act", lambda e, i=i, cs=cs: e.activation(out=sz[i][:], in_=zt[:, cs], func=AF.Silu), reads=[zt], writes=[sz[i]])
                    S.op("pool", lambda e, i=i, cs=cs: e.tensor_tensor(out=gst[:, cs], in0=ot[i][:], in1=sz[i][:], op=ALU.mult),
                         reads=[ot[i], sz[i]], writes=[gst])
                S.dma("act", GT[h * 128:(h + 1) * 128, :], gst[:], reads=[gst], writes=[GTd])
        S.barrier()
        with ExitStack() as es:
            S.es = es
            GTv = GT.rearrange("(c p) t -> p c t", p=128)
            emit_epilogue(S, ident, lambda t, dst: S.dma("sp", dst[:], GTv[:, :, t * 128:(t + 1) * 128], reads=[GTd], writes=[dst]),
                          resid, resid_d, wout, lng, lnb, ro, rod, XTo, xtd)
        S.barrier()


def emit_mla_a(S, ident, ones, io):
    w_in_c, w_uq_c, w_kk_c, w_kv_c, qg_in, kg_in, cmask_in = (io[k] for k in ("w_in_c", "w_uq_c", "w_kk_c", "w_kv_c", "qg", "kg", "cmask"))
    GTP, GTPd = io["GTP"], io["GTP_d"]
    QN, QP, KN, KP, VV, SZT, Ctab, Stab = (io[k] for k in ("QN", "QP", "KN", "KP", "VVm", "SZT", "Ctab64", "Stab64"))
    QNd, QPd, KNd, KPd, VVd, SZd, Cd, Sd = (io[k + "_d"] for k in ("QN", "QP", "KN", "KP", "VVm", "SZT", "Ctab64", "Stab64"))
    if True:
        with ExitStack() as es:
            S.es = es
            wi = S.sb("wi", [128, 16, 1600], BF16)
            wuq = S.sb("wuq", [128, 4, 768], BF16)
            wkk = S.sb("wkk", [128, 4, 512], BF16)
            wkv = S.sb("wkv", [128, 4, 512], BF16)
            qg = S.sb("qg", [128, 4])
            kg = S.sb("kg", [128, 4])
            wst = [S.sb("wst%d" % i, [128, 4, 800]) for i in range(2)]
            S.dma("sp", qg[:], qg_in, writes=[qg])
            S.dma("sp", kg[:], kg_in, writes=[kg])
            w_inv = w_in_c.rearrange("(c p) n -> p c n", p=128)
            k = 0
            for c4 in range(4):
                for half in range(2):
                    i = k % 2
                    k += 1
                    S.dma("sp", wst[i][:], w_inv[:, c4 * 4:(c4 + 1) * 4, half * 800:(half + 1) * 800], writes=[wst[i]])
                    S.op("pool", lambda e, i=i, c4=c4, half=half: e.tensor_copy(
                        out=wi[:, c4 * 4:(c4 + 1) * 4, half * 800:(half + 1) * 800], in_=wst[i][:]), reads=[wst[i]], writes=[wi])
            for (src, dstw, gain, ncol) in ((w_uq_c, wuq, qg, 768), (w_kk_c, wkk, kg, 512), (w_kv_c, wkv, kg, 512)):
                i = k % 2
                k += 1
                S.dma("sp", wst[i][:, :, 0:ncol], src.rearrange("(c p) n -> p c n", p=128), writes=[wst[i]])
                for c in range(4):
                    S.op("dve", lambda e, i=i, c=c, dstw=dstw, gain=gain, ncol=ncol: e.tensor_scalar(
                        out=dstw[:, c, :], in0=wst[i][:, c, 0:ncol], scalar1=gain[:, c:c + 1], scalar2=None, op0=ALU.mult),
                        reads=[wst[i], gain], writes=[dstw])
            xT = [S.sb("xT%d" % i, [128, 16, 512], BF16) for i in range(2)]
            Cb = [S.sb("Cb%d" % i, [64, 512]) for i in range(2)]
            Sb = [S.sb("Sb%d" % i, [64, 512]) for i in range(2)]
            cqb = S.sb("cqb", [128, 4, 512], BF16)
            sq = S.sb("sq", [128, 4, 512], BF16)
            ckvb = S.sb("ckvb", [128, 4, 512], BF16)
            sq2 = S.sb("sq2", [128, 4, 512], BF16)
            rq = S.sb("rq", [128, 512])
            rk = S.sb("rk", [128, 512])
            rtok = S.sb("rtok", [128, 4])
            st = [S.sb("st%d" % i, [128, 512], BF16) for i in range(3)]
            t1 = S.sb("t1", [64, 512])
            ta = S.sb("ta", [64, 512])
            tb = S.sb("tb", [64, 512])
            pa = [S.ps("pa%d" % i, [128, 512]) for i in range(3)]
            pb = S.ps("pb", [128, 512])
            pc = [S.ps("pc%d" % i, [64, 512]) for i in range(2)]
            pd = S.ps("pd", [128, 4])
            cnt = {"pa": 0, "st": 0, "pc": 0}

            def nxt(key, n):
                cnt[key] += 1
                return (cnt[key] - 1) % n

            def big_mm(p, lhs_fn, xi):
                for c in range(16):
                    S.op("pe", lambda e, c=c: e.matmul(p[:], lhs_fn(c), xT[xi][:, c, :], start=(c == 0), stop=(c == 15)),
                         reads=[wi, xT[xi]], writes=[p], signal=(c == 15))

            def rstd_from(ps_buf, dst, width):
                S.op("dve", lambda e: e.tensor_scalar(out=dst[:, 0:width], in0=ps_buf[:, 0:width], scalar1=1.0 / 512, scalar2=RMS_EPS,
                                                      op0=ALU.mult, op1=ALU.add), reads=[ps_buf], writes=[dst])
                S.op("act", lambda e: e.activation(out=dst[:, 0:width], in_=dst[:, 0:width], func=AF.Sqrt), reads=[dst], writes=[dst])
                S.op("dve", lambda e: e.reciprocal(out=dst[:, 0:width], in_=dst[:, 0:width]), reads=[dst], writes=[dst])

            def rope64(srcbuf, xi, dst_ap, dstd, post=None):
                si = nxt("st", 3)
                S.op("dve", lambda e: e.tensor_tensor(out=ta[:], in0=srcbuf[0:64, :], in1=Cb[xi][:], op=ALU.mult),
                     reads=[srcbuf, Cb[xi]], writes=[ta])
                S.op("dve", lambda e: e.tensor_tensor(out=tb[0:32, :], in0=srcbuf[32:64, :], in1=Sb[xi][0:32, :], op=ALU.mult),
                     reads=[srcbuf, Sb[xi]], writes=[tb])
                S.op("dve", lambda e: e.tensor_tensor(out=tb[32:64, :], in0=srcbuf[0:32, :], in1=Sb[xi][32:64, :], op=ALU.mult),
                     reads=[srcbuf, Sb[xi]], writes=[tb])
                if post is None:
                    S.op("pool", lambda e: e.tensor_tensor(out=st[si][0:64, :], in0=ta[:], in1=tb[:], op=ALU.add),
                         reads=[ta, tb], writes=[st[si]])
                else:
                    S.op("pool", lambda e: e.tensor_tensor(out=t1[:], in0=ta[:], in1=tb[:], op=ALU.add),
                         reads=[ta, tb], writes=[t1])
                    S.op("dve", lambda e: e.tensor_tensor(out=st[si][0:64, :], in0=t1[:], in1=post[0:64, :], op=ALU.mult),
                         reads=[t1, post], writes=[st[si]])
                S.dma("act", dst_ap, st[si][0:64, :], reads=[st[si]], writes=[dstd])

            def load_blk(b):
                xi = b % 2
                io["xt_load"](b, xT[xi])
                S.dma("sp", Cb[xi][:], Ctab[:, b * 512:(b + 1) * 512], reads=[Cd], writes=[Cb[xi]])
                S.dma("sp", Sb[xi][:], Stab[:, b * 512:(b + 1) * 512], reads=[Sd], writes=[Sb[xi]])

            load_blk(0)
            for b in range(SEQ // 512):
                xi = b % 2
                if b + 1 < SEQ // 512:
                    load_blk(b + 1)
                bs = slice(b * 512, (b + 1) * 512)
                for (coff, cb_, sq_, rr) in ((0, cqb, sq, rq), (512, ckvb, sq2, rk)):
                    for f in range(4):
                        p = pa[nxt("pa", 3)]
                        big_mm(p, lambda c, f=f, coff=coff: wi[:, c, coff + f * 128:coff + (f + 1) * 128], xi)
                        S.op("act", lambda e, p=p, f=f, cb_=cb_: e.activation(out=cb_[:, f, :], in_=p[:], func=AF.Copy),
                             reads=[p], writes=[cb_])
                        S.op("act", lambda e, p=p, f=f, sq_=sq_: e.activation(out=sq_[:, f, :], in_=p[:], func=AF.Square),
                             reads=[p], writes=[sq_])
                    for f in range(4):
                        S.op("pe", lambda e, f=f, sq_=sq_: e.matmul(pb[:], ones[:], sq_[:, f, :], start=(f == 0), stop=(f == 3)),
                             reads=[ones, sq_], writes=[pb], signal=(f == 3))
                    rstd_from(pb, rr, 512)
                for tt in range(4):
                    for f in range(4):
                        S.op("pe", lambda e, f=f, tt=tt: e.matmul(pd[:, tt:tt + 1], sq2[:, f, tt * 128:(tt + 1) * 128], ones[:, 0:1],
                                                                  start=(f == 0), stop=(f == 3)),
                             reads=[ones, sq2], writes=[pd], signal=(f == 3 and tt == 3))
                rstd_from(pd, rtok, 4)
                for h in range(4):
                    p = pa[nxt("pa", 3)]
                    for f in range(4):
                        S.op("pe", lambda e, p=p, f=f, h=h: e.matmul(p[:], wuq[:, f, h * 192:h * 192 + 128], cqb[:, f, :],
                                                                     start=(f == 0), stop=(f == 3)),
                             reads=[wuq, cqb], writes=[p], signal=(f == 3))
                    si = nxt("st", 3)
                    S.op("dve", lambda e, p=p, si=si: e.tensor_tensor(out=st[si][:], in0=p[:], in1=rq[:], op=ALU.mult),
                         reads=[p, rq], writes=[st[si]])
                    S.dma("act", QN[h, :, bs], st[si][:], reads=[st[si]], writes=[QNd])
                    p2 = pc[nxt("pc", 2)]
                    for f in range(4):
                        S.op("pe", lambda e, p2=p2, f=f, h=h: e.matmul(p2[:], wuq[:, f, h * 192 + 128:h * 192 + 192], cqb[:, f, :],
                                                                       start=(f == 0), stop=(f == 3)),
                             reads=[wuq, cqb], writes=[p2], signal=(f == 3))
                    rope64(p2, xi, QP[h, :, bs], QPd, post=rq)
                p2 = pc[nxt("pc", 2)]
                for c in range(16):
                    S.op("pe", lambda e, p2=p2, c=c: e.matmul(p2[:], wi[:, c, 1024:1088], xT[xi][:, c, :], start=(c == 0), stop=(c == 15)),
                         reads=[wi, xT[xi]], writes=[p2], signal=(c == 15))
                rope64(p2, xi, KP[:, bs], KPd)
                for h in range(4):
                    p = pa[nxt("pa", 3)]
                    for f in range(4):
                        S.op("pe", lambda e, p=p, f=f, h=h: e.matmul(p[:], wkk[:, f, h * 128:(h + 1) * 128], ckvb[:, f, :],
                                                                     start=(f == 0), stop=(f == 3)),
                             reads=[wkk, ckvb], writes=[p], signal=(f == 3))
                    si = nxt("st", 3)
                    S.op("dve", lambda e, p=p, si=si: e.tensor_tensor(out=st[si][:], in0=p[:], in1=rk[:], op=ALU.mult),
                         reads=[p, rk], writes=[st[si]])
                    S.dma("act", KN[h, :, bs], st[si][:], reads=[st[si]], writes=[KNd])
                for tt in range(4):
                    p = pa[nxt("pa", 3)]
                    for f in range(4):
                        S.op("pe", lambda e, p=p, f=f, tt=tt: e.matmul(p[:], ckvb[:, f, tt * 128:(tt + 1) * 128], wkv[:, f, :],
                                                                       start=(f == 0), stop=(f == 3)),
                             reads=[wkv, ckvb], writes=[p], signal=(f == 3))
                    si = nxt("st", 3)
                    S.op("dve", lambda e, p=p, si=si, tt=tt: e.tensor_scalar(out=st[si][:], in0=p[:], scalar1=rtok[:, tt:tt + 1], scalar2=None,
                                                                             op0=ALU.mult), reads=[p, rtok], writes=[st[si]])
                    S.dma("act", VV[b * 512 + tt * 128:b * 512 + (tt + 1) * 128, :], st[si][:], reads=[st[si]], writes=[VVd])
                for f in range(4):
                    p = pa[nxt("pa", 3)]
                    big_mm(p, lambda c, f=f: wi[:, c, 1088 + f * 128:1088 + (f + 1) * 128], xi)
                    si = nxt("st", 3)
                    S.op("act", lambda e, p=p, si=si: e.activation(out=st[si][:], in_=p[:], func=AF.Silu), reads=[p], writes=[st[si]])
                    S.dma("act", SZT[f * 128:(f + 1) * 128, bs], st[si][:], reads=[st[si]], writes=[SZd])
        S.barrier()
        with ExitStack() as es:
            S.es = es
            cm = S.sb("cm", [128, 2048], BF16)
            S.dma("sp", cm[:], cmask_in, writes=[cm])
            kn = S.sb("kn", [128, SEQ], BF16)
            kp = S.sb("kp", [64, SEQ], BF16)
            vt = S.sb("vt", [128, 128, 128], BF16)
            S.dma("sp", kp[:], KP, reads=[KPd], writes=[kp])
            qn = [S.sb("qn%d" % i, [128, 512], BF16) for i in range(2)]
            qp = [S.sb("qp%d" % i, [64, 512], BF16) for i in range(2)]
            szb = [S.sb("szb%d" % i, [128, 512], BF16) for i in range(2)]
            pt = [S.sb("pt%d" % i, [128, 512], BF16) for i in range(3)]
            rl = [S.sb("rl%d" % i, [128, 512]) for i in range(2)]
            ot = [S.sb("ot%d" % i, [128, 512]) for i in range(2)]
            gst = [S.sb("gst%d" % i, [128, 512], BF16) for i in range(2)]
            kn_e = [S.sb("kn_e%d" % i, [128, 8192], BF16) for i in range(2)]
            vt_e = [S.sb("vt_e%d" % i, [128, 64, 128], BF16) for i in range(2)]
            ps_s = [S.ps("ps_s%d" % i, [128, 512]) for i in range(3)]
            po = [S.ps("po%d" % i, [128, 512]) for i in range(2)]
            pl = [S.ps("pl%d" % i, [128, 512]) for i in range(2)]
            scale = 192.0 ** -0.5
            sk = [0]
            pre = [False]
            work = [(h, qc) for h in range(4) for qc in range(SEQ // 512)]

            def load_q(idx):
                h, qc = work[idx]
                i = idx % 2
                cs = slice(qc * 512, (qc + 1) * 512)
                S.dma("sp", qn[i][:], QN[h, :, cs], reads=[QNd], writes=[qn[i]])
                S.dma("sp", qp[i][:], QP[h, :, cs], reads=[QPd], writes=[qp[i]])
                S.dma("sp", szb[i][:], SZT[h * 128:(h + 1) * 128, cs], reads=[SZd], writes=[szb[i]])

            def load_early(h):
                S.dma("sp", kn_e[h % 2][:], KN[h, :, 0:8192], reads=[KNd], writes=[kn_e[h % 2]])
                for v2 in range(2):
                    S.dma("sp", vt_e[h % 2][:, v2 * 32:(v2 + 1) * 32, :],
                          VV[v2 * 4096:(v2 + 1) * 4096, h * 128:(h + 1) * 128].rearrange("(t p) d -> p t d", p=128),
                          reads=[VVd], writes=[vt_e[h % 2]])

            for idx, (h, qc) in enumerate(work):
                i = idx % 2
                if qc == 0:
                    if h == 0:
                        load_early(0)
                        load_q(idx)
                    S.dma("sp", kn[:], KN[h], reads=[KNd], writes=[kn])
                    for v4 in range(4):
                        S.dma("sp", vt[:, v4 * 32:(v4 + 1) * 32, :],
                              VV[v4 * 4096:(v4 + 1) * 4096, h * 128:(h + 1) * 128].rearrange("(t p) d -> p t d", p=128),
                              reads=[VVd], writes=[vt])
                    if h + 1 < 4:
                        load_early(h + 1)
                ksrc, vsrc = (kn_e[h % 2], vt_e[h % 2]) if qc < 16 else (kn, vt)
                if idx + 1 < len(work):
                    load_q(idx + 1)
                nk = 4 * (qc + 1)

                def Sm(idx_, kt, base):
                    h_, qc_ = work[idx_]
                    ks = kn_e[h_ % 2] if qc_ < 16 else kn
                    i_ = idx_ % 2
                    si = (base + kt) % 3
                    S.op("pe", lambda e: e.matmul(ps_s[si][:], ks[:, kt * 128:(kt + 1) * 128], qn[i_][:], start=True, stop=False),
                         reads=[ks, qn[i_]], writes=[ps_s[si]], signal=False)
                    S.op("pe", lambda e: e.matmul(ps_s[si][:], kp[:, kt * 128:(kt + 1) * 128], qp[i_][:], start=False, stop=True),
                         reads=[kp, qp[i_], ks, qn[i_]], writes=[ps_s[si]])

                if not pre[0]:
                    Sm(idx, 0, sk[0])
                pre[0] = False
                for kt in range(nk):
                    if kt + 1 < nk:
                        Sm(idx, kt + 1, sk[0])
                    elif idx + 1 < len(work):
                        Sm(idx + 1, 0, sk[0] + nk)
                        pre[0] = True
                    si = (sk[0] + kt) % 3
                    S.op("act", lambda e, si=si: e.activation(out=pt[si][:], in_=ps_s[si][:], func=AF.Exp, scale=scale),
                         reads=[ps_s[si]], writes=[pt[si]])
                    d = kt - 4 * qc
                    if d >= 0:
                        S.op("dve", lambda e, si=si, d=d: e.tensor_tensor(out=pt[si][:], in0=pt[si][:], in1=cm[:, d * 512:(d + 1) * 512], op=ALU.mult),
                             reads=[pt[si], cm], writes=[pt[si]])
                    S.op("pe", lambda e, si=si, kt=kt: e.matmul(po[i][:], vsrc[:, kt, :], pt[si][:], start=(kt == 0), stop=(kt == nk - 1)),
                         reads=[vsrc, pt[si]], writes=[po[i]], signal=False)
                    S.op("pe", lambda e, si=si, kt=kt: e.matmul(pl[i][:], ones[:], pt[si][:], start=(kt == 0), stop=(kt == nk - 1)),
                         reads=[ones, pt[si], vsrc], writes=[pl[i], po[i]])
                sk[0] += nk
                S.op("dve", lambda e: e.reciprocal(out=rl[i][:], in_=pl[i][:]), reads=[pl[i]], writes=[rl[i]])
                S.op("dve", lambda e: e.tensor_tensor(out=ot[i][:], in0=po[i][:], in1=rl[i][:], op=ALU.mult),
                     reads=[po[i], rl[i]], writes=[ot[i]])
                S.op("pool", lambda e: e.tensor_tensor(out=gst[i][:], in0=ot[i][:], in1=szb[i][:], op=ALU.mult),
                     reads=[ot[i], szb[i]], writes=[gst[i]])
                S.dma("act", GTP[(qc // 8) * 512 + h * 128:(qc // 8) * 512 + (h + 1) * 128, (qc % 8) * 512:(qc % 8 + 1) * 512], gst[i][:],
                      reads=[gst[i]], writes=[GTPd])
                if qc == SEQ // 512 - 1:
                    io["gt_head_done"](h)
        S.barrier()


def _inv_freq(dim):
    return np.asarray(1.0 / (10000.0 ** (jnp.arange(0, dim, 2, dtype=jnp.float32) / dim)), dtype=np.float32)


def _dil_consts():
    inv = _inv_freq(128)
    inv128 = np.concatenate([inv, inv]).reshape(128, 1).astype(np.float32)
    sgn = np.concatenate([-np.ones(64), np.ones(64)]).reshape(128, 1).astype(np.float32)
    k = np.arange(128)[:, None]
    q = np.arange(128)[None, :]
    m_prev = (k >= q).astype(np.float32)
    m_cur = (k <= q).astype(np.float32)
    return inv128, sgn, m_prev, m_cur


_PROGS = {}


def build_fused():
    nc = bass.Bass("TRN2", target_bir_lowering=False)

    def I(name, shape, dt=F32):
        return nc.dram_tensor(name, shape, dt, kind="ExternalInput").ap()

    def T(name, shape, dt):
        return nc.dram_tensor(name, shape, dt, kind="Internal").ap()

    x_own = I("x_own", [NTOK, D])
    x_halo = I("x_halo", [2048, D])
    pos_d = I("pos_d", [1, 6144], I32)
    pos_a = I("pos_a", [1, SEQ], I32)
    inv128, sgn128 = I("inv128", [128, 1]), I("sgn128", [128, 1])
    inv64, sgn64 = I("inv64", [64, 1]), I("sgn64", [64, 1])
    masks = I("masks", [128, 512], BF16)
    cmask = I("cmask", [128, 2048], BF16)
    dsa_w_in = [I("dsa_w_in%d" % j, [D, 20480]) for j in range(2)]
    dsa_w_out = [I("dsa_w_out%d" % j, [D, D]) for j in range(2)]
    mla_w_in_c = [I("mla_w_in_c%d" % j, [D, 1600]) for j in range(2)]
    mla_w_uq_c = [I("mla_w_uq_c%d" % j, [512, 768]) for j in range(2)]
    mla_w_kk_c = [I("mla_w_kk_c%d" % j, [512, 512]) for j in range(2)]
    mla_w_kv_c = [I("mla_w_kv_c%d" % j, [512, 512]) for j in range(2)]
    mla_qg = [I("mla_qg%d" % j, [128, 4]) for j in range(2)]
    mla_kg = [I("mla_kg%d" % j, [128, 4]) for j in range(2)]
    mla_w_out = [I("mla_w_out%d" % j, [D, D]) for j in range(2)]
    lng = [I("lng%d" % l, [1, D]) for l in range(DEPTH)]
    lnb = [I("lnb%d" % l, [1, D]) for l in range(DEPTH)]
    out = nc.dram_tensor("out", [NTOK, D], F32, kind="ExternalOutput").ap()
    XTS_t = nc.dram_tensor("XT_send", [D, NTOK], BF16)
    XTALL_t = nc.dram_tensor("XT_allr", [4 * D, NTOK], BF16)
    GTS_t = nc.dram_tensor("GT_send", [2048, NTOK], BF16)
    GTALL_t = nc.dram_tensor("GT_allr", [4 * 2048, NTOK], BF16)
    XTS, XTALL, GTS, GTALL = XTS_t.ap(), XTALL_t.ap(), GTS_t.ap(), GTALL_t.ap()
    XTH0 = T("XT_halo0", [D, 2048], BF16)
    R = [T("R%d" % i, [NTOK, D], F32) for i in range(3)]
    scr = {
        "QT": T("QT", [48, 128, 4096], BF16), "KT": T("KT", [48, 128, 6144], BF16), "VV": T("VV", [3, 6144, D], BF16),
        "ZT": T("ZT", [D, NTOK], BF16), "GT": T("GT", [D, NTOK], BF16),
        "Ctab": T("Ctab", [128, 6144], F32), "Stab": T("Stab", [128, 6144], F32),
        "QN": T("QN", [4, 128, SEQ], BF16), "QP": T("QP", [4, 64, SEQ], BF16), "KN": T("KN", [4, 128, SEQ], BF16),
        "KP": T("KP", [64, SEQ], BF16), "VVm": T("VVm", [SEQ, 512], BF16), "SZT": T("SZT", [512, SEQ], BF16),
        "Ctab64": T("Ctab64", [64, SEQ], F32), "Stab64": T("Stab64", [64, SEQ], F32),
    }
    with ExitStack() as es0:
        S = Sched(nc, es0)
        ident, ones = make_consts(S)
        io0 = dict(scr)
        for k in list(scr):
            io0[k + "_d"] = S.dram(scr[k])
        XTS_d, XTALL_d, GTS_d, GTALL_d, XTH0_d = S.dram(XTS), S.dram(XTALL), S.dram(GTS), S.dram(GTALL), S.dram(XTH0)
        R_d = [S.dram(r_) for r_ in R]
        out_d = S.dram(out)
        ext = Buf(None)
        pid = nc.gpsimd.partition_id()
        rown = (pid % 4) * 2048
        rprev = (pid + 3) % 4
        XTALL4 = XTALL.rearrange("(c s p) t -> c s p t", c=16, s=4)
        with ExitStack() as es_init:
            for (pp_, n_, iv, sg, ck, sk_, P) in ((pos_d, 6144, inv128, sgn128, "Ctab", "Stab", 128),
                                                 (pos_a, SEQ, inv64, sgn64, "Ctab64", "Stab64", 64)):
                S.es = es_init
                inv = S.sb("inv", [P, 1])
                sgn = S.sb("sgn", [P, 1])
                S.dma("sp", inv[:], iv, writes=[inv])
                S.dma("sp", sgn[:], sg, writes=[sgn])
                rope_tables(S, pp_, n_, inv, sgn, io0[ck + "_d"], io0[sk_ + "_d"], P)
            emit_p0(S, ident, x_own, NTOK, XTS, XTS_d)
            emit_p0(S, ident, x_halo, 2048, XTH0, XTH0_d)
        S.barrier()
        XTH0v = XTH0.rearrange("(c p) t -> p c t", p=128)

        def xt_load(b, dst):
            rr, cb = b // 8, b % 8
            S.dma("sp", dst[:], XTALL4[:, rr, :, cb * 512:(cb + 1) * 512].rearrange("c p t -> p c t"),
                  reads=[XTALL_d], writes=[dst])

        GTloc, GTloc_d = scr["GT"], io0["GT_d"]
        GTlv = GTloc.rearrange("(c p) t -> p c t", p=128)

        def gt_gather():
            S.dma("pool", GTloc, GTALL[bass.ds(rown, 2048), :], reads=[GTALL_d], writes=[GTloc_d])

        def gt_head_done(h):
            for tr in range(4):
                k = tr * 4 + h
                S.collective(GTS[k * 128:(k + 1) * 128, :], GTALL[k * 512:(k + 1) * 512, :], GTS_d, GTALL_d)

        def gt_load_loc(t, dst):
            S.dma("sp", dst[:], GTlv[:, :, t * 128:(t + 1) * 128], reads=[GTloc_d], writes=[dst])

        def halo_static(xT):
            S.dma("sp", xT[:], XTH0v, reads=[XTH0_d], writes=[xT])

        def halo_dyn(xT):
            src = XTALL4[:, bass.ds(rprev, 1), :, 2048:4096].rearrange("c o p t -> p (c o) t")
            S.dma("pool", xT[:], src, reads=[XTALL_d], writes=[xT])

        resid_in, resid_in_d = x_own, ext
        for layer in range(DEPTH):
            j = layer // 2
            last = layer == DEPTH - 1
            ro, rod = (out, out_d) if last else (R[layer], R_d[layer])
            if layer % 2 == 0:
                io = dict(io0)
                io.update({"XT_own": XTS, "XT_own_d": XTS_d, "resid": resid_in, "resid_d": resid_in_d,
                           "w_in": dsa_w_in[j], "wout": dsa_w_out[j], "lng": lng[layer], "lnb": lnb[layer], "masks": masks,
                           "ro": ro, "rod": rod, "XTo": XTS, "xtd": XTS_d,
                           "halo_load": halo_static if layer == 0 else halo_dyn})
                emit_dil(S, ident, ones, io)
            else:
                io = dict(io0)
                io.update({"w_in_c": mla_w_in_c[j], "w_uq_c": mla_w_uq_c[j], "w_kk_c": mla_w_kk_c[j], "w_kv_c": mla_w_kv_c[j],
                           "qg": mla_qg[j], "kg": mla_kg[j], "cmask": cmask, "GTP": GTS, "GTP_d": GTS_d, "xt_load": xt_load,
                           "gt_head_done": gt_head_done})
                emit_mla_a(S, ident, ones, io)
                gt_gather()
                with ExitStack() as es:
                    S.es = es
                    emit_epilogue(S, ident, gt_load_loc, resid_in, resid_in_d, mla_w_out[j], lng[layer], lnb[layer],
                                  ro, rod, None if last else XTS, None if last else XTS_d, perm=MLA_PERM)
                S.barrier()
            if not last:
                S.allgather16(XTS_t, XTALL_t, XTS_d, XTALL_d)
            resid_in, resid_in_d = ro, rod
        S.drain([out_d])
    return nc


def kernel(x, positions, dsa_w_in, dsa_w_out, mla_w_in, mla_q_norm, mla_w_uq, mla_kv_norm, mla_w_ukv,
           mla_w_out, ln_g, ln_b):
    x = np.asarray(x)
    positions = np.asarray(positions)
    args = [np.asarray(a) for a in (dsa_w_in, dsa_w_out, mla_w_in, mla_q_norm, mla_w_uq, mla_kv_norm, mla_w_ukv, mla_w_out, ln_g, ln_b)]
    dsa_w_in, dsa_w_out, mla_w_in, mla_q_norm, mla_w_uq, mla_kv_norm, mla_w_ukv, mla_w_out, ln_g, ln_b = args
    if "fused" not in _PROGS:
        _PROGS["fused"] = build_fused()
    nc = _PROGS["fused"]
    bf = ml_dtypes.bfloat16
    inv128, sgn128, m_prev, m_cur = _dil_consts()
    inv = _inv_freq(64)
    inv64 = np.concatenate([inv, inv]).reshape(64, 1).astype(np.float32)
    sgn64 = np.concatenate([-np.ones(32), np.ones(32)]).reshape(64, 1).astype(np.float32)
    kk = np.arange(128)[:, None]
    qq = np.arange(512)[None, :]
    cmask = np.concatenate([((d * 128 + kk) <= qq).astype(np.float32) for d in range(4)], axis=1).astype(bf)
    in_maps = []
    for c in range(NCORES):
        bb, r = c // 4, c % 4
        m = {"x_own": np.ascontiguousarray(x[bb, r * NTOK:(r + 1) * NTOK])}
        if r == 0:
            m["x_halo"] = np.zeros((2048, D), np.float32)
            hpos = np.zeros((2048,), np.int32)
            m_halo = np.zeros_like(m_prev)
        else:
            m["x_halo"] = np.ascontiguousarray(x[bb, r * NTOK - 2048:r * NTOK])
            hpos = positions[bb, r * NTOK - 2048:r * NTOK]
            m_halo = m_prev
        m["masks"] = np.concatenate([m_prev, m_cur, m_halo, m_cur], axis=1).astype(bf)
        m["pos_d"] = np.concatenate([hpos, positions[bb, r * NTOK:(r + 1) * NTOK]]).reshape(1, 6144).astype(np.int32)
        m["pos_a"] = np.ascontiguousarray(positions[bb].reshape(1, SEQ)).astype(np.int32)
        m.update({"inv128": inv128, "sgn128": sgn128, "inv64": inv64, "sgn64": sgn64, "cmask": cmask})
        for j in range(2):
            m["dsa_w_in%d" % j] = dsa_w_in[j]
            m["dsa_w_out%d" % j] = dsa_w_out[j]
            w_in = mla_w_in[j]
            m["mla_w_in_c%d" % j] = np.ascontiguousarray(np.concatenate([w_in[:, 0:1088], w_in[:, 1088 + r * 512:1088 + (r + 1) * 512]], axis=1))
            m["mla_w_uq_c%d" % j] = np.ascontiguousarray(mla_w_uq[j][:, r * 768:(r + 1) * 768])
            wk4 = mla_w_ukv[j].reshape(512, 16, 256)[:, 4 * r:4 * r + 4]
            m["mla_w_kk_c%d" % j] = np.ascontiguousarray(wk4[:, :, 0:128].reshape(512, 512))
            m["mla_w_kv_c%d" % j] = np.ascontiguousarray(wk4[:, :, 128:256].reshape(512, 512))
            m["mla_qg%d" % j] = np.ascontiguousarray(mla_q_norm[j].reshape(4, 128).T).astype(np.float32)
            m["mla_kg%d" % j] = np.ascontiguousarray(mla_kv_norm[j].reshape(4, 128).T).astype(np.float32)
            m["mla_w_out%d" % j] = mla_w_out[j]
        for l in range(DEPTH):
            m["lng%d" % l] = np.ascontiguousarray(ln_g[l].reshape(1, D))
            m["lnb%d" % l] = np.ascontiguousarray(ln_b[l].reshape(1, D))
        in_maps.append(m)
    res = run_bass_kernel_spmd(nc, in_maps, core_ids=list(range(NCORES)))
    out = np.empty((2, SEQ, D), np.float32)
    for c in range(NCORES):
        out[c // 4, (c % 4) * NTOK:(c % 4 + 1) * NTOK] = res.results[c]["out"]
    return out
```

```python
import math
from contextlib import ExitStack

import numpy as np
import ml_dtypes
import jax.numpy as jnp

import concourse.bass as bass
import concourse.mybir as mybir
from concourse.bass_utils import run_bass_kernel_spmd

F32 = mybir.dt.float32
BF16 = mybir.dt.bfloat16
I32 = mybir.dt.int32
AF = mybir.ActivationFunctionType
ALU = mybir.AluOpType

D = 2048
SEQ = 16384
NTOK = 4096
DEPTH = 4
ALPHA = (2 * DEPTH) ** 0.25
LN_EPS = 1e-5
RMS_EPS = 1e-6
DIL = (1, 4, 16)
NCORES = 8
TWO_PI = 2.0 * math.pi
CW1 = 6.28125
CW2 = TWO_PI - CW1
MAGIC = 12582912.0


class Buf:
    __slots__ = ("t", "w", "r")

    def __init__(self, t):
        self.t = t
        self.w = {}
        self.r = {}

    def __getitem__(self, k):
        return self.t[k]


class Sched:
    def __init__(self, nc, es):
        self.nc = nc
        self.es = es
        self.es_sem = es
        self.eng = {"pe": nc.tensor, "act": nc.scalar, "dve": nc.vector, "pool": nc.gpsimd, "sp": nc.sync}
        self.esem = {}
        self.ecnt = {}
        self.known = {e: {} for e in self.eng}
        self.nsem = 0
        for e in ("pe", "act", "dve", "pool"):
            self._roll(e)
        self.dpool = {}
        self.dpos = {}
        for q, n in (("sp", 20), ("act", 6), ("pool", 8)):
            self.dpool[q] = [[self._newsem(), 0] for _ in range(n)]
            self.dpos[q] = 0

    def _newsem(self):
        self.nsem += 1
        return self.es_sem.enter_context(self.nc.semaphore("s%d" % self.nsem))

    def _roll(self, e):
        self.esem[e] = self._newsem()
        self.ecnt[e] = 0

    def sb(self, name, shape, dt=F32):
        self.nsem += 1
        return Buf(self.es.enter_context(self.nc.sbuf_tensor("sb%d_%s" % (self.nsem, name), shape, dt)))

    def ps(self, name, shape, dt=F32):
        self.nsem += 1
        return Buf(self.es.enter_context(self.nc.psum_tensor("ps%d_%s" % (self.nsem, name), shape, dt)))

    def dram(self, t):
        return Buf(t)

    def _waits(self, e, reads, writes):
        deps = {}

        def add(tok):
            if tok is None:
                return
            s, v = tok
            if deps.get(id(s), (None, 0))[1] < v:
                deps[id(s)] = (s, v)

        for b in reads:
            for s, v in b.w.values():
                add((s, v))
        for b in writes:
            for s, v in b.w.values():
                add((s, v))
            for s, v in b.r.values():
                add((s, v))
        kn = self.known[e]
        for sid, (s, v) in deps.items():
            if e == "pe" and s is self.esem["pe"]:
                continue
            if kn.get(sid, 0) >= v:
                continue
            self.eng[e].wait_ge(s, v)
            kn[sid] = v

    def _commit(self, tok, reads, writes):
        s, v = tok
        for b in reads:
            b.r[id(s)] = (s, v)
        for b in writes:
            b.w[id(s)] = tok
            b.r = {}

    def op(self, e, fn, reads=(), writes=(), signal=True):
        self._waits(e, reads, writes)
        inst = fn(self.eng[e])
        if signal:
            if self.ecnt[e] >= 30000:
                self._roll(e)
            self.ecnt[e] += 1
            inst.then_inc(self.esem[e], 1)
            self._commit((self.esem[e], self.ecnt[e]), reads, writes)
        return inst

    def dma(self, q, out, in_, reads=(), writes=(), **kw):
        pool = self.dpool[q]
        i = self.dpos[q]
        self.dpos[q] = (i + 1) % len(pool)
        slot = pool[i]
        if slot[1] >= 1800:
            slot[0] = self._newsem()
            slot[1] = 0
        s = slot[0]
        kn = self.known[q]
        if slot[1] > 0 and kn.get(id(s), 0) < slot[1] * 16:
            self.eng[q].wait_ge(s, slot[1] * 16)
            kn[id(s)] = slot[1] * 16
        self._waits(q, reads, writes)
        slot[1] += 1
        self.eng[q].dma_start(out=out, in_=in_, **kw).then_inc(s, 16)
        self._commit((s, slot[1] * 16), reads, writes)

    def collective(self, send_ap, recv_ap, send_d, recv_d):
        if not hasattr(self, "csem"):
            self.csem = self._newsem()
            self.ccnt = 0
        rd = Buf(None)
        rd.r = recv_d.r
        self._waits("pool", [send_d], [rd])
        self.ccnt += 1
        self.nc.gpsimd.collective_compute(
            "AllGather", ALU.bypass, replica_groups=[[0, 1, 2, 3], [4, 5, 6, 7]],
            ins=[send_ap], outs=[recv_ap]).then_inc(self.csem)
        self._commit((self.csem, self.ccnt), [send_d], [recv_d])

    def allgather16(self, send_t, recv_t, send_d, recv_d):
        sa = send_t.ap()
        ra = recv_t.ap()
        for k in range(16):
            self.collective(sa[k * 128:(k + 1) * 128, :], ra[k * 512:(k + 1) * 512, :], send_d, recv_d)

    def barrier(self):
        toks = []
        for e in ("pe", "act", "dve", "pool"):
            if self.ecnt[e] > 0:
                toks.append((self.esem[e], self.ecnt[e]))
        for q in self.dpool:
            for s_, n in self.dpool[q]:
                if n > 0:
                    toks.append((s_, n * 16))
        if hasattr(self, "csem") and self.ccnt > 0:
            toks.append((self.csem, self.ccnt))
        for e in self.eng:
            kn = self.known[e]
            for s_, v in toks:
                if e == "pe" and s_ is self.esem["pe"]:
                    continue
                if kn.get(id(s_), 0) >= v:
                    continue
                self.eng[e].wait_ge(s_, v)
                kn[id(s_)] = v

    def drain(self, bufs):
        self._waits("sp", bufs, ())


def ap3(t, off, dims):
    return bass.AP(t.tensor if hasattr(t, "tensor") else t, off, dims)


def sb_ap(buf, off, dims):
    base = buf.t[:]
    return bass.AP(base.tensor, base.offset + off, dims)


def make_consts(S):
    nc = S.nc
    ident = S.sb("ident", [128, 128], BF16)
    ones = S.sb("ones", [128, 128], BF16)
    S.op("pool", lambda g: g.memset(ident[:], 1.0), writes=[ident])
    S.op("pool", lambda g: g.affine_select(out=ident[:], in_=ident[:], pattern=[[-1, 128]],
                                           compare_op=ALU.is_equal, fill=0.0, base=0, channel_multiplier=1),
         reads=[ident], writes=[ident])
    S.op("pool", lambda g: g.memset(ones[:], 1.0), writes=[ones])
    return ident, ones


def emit_to_xt(S, src_bf, ident, ptr, xts, XTd, XT_ap_fn, t, q="sp"):
    for c in range(16):
        S.op("pe", lambda e, c=c: e.transpose(ptr[:, c * 128:(c + 1) * 128], src_bf[:, c * 128:(c + 1) * 128], ident[:]),
             reads=[src_bf, ident], writes=[ptr], signal=(c == 15))
    S.op("dve", lambda e: e.tensor_copy(out=xts[:], in_=ptr[:]), reads=[ptr], writes=[xts])
    S.dma(q, XT_ap_fn(t), xts[:].rearrange("p (c t) -> p c t", c=16), reads=[xts], writes=[XTd])


def rope_tables(S, pos_ap, ntok, inv, sgn, Cd, Sd, nparts, chunk=2048):
    P = nparts
    pi_ = S.sb("rt_pi", [P, chunk], I32)
    pf = S.sb("rt_pf", [P, chunk])
    ang = S.sb("rt_ang", [P, chunk])
    k = S.sb("rt_k", [P, chunk])
    r = S.sb("rt_r", [P, chunk])
    o = S.sb("rt_o", [P, chunk])
    for c0 in range(0, ntok, chunk):
        src = bass.AP(pos_ap.tensor, pos_ap.offset + c0, [[0, P], [1, chunk]])
        S.dma("sp", pi_[:], src, writes=[pi_])
        S.op("dve", lambda e: e.tensor_copy(out=pf[:], in_=pi_[:]), reads=[pi_], writes=[pf])
        S.op("dve", lambda e: e.tensor_scalar(out=ang[:], in0=pf[:], scalar1=inv[:, 0:1], scalar2=None, op0=ALU.mult),
             reads=[pf, inv], writes=[ang])
        for which, dst in ((0, Sd), (1, Cd)):
            S.op("dve", lambda e: e.tensor_scalar(out=k[:], in0=ang[:], scalar1=1.0 / TWO_PI, scalar2=MAGIC,
                                                  op0=ALU.mult, op1=ALU.add), reads=[ang], writes=[k])
            S.op("dve", lambda e: e.tensor_scalar(out=k[:], in0=k[:], scalar1=-MAGIC, scalar2=None, op0=ALU.add),
                 reads=[k], writes=[k])
            S.op("dve", lambda e: e.scalar_tensor_tensor(out=r[:], in0=k[:], scalar=-CW1, in1=ang[:], op0=ALU.mult, op1=ALU.add),
                 reads=[k, ang], writes=[r])
            S.op("dve", lambda e: e.scalar_tensor_tensor(out=r[:], in0=k[:], scalar=-CW2, in1=r[:], op0=ALU.mult, op1=ALU.add),
                 reads=[k, r], writes=[r])
            S.op("dve", lambda e: e.tensor_scalar(out=r[:], in0=r[:], scalar1=3.1415925, scalar2=-3.1415925, op0=ALU.min, op1=ALU.max),
                 reads=[r], writes=[r])
            if which == 1:
                S.op("dve", lambda e: e.scalar_tensor_tensor(out=r[:], in0=r[:], scalar=-1.0, in1=r[:], op0=ALU.mult, op1=ALU.max),
                     reads=[r], writes=[r])
                S.op("dve", lambda e: e.tensor_scalar(out=r[:], in0=r[:], scalar1=-1.0, scalar2=math.pi / 2, op0=ALU.mult, op1=ALU.add),
                     reads=[r], writes=[r])
            S.op("act", lambda e: e.activation(out=o[:], in_=r[:], func=AF.Sin), reads=[r], writes=[o])
            if which == 0:
                S.op("dve", lambda e: e.tensor_scalar(out=o[:], in0=o[:], scalar1=sgn[:, 0:1], scalar2=None, op0=ALU.mult),
                     reads=[o, sgn], writes=[o])
            S.dma("sp", dst.t[:, c0:c0 + chunk], o[:], reads=[o], writes=[dst])


def dil_dims(g, m):
    if g == 0:
        return 512 * m, [[1, 512]]
    if g == 1:
        return 512 * m, [[1, 4], [4, 128]]
    return 4 * m, [[1, 4], [16, 128]]


def dil_tile_dims(g, tt):
    if g == 0:
        return 128 * tt, [[1, 128]]
    if g == 1:
        return 512 * (tt // 4) + (tt % 4), [[4, 128]]
    return tt, [[16, 128]]


def emit_p0(S, ident, x, ntok, XT, XTd):
    with ExitStack() as es:
        S.es = es
        xf = [S.sb("xf%d" % i, [128, D]) for i in range(2)]
        xb = [S.sb("xb%d" % i, [128, D], BF16) for i in range(2)]
        ptr = [S.ps("ptr%d" % i, [128, D], BF16) for i in range(2)]
        xts = [S.sb("xts%d" % i, [128, D], BF16) for i in range(2)]
        XTv = XT.rearrange("(c p) t -> p c t", p=128)
        for t in range(ntok // 128):
            i = t % 2
            S.dma("sp", xf[i][:], x[t * 128:(t + 1) * 128, :], writes=[xf[i]])
            S.op("pool", lambda e, i=i: e.tensor_copy(out=xb[i][:], in_=xf[i][:]), reads=[xf[i]], writes=[xb[i]])
            emit_to_xt(S, xb[i], ident, ptr[i], xts[i], XTd, lambda t: XTv[:, :, t * 128:(t + 1) * 128], t)
    S.barrier()


MLA_PERM = [(c % 4) * 4 + c // 4 for c in range(16)]


def emit_epilogue(S, ident, gt_load, resid, resid_d, wout, lng, lnb, resid_out, resid_out_d, XT_out, XT_out_d, perm=None):
    perm = perm or list(range(16))
    nc = S.nc
    wo = S.sb("wo", [128, 16, D], BF16)
    wst = [S.sb("wst%d" % i, [128, 4, 512]) for i in range(2)]
    woutv = wout.rearrange("(c p) n -> p c n", p=128)
    k = 0
    for c4 in range(4):
        for n in range(4):
            i = k % 2
            k += 1
            S.dma("sp", wst[i][:], woutv[:, c4 * 4:(c4 + 1) * 4, n * 512:(n + 1) * 512], writes=[wst[i]])
            S.op("pool", lambda e, i=i, c4=c4, n=n: e.tensor_copy(out=wo[:, c4 * 4:(c4 + 1) * 4, n * 512:(n + 1) * 512], in_=wst[i][:]),
                 reads=[wst[i]], writes=[wo])
    g_b = S.sb("lng_b", [128, D])
    b_b = S.sb("lnb_b", [128, D])
    S.dma("sp", g_b[:], bass.AP(lng.tensor, lng.offset, [[0, 128], [1, D]]), writes=[g_b])
    S.dma("sp", b_b[:], bass.AP(lnb.tensor, lnb.offset, [[0, 128], [1, D]]), writes=[b_b])
    gt = [S.sb("gt%d" % i, [128, 16, 128], BF16) for i in range(2)]
    rs = [S.sb("rs%d" % i, [128, D]) for i in range(2)]
    v = [S.sb("v%d" % i, [128, D]) for i in range(2)]
    xo = [S.sb("xo%d" % i, [128, D]) for i in range(2)]
    xb = [S.sb("exb%d" % i, [128, D], BF16) for i in range(2)]
    st6_ = [S.sb("st6_%d" % i, [128, 4, 6]) for i in range(2)]
    mv_ = [S.sb("mv%d" % i, [128, 2]) for i in range(2)]
    rstd_ = [S.sb("rstd%d" % i, [128, 1]) for i in range(2)]
    py = [S.ps("py%d" % n, [128, 512]) for n in range(4)]
    ptr = S.ps("eptr", [128, D], BF16)
    xts = [S.sb("exts%d" % i, [128, D], BF16) for i in range(2)]
    XTv = XT_out.rearrange("(c p) t -> p c t", p=128) if XT_out is not None else None
    def _loads(t):
        gt_load(t, gt[t % 2])
        S.dma("sp", rs[t % 2][:], resid[t * 128:(t + 1) * 128, :], reads=[resid_d], writes=[rs[t % 2]])

    _loads(0)
    for t in range(NTOK // 128):
        i = t % 2
        if t + 1 < NTOK // 128:
            _loads(t + 1)
        for n in range(4):
            for c in range(16):
                S.op("pe", lambda e, c=c, n=n, i=i: e.matmul(py[n][:], gt[i][:, c, :], wo[:, perm[c], n * 512:(n + 1) * 512],
                                                             start=(c == 0), stop=(c == 15)),
                     reads=[gt[i], wo], writes=[py[n]], signal=(c == 15))
            S.op("dve", lambda e, n=n, i=i: e.scalar_tensor_tensor(out=v[i][:, n * 512:(n + 1) * 512], in0=rs[i][:, n * 512:(n + 1) * 512],
                                                                    scalar=ALPHA, in1=py[n][:], op0=ALU.mult, op1=ALU.add),
                 reads=[rs[i], py[n]], writes=[v[i]])
        st6, mv, rstd = st6_[i], mv_[i], rstd_[i]
        for n in range(4):
            S.op("dve", lambda e, n=n, i=i: e.bn_stats(out=st6[:, n, :], in_=v[i][:, n * 512:(n + 1) * 512]),
                 reads=[v[i]], writes=[st6])
        S.op("dve", lambda e: e.bn_aggr(out=mv[:], in_=st6[:]), reads=[st6], writes=[mv])
        S.op("dve", lambda e: e.tensor_scalar(out=rstd[:], in0=mv[:, 1:2], scalar1=LN_EPS, scalar2=None, op0=ALU.add),
             reads=[mv], writes=[rstd])
        S.op("act", lambda e: e.activation(out=rstd[:], in_=rstd[:], func=AF.Sqrt), reads=[rstd], writes=[rstd])
        S.op("dve", lambda e: e.reciprocal(out=rstd[:], in_=rstd[:]), reads=[rstd], writes=[rstd])
        S.op("dve", lambda e, i=i: e.scalar_tensor_tensor(out=v[i][:], in0=v[i][:], scalar=mv[:, 0:1], in1=g_b[:],
                                                          op0=ALU.subtract, op1=ALU.mult), reads=[v[i], mv, g_b], writes=[v[i]])
        S.op("dve", lambda e, i=i: e.scalar_tensor_tensor(out=xo[i][:], in0=v[i][:], scalar=rstd[:, 0:1], in1=b_b[:],
                                                          op0=ALU.mult, op1=ALU.add), reads=[v[i], rstd, b_b], writes=[xo[i]])
        S.dma("act", resid_out[t * 128:(t + 1) * 128, :], xo[i][:], reads=[xo[i]], writes=[resid_out_d])
        if XT_out is not None:
            S.op("pool", lambda e, i=i: e.tensor_copy(out=xb[i][:], in_=xo[i][:]), reads=[xo[i]], writes=[xb[i]])
            emit_to_xt(S, xb[i], ident, ptr, xts[i], XT_out_d, lambda t: XTv[:, :, t * 128:(t + 1) * 128], t, q="act")


def emit_dil(S, ident, ones, io):
    XTo_in, XTo_in_d = io["XT_own"], io["XT_own_d"]
    resid, resid_d = io["resid"], io["resid_d"]
    w_in, wout, lng, lnb, masks_in = io["w_in"], io["wout"], io["lng"], io["lnb"], io["masks"]
    ro, rod, XTo, xtd = io["ro"], io["rod"], io["XTo"], io["xtd"]
    QT, KT, VV, ZT, GT, Ctab, Stab = (io[k] for k in ("QT", "KT", "VV", "ZT", "GT", "Ctab", "Stab"))
    QTd, KTd, VVd, ZTd, GTd, Cd, Sd = (io[k + "_d"] for k in ("QT", "KT", "VV", "ZT", "GT", "Ctab", "Stab"))
    if True:
        with ExitStack() as es:
            S.es = es
            xT = S.sb("xT", [128, 16, 2048], BF16)
            Cb = S.sb("Cb", [128, 2048])
            Sb = S.sb("Sb", [128, 2048])
            wf = [S.sb("wf%d" % i, [128, 16, 256]) for i in range(3)]
            wb = [S.sb("wb%d" % i, [128, 16, 256], BF16) for i in range(2)]
            qst = [S.sb("qst%d" % i, [128, 2048], BF16) for i in range(2)]
            vst = [S.sb("vst%d" % i, [128, 256], BF16) for i in range(2)]
            t1 = [S.sb("t1_%d" % i, [128, 512]) for i in range(2)]
            t2 = [S.sb("t2_%d" % i, [128, 512]) for i in range(2)]
            pp = [S.ps("pp%d" % i, [128, 512]) for i in range(4)]
            pv = [S.ps("pv%d" % i, [128, 256]) for i in range(2)]
            w_inv = w_in.rearrange("(c p) n -> p c n", p=128)
            ppk = 0
            qk = 0
            vk = 0
            alltiles = []
            for blk in (1, 2, 0):
                for g in range(3):
                    for kind in range(3):
                        if blk == 0 and kind == 0:
                            continue
                        for h0 in range(0, 16, 2):
                            alltiles.append((blk, kind, g, h0))
                if blk > 0:
                    for h0 in range(0, 16, 2):
                        alltiles.append((blk, 3, 0, h0))

            def w_load(n):
                (_, kind, g, h0) = alltiles[n]
                col0 = 18432 + h0 * 128 if kind == 3 else ((g * 3 + kind) * 16 + h0) * 128
                S.dma("sp", wf[n % 3][:], w_inv[:, :, col0:col0 + 256], writes=[wf[n % 3]])

            def w_cast(n):
                S.op("act", lambda e: e.activation(out=wb[n % 2][:], in_=wf[n % 3][:], func=AF.Copy),
                     reads=[wf[n % 3]], writes=[wb[n % 2]])

            w_load(0)
            w_load(1)
            w_cast(0)
            cur_blk = -1
            for n, (blk, kind, g, h0) in enumerate(alltiles):
                if blk != cur_blk:
                    cur_blk = blk
                    if blk == 0:
                        io["halo_load"](xT)
                    else:
                        S.dma("sp", xT[:], XTo_in[:, (blk - 1) * 2048:blk * 2048].rearrange("(c p) t -> p c t", p=128),
                              reads=[XTo_in_d], writes=[xT])
                    S.dma("sp", Cb[:], Ctab[:, blk * 2048:(blk + 1) * 2048], reads=[Cd], writes=[Cb])
                    S.dma("sp", Sb[:], Stab[:, blk * 2048:(blk + 1) * 2048], reads=[Sd], writes=[Sb])
                if n + 2 < len(alltiles):
                    w_load(n + 2)
                if n + 1 < len(alltiles):
                    w_cast(n + 1)
                wi = n % 2
                if True:
                    if blk == 0:
                        ms = [3] if g < 2 else [0, 1, 2, 3]
                        tts = [15] if g == 0 else ([12, 13, 14, 15] if g == 1 else list(range(16)))
                    else:
                        ms = [0, 1, 2, 3]
                        tts = list(range(16))
                    if kind in (0, 1, 3):
                        for hh in range(2):
                            h = h0 + hh
                            qi = qk % 2
                            qk += 1
                            for m in ms:
                                p = pp[ppk % 4]
                                ppk += 1
                                for c in range(16):
                                    rhs = xT[:, c, m * 512:(m + 1) * 512]
                                    S.op("pe", lambda e, p=p, wi=wi, c=c, hh=hh, rhs=rhs: e.matmul(
                                        p[:], wb[wi][:, c, hh * 128:(hh + 1) * 128], rhs, start=(c == 0), stop=(c == 15)),
                                        reads=[wb[wi], xT], writes=[p], signal=(c == 15))
                                if kind == 3:
                                    dst = qst[qi][:, m * 512:(m + 1) * 512]
                                    S.op("act", lambda e, p=p, dst=dst: e.activation(out=dst, in_=p[:], func=AF.Copy),
                                         reads=[p], writes=[qst[qi]])
                                else:
                                    ti = ppk % 2
                                    ms_ = slice(m * 512, (m + 1) * 512)
                                    S.op("dve", lambda e, p=p, ti=ti, ms_=ms_: e.tensor_tensor(out=t1[ti][:], in0=p[:], in1=Cb[:, ms_], op=ALU.mult),
                                         reads=[p, Cb], writes=[t1[ti]])
                                    S.op("dve", lambda e, p=p, ti=ti, ms_=ms_: e.tensor_tensor(out=t2[ti][0:64, :], in0=p[64:128, :], in1=Sb[0:64, ms_], op=ALU.mult),
                                         reads=[p, Sb], writes=[t2[ti]])
                                    S.op("dve", lambda e, p=p, ti=ti, ms_=ms_: e.tensor_tensor(out=t2[ti][64:128, :], in0=p[0:64, :], in1=Sb[64:128, ms_], op=ALU.mult),
                                         reads=[p, Sb], writes=[t2[ti]])
                                    if g == 0:
                                        dst = qst[qi][:, ms_]
                                        a0, a1 = t1[ti][:], t2[ti][:]
                                    elif g == 1:
                                        dst = sb_ap(qst[qi], 512 * m, [[2048, 128], [1, 128], [128, 4]])
                                        a0 = sb_ap(t1[ti], 0, [[512, 128], [4, 128], [1, 4]])
                                        a1 = sb_ap(t2[ti], 0, [[512, 128], [4, 128], [1, 4]])
                                    else:
                                        dst = sb_ap(qst[qi], 32 * m, [[2048, 128], [1, 32], [128, 16]])
                                        a0 = sb_ap(t1[ti], 0, [[512, 128], [16, 32], [1, 16]])
                                        a1 = sb_ap(t2[ti], 0, [[512, 128], [16, 32], [1, 16]])
                                    S.op("pool", lambda e, dst=dst, a0=a0, a1=a1: e.tensor_tensor(out=dst, in0=a0, in1=a1, op=ALU.add),
                                         reads=[t1[ti], t2[ti]], writes=[qst[qi]])
                            c_lo, c_hi = ms[0] * 512, (ms[-1] + 1) * 512
                            if kind == 0:
                                S.dma("pool", QT[g * 16 + h, :, (blk - 1) * 2048 + c_lo:(blk - 1) * 2048 + c_hi], qst[qi][:, c_lo:c_hi],
                                      reads=[qst[qi]], writes=[QTd])
                            elif kind == 1:
                                S.dma("pool", KT[g * 16 + h, :, blk * 2048 + c_lo:blk * 2048 + c_hi], qst[qi][:, c_lo:c_hi],
                                      reads=[qst[qi]], writes=[KTd])
                            else:
                                S.dma("pool", ZT[h * 128:(h + 1) * 128, (blk - 1) * 2048:blk * 2048], qst[qi][:],
                                      reads=[qst[qi]], writes=[ZTd])
                    else:
                        for tt in tts:
                            off, dims = dil_tile_dims(g, tt)
                            p = pv[vk % 2]
                            vi = vk % 2
                            vk += 1
                            for c in range(16):
                                lhsT = sb_ap(xT, c * 2048 + off, [[16 * 2048, 128]] + dims)
                                S.op("pe", lambda e, p=p, wi=wi, c=c, lhsT=lhsT: e.matmul(
                                    p[:], lhsT, wb[wi][:, c, :], start=(c == 0), stop=(c == 15)),
                                    reads=[wb[wi], xT], writes=[p], signal=(c == 15))
                            S.op("act", lambda e, p=p, vi=vi: e.activation(out=vst[vi][:], in_=p[:], func=AF.Copy),
                                 reads=[p], writes=[vst[vi]])
                            S.dma("act", VV[g, blk * 2048 + tt * 128:blk * 2048 + (tt + 1) * 128, h0 * 128:h0 * 128 + 256], vst[vi][:],
                                  reads=[vst[vi]], writes=[VVd])
        S.barrier()
        with ExitStack() as es:
            S.es = es
            msk = S.sb("msk", [128, 512], BF16)
            S.dma("sp", msk[:], masks_in, writes=[msk])
            accO = S.sb("accO", [128, NTOK])
            accL = S.sb("accL", [128, NTOK])
            qt = [S.sb("qt%d" % i, [128, 4096], BF16) for i in range(2)]
            kt = [S.sb("kt%d" % i, [128, 6144], BF16) for i in range(2)]
            vt = [S.sb("vt%d" % i, [128, 48, 128], BF16) for i in range(2)]
            zt = S.sb("zt", [128, NTOK], BF16)
            gst = S.sb("gst", [128, NTOK], BF16)
            pt = [S.sb("pt%d" % i, [128, 256], BF16) for i in range(4)]
            rl = [S.sb("rl%d" % i, [128, 512]) for i in range(2)]
            ot = [S.sb("ot%d" % i, [128, 512]) for i in range(2)]
            sz = [S.sb("sz%d" % i, [128, 512]) for i in range(2)]
            ps_s = [S.ps("ps_s%d" % i, [128, 256]) for i in range(4)]
            po = [S.ps("po%d" % i, [128, 512]) for i in range(2)]
            pl = [S.ps("pl%d" % i, [128, 512]) for i in range(2)]
            scale = 128.0 ** -0.5
            hg = [(h, g) for h in range(16) for g in range(3)]

            def load_hg(idx):
                h, g = hg[idx]
                bi = idx % 2
                S.dma("sp", qt[bi][:], QT[g * 16 + h], reads=[QTd], writes=[qt[bi]])
                S.dma("sp", kt[bi][:], KT[g * 16 + h], reads=[KTd], writes=[kt[bi]])
                S.dma("sp", vt[bi][:], VV[g, :, h * 128:(h + 1) * 128].rearrange("(t p) d -> p t d", p=128),
                      reads=[VVd], writes=[vt[bi]])

            load_hg(0)
            sk = 0
            ok = 0
            for h in range(16):
                S.dma("sp", zt[:], ZT[h * 128:(h + 1) * 128, :], reads=[ZTd], writes=[zt])
                for g in range(3):
                    idx = h * 3 + g
                    bi = idx % 2
                    if idx + 1 < len(hg):
                        load_hg(idx + 1)
                    Pg = 128 * DIL[g]
                    qbs = []
                    for sbk in range(2):
                        for m in range(4):
                            for qb in range(4):
                                col0 = sbk * 2048 + m * 512 + qb * 128
                                qbs.append((sbk, m, qb, col0, 2048 + col0, 2048 + col0 - Pg))

                    def emitS(i):
                        (sbk, m, qb, col0, kc, prev) = qbs[i]
                        si = (sk + i) % 4
                        S.op("pe", lambda e: e.matmul(ps_s[si][:, 0:128], kt[bi][:, prev:prev + 128], qt[bi][:, col0:col0 + 128],
                                                      start=True, stop=True),
                             reads=[kt[bi], qt[bi]], writes=[ps_s[si]], signal=False)
                        S.op("pe", lambda e: e.matmul(ps_s[si][:, 128:256], kt[bi][:, kc:kc + 128], qt[bi][:, col0:col0 + 128],
                                                      start=True, stop=True),
                             reads=[kt[bi], qt[bi]], writes=[ps_s[si]])

                    emitS(0)
                    for i, (sbk, m, qb, col0, kc, prev) in enumerate(qbs):
                        if i + 1 < len(qbs):
                            emitS(i + 1)
                        si = (sk + i) % 4
                        if qb == 0:
                            oi = ok % 2
                            ok += 1
                        S.op("act", lambda e, si=si: e.activation(out=pt[si][:], in_=ps_s[si][:], func=AF.Exp, scale=scale),
                             reads=[ps_s[si]], writes=[pt[si]])
                        moff = 256 if prev < 2048 else 0
                        S.op("dve", lambda e, si=si, moff=moff: e.tensor_tensor(out=pt[si][:], in0=pt[si][:], in1=msk[:, moff:moff + 256], op=ALU.mult),
                             reads=[pt[si], msk], writes=[pt[si]])
                        oc = slice(qb * 128, (qb + 1) * 128)
                        S.op("pe", lambda e, oi=oi, si=si, prev=prev, oc=oc: e.matmul(
                            po[oi][:, oc], vt[bi][:, prev // 128, :], pt[si][:, 0:128], start=True, stop=False),
                            reads=[vt[bi], pt[si]], writes=[po[oi]], signal=False)
                        S.op("pe", lambda e, oi=oi, si=si, kc=kc, oc=oc: e.matmul(
                            po[oi][:, oc], vt[bi][:, kc // 128, :], pt[si][:, 128:256], start=False, stop=True),
                            reads=[vt[bi], pt[si]], writes=[po[oi]], signal=False)
                        S.op("pe", lambda e, oi=oi, si=si, oc=oc: e.matmul(
                            pl[oi][:, oc], ones[:], pt[si][:, 0:128], start=True, stop=False),
                            reads=[ones, pt[si]], writes=[pl[oi]], signal=False)
                        S.op("pe", lambda e, oi=oi, si=si, oc=oc: e.matmul(
                            pl[oi][:, oc], ones[:], pt[si][:, 128:256], start=False, stop=True),
                            reads=[ones, pt[si], vt[bi]], writes=[pl[oi], po[oi]])
                        if qb == 3:
                            off, dims = dil_dims(g, m)
                            dO = sb_ap(accO, sbk * 2048 + off, [[NTOK, 128]] + dims)
                            dL = sb_ap(accL, sbk * 2048 + off, [[NTOK, 128]] + dims)
                            if g == 0:
                                S.op("dve", lambda e, oi=oi, dO=dO: e.tensor_copy(out=dO, in_=po[oi][:]), reads=[po[oi]], writes=[accO])
                                S.op("act", lambda e, oi=oi, dL=dL: e.activation(out=dL, in_=pl[oi][:], func=AF.Copy), reads=[pl[oi]], writes=[accL])
                            else:
                                S.op("dve", lambda e, oi=oi, dO=dO: e.tensor_tensor(out=dO, in0=po[oi][:], in1=dO, op=ALU.add),
                                     reads=[po[oi], accO], writes=[accO])
                                S.op("dve", lambda e, oi=oi, dL=dL: e.tensor_tensor(out=dL, in0=pl[oi][:], in1=dL, op=ALU.add),
                                     reads=[pl[oi], accL], writes=[accL])
                    sk += len(qbs)
                for c8 in range(8):
                    i = c8 % 2
                    cs = slice(c8 * 512, (c8 + 1) * 512)
                    S.op("dve", lambda e, i=i, cs=cs: e.reciprocal(out=rl[i][:], in_=accL[:, cs]), reads=[accL], writes=[rl[i]])
                    S.op("pool", lambda e, i=i, cs=cs: e.tensor_tensor(out=ot[i][:], in0=accO[:, cs], in1=rl[i][:], op=ALU.mult),
                         reads=[accO, rl[i]], writes=[ot[i]])
                    S.op("act", lambda e, i=i, cs=cs: e.activation(out=sz[i][:], in_=zt[:, cs], func=AF.Silu), reads=[zt], writes=[sz[i]])
                    S.op("pool", lambda e, i=i, cs=cs: e.tensor_tensor(out=gst[:, cs], in0=ot[i][:], in1=sz[i][:], op=ALU.mult),
                         reads=[ot[i], sz[i]], writes=[gst])
                S.dma("act", GT[h * 128:(h + 1) * 128, :], gst[:], reads=[gst], writes=[GTd])
        S.barrier()
        with ExitStack() as es:
            S.es = es
            GTv = GT.rearrange("(c p) t -> p c t", p=128)
            emit_epilogue(S, ident, lambda t, dst: S.dma("sp", dst[:], GTv[:, :, t * 128:(t + 1) * 128], reads=[GTd], writes=[dst]),
                          resid, resid_d, wout, lng, lnb, ro, rod, XTo, xtd)
        S.barrier()


def emit_mla_a(S, ident, ones, io):
    w_in_c, w_uq_c, w_kk_c, w_kv_c, qg_in, kg_in, cmask_in = (io[k] for k in ("w_in_c", "w_uq_c", "w_kk_c", "w_kv_c", "qg", "kg", "cmask"))
    GTP, GTPd = io["GTP"], io["GTP_d"]
    QN, QP, KN, KP, VV, SZT, Ctab, Stab = (io[k] for k in ("QN", "QP", "KN", "KP", "VVm", "SZT", "Ctab64", "Stab64"))
    QNd, QPd, KNd, KPd, VVd, SZd, Cd, Sd = (io[k + "_d"] for k in ("QN", "QP", "KN", "KP", "VVm", "SZT", "Ctab64", "Stab64"))
    if True:
        with ExitStack() as es:
            S.es = es
            wi = S.sb("wi", [128, 16, 1600], BF16)
            wuq = S.sb("wuq", [128, 4, 768], BF16)
            wkk = S.sb("wkk", [128, 4, 512], BF16)
            wkv = S.sb("wkv", [128, 4, 512], BF16)
            qg = S.sb("qg", [128, 4])
            kg = S.sb("kg", [128, 4])
            wst = [S.sb("wst%d" % i, [128, 4, 800]) for i in range(2)]
            S.dma("sp", qg[:], qg_in, writes=[qg])
            S.dma("sp", kg[:], kg_in, writes=[kg])
            w_inv = w_in_c.rearrange("(c p) n -> p c n", p=128)
            k = 0
            for c4 in range(4):
                for half in range(2):
                    i = k % 2
                    k += 1
                    S.dma("sp", wst[i][:], w_inv[:, c4 * 4:(c4 + 1) * 4, half * 800:(half + 1) * 800], writes=[wst[i]])
                    S.op("pool", lambda e, i=i, c4=c4, half=half: e.tensor_copy(
                        out=wi[:, c4 * 4:(c4 + 1) * 4, half * 800:(half + 1) * 800], in_=wst[i][:]), reads=[wst[i]], writes=[wi])
            for (src, dstw, gain, ncol) in ((w_uq_c, wuq, qg, 768), (w_kk_c, wkk, kg, 512), (w_kv_c, wkv, kg, 512)):
                i = k % 2
                k += 1
                S.dma("sp", wst[i][:, :, 0:ncol], src.rearrange("(c p) n -> p c n", p=128), writes=[wst[i]])
                for c in range(4):
                    S.op("dve", lambda e, i=i, c=c, dstw=dstw, gain=gain, ncol=ncol: e.tensor_scalar(
                        out=dstw[:, c, :], in0=wst[i][:, c, 0:ncol], scalar1=gain[:, c:c + 1], scalar2=None, op0=ALU.mult),
                        reads=[wst[i], gain], writes=[dstw])
            xT = [S.sb("xT%d" % i, [128, 16, 512], BF16) for i in range(2)]
            Cb = [S.sb("Cb%d" % i, [64, 512]) for i in range(2)]
            Sb = [S.sb("Sb%d" % i, [64, 512]) for i in range(2)]
            cqb = S.sb("cqb", [128, 4, 512], BF16)
            sq = S.sb("sq", [128, 4, 512], BF16)
            ckvb = S.sb("ckvb", [128, 4, 512], BF16)
            sq2 = S.sb("sq2", [128, 4, 512], BF16)
            rq = S.sb("rq", [128, 512])
            rk = S.sb("rk", [128, 512])
            rtok = S.sb("rtok", [128, 4])
            st = [S.sb("st%d" % i, [128, 512], BF16) for i in range(3)]
            t1 = S.sb("t1", [64, 512])
            ta = S.sb("ta", [64, 512])
            tb = S.sb("tb", [64, 512])
            pa = [S.ps("pa%d" % i, [128, 512]) for i in range(3)]
            pb = S.ps("pb", [128, 512])
            pc = [S.ps("pc%d" % i, [64, 512]) for i in range(2)]
            pd = S.ps("pd", [128, 4])
            cnt = {"pa": 0, "st": 0, "pc": 0}

            def nxt(key, n):
                cnt[key] += 1
                return (cnt[key] - 1) % n

            def big_mm(p, lhs_fn, xi):
                for c in range(16):
                    S.op("pe", lambda e, c=c: e.matmul(p[:], lhs_fn(c), xT[xi][:, c, :], start=(c == 0), stop=(c == 15)),
                         reads=[wi, xT[xi]], writes=[p], signal=(c == 15))

            def rstd_from(ps_buf, dst, width):
                S.op("dve", lambda e: e.tensor_scalar(out=dst[:, 0:width], in0=ps_buf[:, 0:width], scalar1=1.0 / 512, scalar2=RMS_EPS,
                                                      op0=ALU.mult, op1=ALU.add), reads=[ps_buf], writes=[dst])
                S.op("act", lambda e: e.activation(out=dst[:, 0:width], in_=dst[:, 0:width], func=AF.Sqrt), reads=[dst], writes=[dst])
                S.op("dve", lambda e: e.reciprocal(out=dst[:, 0:width], in_=dst[:, 0:width]), reads=[dst], writes=[dst])

            def rope64(srcbuf, xi, dst_ap, dstd, post=None):
                si = nxt("st", 3)
                S.op("dve", lambda e: e.tensor_tensor(out=ta[:], in0=srcbuf[0:64, :], in1=Cb[xi][:], op=ALU.mult),
                     reads=[srcbuf, Cb[xi]], writes=[ta])
                S.op("dve", lambda e: e.tensor_tensor(out=tb[0:32, :], in0=srcbuf[32:64, :], in1=Sb[xi][0:32, :], op=ALU.mult),
                     reads=[srcbuf, Sb[xi]], writes=[tb])
                S.op("dve", lambda e: e.tensor_tensor(out=tb[32:64, :], in0=srcbuf[0:32, :], in1=Sb[xi][32:64, :], op=ALU.mult),
                     reads=[srcbuf, Sb[xi]], writes=[tb])
                if post is None:
                    S.op("pool", lambda e: e.tensor_tensor(out=st[si][0:64, :], in0=ta[:], in1=tb[:], op=ALU.add),
                         reads=[ta, tb], writes=[st[si]])
                else:
                    S.op("pool", lambda e: e.tensor_tensor(out=t1[:], in0=ta[:], in1=tb[:], op=ALU.add),
                         reads=[ta, tb], writes=[t1])
                    S.op("dve", lambda e: e.tensor_tensor(out=st[si][0:64, :], in0=t1[:], in1=post[0:64, :], op=ALU.mult),
                         reads=[t1, post], writes=[st[si]])
                S.dma("act", dst_ap, st[si][0:64, :], reads=[st[si]], writes=[dstd])

            def load_blk(b):
                xi = b % 2
                io["xt_load"](b, xT[xi])
                S.dma("sp", Cb[xi][:], Ctab[:, b * 512:(b + 1) * 512], reads=[Cd], writes=[Cb[xi]])
                S.dma("sp", Sb[xi][:], Stab[:, b * 512:(b + 1) * 512], reads=[Sd], writes=[Sb[xi]])

            load_blk(0)
            for b in range(SEQ // 512):
                xi = b % 2
                if b + 1 < SEQ // 512:
                    load_blk(b + 1)
                bs = slice(b * 512, (b + 1) * 512)
                for (coff, cb_, sq_, rr) in ((0, cqb, sq, rq), (512, ckvb, sq2, rk)):
                    for f in range(4):
                        p = pa[nxt("pa", 3)]
                        big_mm(p, lambda c, f=f, coff=coff: wi[:, c, coff + f * 128:coff + (f + 1) * 128], xi)
                        S.op("act", lambda e, p=p, f=f, cb_=cb_: e.activation(out=cb_[:, f, :], in_=p[:], func=AF.Copy),
                             reads=[p], writes=[cb_])
                        S.op("act", lambda e, p=p, f=f, sq_=sq_: e.activation(out=sq_[:, f, :], in_=p[:], func=AF.Square),
                             reads=[p], writes=[sq_])
                    for f in range(4):
                        S.op("pe", lambda e, f=f, sq_=sq_: e.matmul(pb[:], ones[:], sq_[:, f, :], start=(f == 0), stop=(f == 3)),
                             reads=[ones, sq_], writes=[pb], signal=(f == 3))
                    rstd_from(pb, rr, 512)
                for tt in range(4):
                    for f in range(4):
                        S.op("pe", lambda e, f=f, tt=tt: e.matmul(pd[:, tt:tt + 1], sq2[:, f, tt * 128:(tt + 1) * 128], ones[:, 0:1],
                                                                  start=(f == 0), stop=(f == 3)),
                             reads=[ones, sq2], writes=[pd], signal=(f == 3 and tt == 3))
                rstd_from(pd, rtok, 4)
                for h in range(4):
                    p = pa[nxt("pa", 3)]
                    for f in range(4):
                        S.op("pe", lambda e, p=p, f=f, h=h: e.matmul(p[:], wuq[:, f, h * 192:h * 192 + 128], cqb[:, f, :],
                                                                     start=(f == 0), stop=(f == 3)),
                             reads=[wuq, cqb], writes=[p], signal=(f == 3))
                    si = nxt("st", 3)
                    S.op("dve", lambda e, p=p, si=si: e.tensor_tensor(out=st[si][:], in0=p[:], in1=rq[:], op=ALU.mult),
                         reads=[p, rq], writes=[st[si]])
                    S.dma("act", QN[h, :, bs], st[si][:], reads=[st[si]], writes=[QNd])
                    p2 = pc[nxt("pc", 2)]
                    for f in range(4):
                        S.op("pe", lambda e, p2=p2, f=f, h=h: e.matmul(p2[:], wuq[:, f, h * 192 + 128:h * 192 + 192], cqb[:, f, :],
                                                                       start=(f == 0), stop=(f == 3)),
                             reads=[wuq, cqb], writes=[p2], signal=(f == 3))
                    rope64(p2, xi, QP[h, :, bs], QPd, post=rq)
                p2 = pc[nxt("pc", 2)]
                for c in range(16):
                    S.op("pe", lambda e, p2=p2, c=c: e.matmul(p2[:], wi[:, c, 1024:1088], xT[xi][:, c, :], start=(c == 0), stop=(c == 15)),
                         reads=[wi, xT[xi]], writes=[p2], signal=(c == 15))
                rope64(p2, xi, KP[:, bs], KPd)
                for h in range(4):
                    p = pa[nxt("pa", 3)]
                    for f in range(4):
                        S.op("pe", lambda e, p=p, f=f, h=h: e.matmul(p[:], wkk[:, f, h * 128:(h + 1) * 128], ckvb[:, f, :],
                                                                     start=(f == 0), stop=(f == 3)),
                             reads=[wkk, ckvb], writes=[p], signal=(f == 3))
                    si = nxt("st", 3)
                    S.op("dve", lambda e, p=p, si=si: e.tensor_tensor(out=st[si][:], in0=p[:], in1=rk[:], op=ALU.mult),
                         reads=[p, rk], writes=[st[si]])
                    S.dma("act", KN[h, :, bs], st[si][:], reads=[st[si]], writes=[KNd])
                for tt in range(4):
                    p = pa[nxt("pa", 3)]
                    for f in range(4):
                        S.op("pe", lambda e, p=p, f=f, tt=tt: e.matmul(p[:], ckvb[:, f, tt * 128:(tt + 1) * 128], wkv[:, f, :],
                                                                       start=(f == 0), stop=(f == 3)),
                             reads=[wkv, ckvb], writes=[p], signal=(f == 3))
                    si = nxt("st", 3)
                    S.op("dve", lambda e, p=p, si=si, tt=tt: e.tensor_scalar(out=st[si][:], in0=p[:], scalar1=rtok[:, tt:tt + 1], scalar2=None,
                                                                             op0=ALU.mult), reads=[p, rtok], writes=[st[si]])
                    S.dma("act", VV[b * 512 + tt * 128:b * 512 + (tt + 1) * 128, :], st[si][:], reads=[st[si]], writes=[VVd])
                for f in range(4):
                    p = pa[nxt("pa", 3)]
                    big_mm(p, lambda c, f=f: wi[:, c, 1088 + f * 128:1088 + (f + 1) * 128], xi)
                    si = nxt("st", 3)
                    S.op("act", lambda e, p=p, si=si: e.activation(out=st[si][:], in_=p[:], func=AF.Silu), reads=[p], writes=[st[si]])
                    S.dma("act", SZT[f * 128:(f + 1) * 128, bs], st[si][:], reads=[st[si]], writes=[SZd])
        S.barrier()
        with ExitStack() as es:
            S.es = es
            cm = S.sb("cm", [128, 2048], BF16)
            S.dma("sp", cm[:], cmask_in, writes=[cm])
            kn = S.sb("kn", [128, SEQ], BF16)
            kp = S.sb("kp", [64, SEQ], BF16)
            vt = S.sb("vt", [128, 128, 128], BF16)
            S.dma("sp", kp[:], KP, reads=[KPd], writes=[kp])
            qn = [S.sb("qn%d" % i, [128, 512], BF16) for i in range(2)]
            qp = [S.sb("qp%d" % i, [64, 512], BF16) for i in range(2)]
            szb = [S.sb("szb%d" % i, [128, 512], BF16) for i in range(2)]
            pt = [S.sb("pt%d" % i, [128, 512], BF16) for i in range(3)]
            rl = [S.sb("rl%d" % i, [128, 512]) for i in range(2)]
            ot = [S.sb("ot%d" % i, [128, 512]) for i in range(2)]
            gst = [S.sb("gst%d" % i, [128, 512], BF16) for i in range(2)]
            kn_e = [S.sb("kn_e%d" % i, [128, 8192], BF16) for i in range(2)]
            vt_e = [S.sb("vt_e%d" % i, [128, 64, 128], BF16) for i in range(2)]
            ps_s = [S.ps("ps_s%d" % i, [128, 512]) for i in range(3)]
            po = [S.ps("po%d" % i, [128, 512]) for i in range(2)]
            pl = [S.ps("pl%d" % i, [128, 512]) for i in range(2)]
            scale = 192.0 ** -0.5
            sk = [0]
            pre = [False]
            work = [(h, qc) for h in range(4) for qc in range(SEQ // 512)]

            def load_q(idx):
                h, qc = work[idx]
                i = idx % 2
                cs = slice(qc * 512, (qc + 1) * 512)
                S.dma("sp", qn[i][:], QN[h, :, cs], reads=[QNd], writes=[qn[i]])
                S.dma("sp", qp[i][:], QP[h, :, cs], reads=[QPd], writes=[qp[i]])
                S.dma("sp", szb[i][:], SZT[h * 128:(h + 1) * 128, cs], reads=[SZd], writes=[szb[i]])

            def load_early(h):
                S.dma("sp", kn_e[h % 2][:], KN[h, :, 0:8192], reads=[KNd], writes=[kn_e[h % 2]])
                for v2 in range(2):
                    S.dma("sp", vt_e[h % 2][:, v2 * 32:(v2 + 1) * 32, :],
                          VV[v2 * 4096:(v2 + 1) * 4096, h * 128:(h + 1) * 128].rearrange("(t p) d -> p t d", p=128),
                          reads=[VVd], writes=[vt_e[h % 2]])

            for idx, (h, qc) in enumerate(work):
                i = idx % 2
                if qc == 0:
                    if h == 0:
                        load_early(0)
                        load_q(idx)
                    S.dma("sp", kn[:], KN[h], reads=[KNd], writes=[kn])
                    for v4 in range(4):
                        S.dma("sp", vt[:, v4 * 32:(v4 + 1) * 32, :],
                              VV[v4 * 4096:(v4 + 1) * 4096, h * 128:(h + 1) * 128].rearrange("(t p) d -> p t d", p=128),
                              reads=[VVd], writes=[vt])
                    if h + 1 < 4:
                        load_early(h + 1)
                ksrc, vsrc = (kn_e[h % 2], vt_e[h % 2]) if qc < 16 else (kn, vt)
                if idx + 1 < len(work):
                    load_q(idx + 1)
                nk = 4 * (qc + 1)

                def Sm(idx_, kt, base):
                    h_, qc_ = work[idx_]
                    ks = kn_e[h_ % 2] if qc_ < 16 else kn
                    i_ = idx_ % 2
                    si = (base + kt) % 3
                    S.op("pe", lambda e: e.matmul(ps_s[si][:], ks[:, kt * 128:(kt + 1) * 128], qn[i_][:], start=True, stop=False),
                         reads=[ks, qn[i_]], writes=[ps_s[si]], signal=False)
                    S.op("pe", lambda e: e.matmul(ps_s[si][:], kp[:, kt * 128:(kt + 1) * 128], qp[i_][:], start=False, stop=True),
                         reads=[kp, qp[i_], ks, qn[i_]], writes=[ps_s[si]])

                if not pre[0]:
                    Sm(idx, 0, sk[0])
                pre[0] = False
                for kt in range(nk):
                    if kt + 1 < nk:
                        Sm(idx, kt + 1, sk[0])
                    elif idx + 1 < len(work):
                        Sm(idx + 1, 0, sk[0] + nk)
                        pre[0] = True
                    si = (sk[0] + kt) % 3
                    S.op("act", lambda e, si=si: e.activation(out=pt[si][:], in_=ps_s[si][:], func=AF.Exp, scale=scale),
                         reads=[ps_s[si]], writes=[pt[si]])
                    d = kt - 4 * qc
                    if d >= 0:
                        S.op("dve", lambda e, si=si, d=d: e.tensor_tensor(out=pt[si][:], in0=pt[si][:], in1=cm[:, d * 512:(d + 1) * 512], op=ALU.mult),
                             reads=[pt[si], cm], writes=[pt[si]])
                    S.op("pe", lambda e, si=si, kt=kt: e.matmul(po[i][:], vsrc[:, kt, :], pt[si][:], start=(kt == 0), stop=(kt == nk - 1)),
                         reads=[vsrc, pt[si]], writes=[po[i]], signal=False)
                    S.op("pe", lambda e, si=si, kt=kt: e.matmul(pl[i][:], ones[:], pt[si][:], start=(kt == 0), stop=(kt == nk - 1)),
                         reads=[ones, pt[si], vsrc], writes=[pl[i], po[i]])
                sk[0] += nk
                S.op("dve", lambda e: e.reciprocal(out=rl[i][:], in_=pl[i][:]), reads=[pl[i]], writes=[rl[i]])
                S.op("dve", lambda e: e.tensor_tensor(out=ot[i][:], in0=po[i][:], in1=rl[i][:], op=ALU.mult),
                     reads=[po[i], rl[i]], writes=[ot[i]])
                S.op("pool", lambda e: e.tensor_tensor(out=gst[i][:], in0=ot[i][:], in1=szb[i][:], op=ALU.mult),
                     reads=[ot[i], szb[i]], writes=[gst[i]])
                S.dma("act", GTP[(qc // 8) * 512 + h * 128:(qc // 8) * 512 + (h + 1) * 128, (qc % 8) * 512:(qc % 8 + 1) * 512], gst[i][:],
                      reads=[gst[i]], writes=[GTPd])
                if qc == SEQ // 512 - 1:
                    io["gt_head_done"](h)
        S.barrier()


def _inv_freq(dim):
    return np.asarray(1.0 / (10000.0 ** (jnp.arange(0, dim, 2, dtype=jnp.float32) / dim)), dtype=np.float32)


def _dil_consts():
    inv = _inv_freq(128)
    inv128 = np.concatenate([inv, inv]).reshape(128, 1).astype(np.float32)
    sgn = np.concatenate([-np.ones(64), np.ones(64)]).reshape(128, 1).astype(np.float32)
    k = np.arange(128)[:, None]
    q = np.arange(128)[None, :]
    m_prev = (k >= q).astype(np.float32)
    m_cur = (k <= q).astype(np.float32)
    return inv128, sgn, m_prev, m_cur


_PROGS = {}


def build_fused():
    nc = bass.Bass("TRN2", target_bir_lowering=False)

    def I(name, shape, dt=F32):
        return nc.dram_tensor(name, shape, dt, kind="ExternalInput").ap()

    def T(name, shape, dt):
        return nc.dram_tensor(name, shape, dt, kind="Internal").ap()

    x_own = I("x_own", [NTOK, D])
    x_halo = I("x_halo", [2048, D])
    pos_d = I("pos_d", [1, 6144], I32)
    pos_a = I("pos_a", [1, SEQ], I32)
    inv128, sgn128 = I("inv128", [128, 1]), I("sgn128", [128, 1])
    inv64, sgn64 = I("inv64", [64, 1]), I("sgn64", [64, 1])
    masks = I("masks", [128, 512], BF16)
    cmask = I("cmask", [128, 2048], BF16)
    dsa_w_in = [I("dsa_w_in%d" % j, [D, 20480]) for j in range(2)]
    dsa_w_out = [I("dsa_w_out%d" % j, [D, D]) for j in range(2)]
    mla_w_in_c = [I("mla_w_in_c%d" % j, [D, 1600]) for j in range(2)]
    mla_w_uq_c = [I("mla_w_uq_c%d" % j, [512, 768]) for j in range(2)]
    mla_w_kk_c = [I("mla_w_kk_c%d" % j, [512, 512]) for j in range(2)]
    mla_w_kv_c = [I("mla_w_kv_c%d" % j, [512, 512]) for j in range(2)]
    mla_qg = [I("mla_qg%d" % j, [128, 4]) for j in range(2)]
    mla_kg = [I("mla_kg%d" % j, [128, 4]) for j in range(2)]
    mla_w_out = [I("mla_w_out%d" % j, [D, D]) for j in range(2)]
    lng = [I("lng%d" % l, [1, D]) for l in range(DEPTH)]
    lnb = [I("lnb%d" % l, [1, D]) for l in range(DEPTH)]
    out = nc.dram_tensor("out", [NTOK, D], F32, kind="ExternalOutput").ap()
    XTS_t = nc.dram_tensor("XT_send", [D, NTOK], BF16)
    XTALL_t = nc.dram_tensor("XT_allr", [4 * D, NTOK], BF16)
    GTS_t = nc.dram_tensor("GT_send", [2048, NTOK], BF16)
    GTALL_t = nc.dram_tensor("GT_allr", [4 * 2048, NTOK], BF16)
    XTS, XTALL, GTS, GTALL = XTS_t.ap(), XTALL_t.ap(), GTS_t.ap(), GTALL_t.ap()
    XTH0 = T("XT_halo0", [D, 2048], BF16)
    R = [T("R%d" % i, [NTOK, D], F32) for i in range(3)]
    scr = {
        "QT": T("QT", [48, 128, 4096], BF16), "KT": T("KT", [48, 128, 6144], BF16), "VV": T("VV", [3, 6144, D], BF16),
        "ZT": T("ZT", [D, NTOK], BF16), "GT": T("GT", [D, NTOK], BF16),
        "Ctab": T("Ctab", [128, 6144], F32), "Stab": T("Stab", [128, 6144], F32),
        "QN": T("QN", [4, 128, SEQ], BF16), "QP": T("QP", [4, 64, SEQ], BF16), "KN": T("KN", [4, 128, SEQ], BF16),
        "KP": T("KP", [64, SEQ], BF16), "VVm": T("VVm", [SEQ, 512], BF16), "SZT": T("SZT", [512, SEQ], BF16),
        "Ctab64": T("Ctab64", [64, SEQ], F32), "Stab64": T("Stab64", [64, SEQ], F32),
    }
    with ExitStack() as es0:
        S = Sched(nc, es0)
        ident, ones = make_consts(S)
        io0 = dict(scr)
        for k in list(scr):
            io0[k + "_d"] = S.dram(scr[k])
        XTS_d, XTALL_d, GTS_d, GTALL_d, XTH0_d = S.dram(XTS), S.dram(XTALL), S.dram(GTS), S.dram(GTALL), S.dram(XTH0)
        R_d = [S.dram(r_) for r_ in R]
        out_d = S.dram(out)
        ext = Buf(None)
        pid = nc.gpsimd.partition_id()
        rown = (pid % 4) * 2048
        rprev = (pid + 3) % 4
        XTALL4 = XTALL.rearrange("(c s p) t -> c s p t", c=16, s=4)
        with ExitStack() as es_init:
            for (pp_, n_, iv, sg, ck, sk_, P) in ((pos_d, 6144, inv128, sgn128, "Ctab", "Stab", 128),
                                                 (pos_a, SEQ, inv64, sgn64, "Ctab64", "Stab64", 64)):
                S.es = es_init
                inv = S.sb("inv", [P, 1])
                sgn = S.sb("sgn", [P, 1])
                S.dma("sp", inv[:], iv, writes=[inv])
                S.dma("sp", sgn[:], sg, writes=[sgn])
                rope_tables(S, pp_, n_, inv, sgn, io0[ck + "_d"], io0[sk_ + "_d"], P)
            emit_p0(S, ident, x_own, NTOK, XTS, XTS_d)
            emit_p0(S, ident, x_halo, 2048, XTH0, XTH0_d)
        S.barrier()
        XTH0v = XTH0.rearrange("(c p) t -> p c t", p=128)

        def xt_load(b, dst):
            rr, cb = b // 8, b % 8
            S.dma("sp", dst[:], XTALL4[:, rr, :, cb * 512:(cb + 1) * 512].rearrange("c p t -> p c t"),
                  reads=[XTALL_d], writes=[dst])

        GTloc, GTloc_d = scr["GT"], io0["GT_d"]
        GTlv = GTloc.rearrange("(c p) t -> p c t", p=128)

        def gt_gather():
            S.dma("pool", GTloc, GTALL[bass.ds(rown, 2048), :], reads=[GTALL_d], writes=[GTloc_d])

        def gt_head_done(h):
            for tr in range(4):
                k = tr * 4 + h
                S.collective(GTS[k * 128:(k + 1) * 128, :], GTALL[k * 512:(k + 1) * 512, :], GTS_d, GTALL_d)

        def gt_load_loc(t, dst):
            S.dma("sp", dst[:], GTlv[:, :, t * 128:(t + 1) * 128], reads=[GTloc_d], writes=[dst])

        def halo_static(xT):
            S.dma("sp", xT[:], XTH0v, reads=[XTH0_d], writes=[xT])

        def halo_dyn(xT):
            src = XTALL4[:, bass.ds(rprev, 1), :, 2048:4096].rearrange("c o p t -> p (c o) t")
            S.dma("pool", xT[:], src, reads=[XTALL_d], writes=[xT])

        resid_in, resid_in_d = x_own, ext
        for layer in range(DEPTH):
            j = layer // 2
            last = layer == DEPTH - 1
            ro, rod = (out, out_d) if last else (R[layer], R_d[layer])
            if layer % 2 == 0:
                io = dict(io0)
                io.update({"XT_own": XTS, "XT_own_d": XTS_d, "resid": resid_in, "resid_d": resid_in_d,
                           "w_in": dsa_w_in[j], "wout": dsa_w_out[j], "lng": lng[layer], "lnb": lnb[layer], "masks": masks,
                           "ro": ro, "rod": rod, "XTo": XTS, "xtd": XTS_d,
                           "halo_load": halo_static if layer == 0 else halo_dyn})
                emit_dil(S, ident, ones, io)
            else:
                io = dict(io0)
                io.update({"w_in_c": mla_w_in_c[j], "w_uq_c": mla_w_uq_c[j], "w_kk_c": mla_w_kk_c[j], "w_kv_c": mla_w_kv_c[j],
                           "qg": mla_qg[j], "kg": mla_kg[j], "cmask": cmask, "GTP": GTS, "GTP_d": GTS_d, "xt_load": xt_load,
                           "gt_head_done": gt_head_done})
                emit_mla_a(S, ident, ones, io)
                gt_gather()
                with ExitStack() as es:
                    S.es = es
                    emit_epilogue(S, ident, gt_load_loc, resid_in, resid_in_d, mla_w_out[j], lng[layer], lnb[layer],
                                  ro, rod, None if last else XTS, None if last else XTS_d, perm=MLA_PERM)
                S.barrier()
            if not last:
                S.allgather16(XTS_t, XTALL_t, XTS_d, XTALL_d)
            resid_in, resid_in_d = ro, rod
        S.drain([out_d])
    return nc


def kernel(x, positions, dsa_w_in, dsa_w_out, mla_w_in, mla_q_norm, mla_w_uq, mla_kv_norm, mla_w_ukv,
           mla_w_out, ln_g, ln_b):
    x = np.asarray(x)
    positions = np.asarray(positions)
    args = [np.asarray(a) for a in (dsa_w_in, dsa_w_out, mla_w_in, mla_q_norm, mla_w_uq, mla_kv_norm, mla_w_ukv, mla_w_out, ln_g, ln_b)]
    dsa_w_in, dsa_w_out, mla_w_in, mla_q_norm, mla_w_uq, mla_kv_norm, mla_w_ukv, mla_w_out, ln_g, ln_b = args
    if "fused" not in _PROGS:
        _PROGS["fused"] = build_fused()
    nc = _PROGS["fused"]
    bf = ml_dtypes.bfloat16
    inv128, sgn128, m_prev, m_cur = _dil_consts()
    inv = _inv_freq(64)
    inv64 = np.concatenate([inv, inv]).reshape(64, 1).astype(np.float32)
    sgn64 = np.concatenate([-np.ones(32), np.ones(32)]).reshape(64, 1).astype(np.float32)
    kk = np.arange(128)[:, None]
    qq = np.arange(512)[None, :]
    cmask = np.concatenate([((d * 128 + kk) <= qq).astype(np.float32) for d in range(4)], axis=1).astype(bf)
    in_maps = []
    for c in range(NCORES):
        bb, r = c // 4, c % 4
        m = {"x_own": np.ascontiguousarray(x[bb, r * NTOK:(r + 1) * NTOK])}
        if r == 0:
            m["x_halo"] = np.zeros((2048, D), np.float32)
            hpos = np.zeros((2048,), np.int32)
            m_halo = np.zeros_like(m_prev)
        else:
            m["x_halo"] = np.ascontiguousarray(x[bb, r * NTOK - 2048:r * NTOK])
            hpos = positions[bb, r * NTOK - 2048:r * NTOK]
            m_halo = m_prev
        m["masks"] = np.concatenate([m_prev, m_cur, m_halo, m_cur], axis=1).astype(bf)
        m["pos_d"] = np.concatenate([hpos, positions[bb, r * NTOK:(r + 1) * NTOK]]).reshape(1, 6144).astype(np.int32)
        m["pos_a"] = np.ascontiguousarray(positions[bb].reshape(1, SEQ)).astype(np.int32)
        m.update({"inv128": inv128, "sgn128": sgn128, "inv64": inv64, "sgn64": sgn64, "cmask": cmask})
        for j in range(2):
            m["dsa_w_in%d" % j] = dsa_w_in[j]
            m["dsa_w_out%d" % j] = dsa_w_out[j]
            w_in = mla_w_in[j]
            m["mla_w_in_c%d" % j] = np.ascontiguousarray(np.concatenate([w_in[:, 0:1088], w_in[:, 1088 + r * 512:1088 + (r + 1) * 512]], axis=1))
            m["mla_w_uq_c%d" % j] = np.ascontiguousarray(mla_w_uq[j][:, r * 768:(r + 1) * 768])
            wk4 = mla_w_ukv[j].reshape(512, 16, 256)[:, 4 * r:4 * r + 4]
            m["mla_w_kk_c%d" % j] = np.ascontiguousarray(wk4[:, :, 0:128].reshape(512, 512))
            m["mla_w_kv_c%d" % j] = np.ascontiguousarray(wk4[:, :, 128:256].reshape(512, 512))
            m["mla_qg%d" % j] = np.ascontiguousarray(mla_q_norm[j].reshape(4, 128).T).astype(np.float32)
            m["mla_kg%d" % j] = np.ascontiguousarray(mla_kv_norm[j].reshape(4, 128).T).astype(np.float32)
            m["mla_w_out%d" % j] = mla_w_out[j]
        for l in range(DEPTH):
            m["lng%d" % l] = np.ascontiguousarray(ln_g[l].reshape(1, D))
            m["lnb%d" % l] = np.ascontiguousarray(ln_b[l].reshape(1, D))
        in_maps.append(m)
    res = run_bass_kernel_spmd(nc, in_maps, core_ids=list(range(NCORES)))
    out = np.empty((2, SEQ, D), np.float32)
    for c in range(NCORES):
        out[c // 4, (c % 4) * NTOK:(c % 4 + 1) * NTOK] = res.results[c]["out"]
    return out
```
